# Optimizing a Trainium2 kernel written in Bass

```python
import jax, jax.numpy as jnp
from jax import lax
import numpy as np

D_MODEL = 2048
BATCH = 8
SEQ = 2048
DEPTH = 1

D_MIX = D_MODEL
NSA_WIDTH = D_MIX // 2
NSA_HEAD_DIM = 64
NSA_HEADS = NSA_WIDTH // NSA_HEAD_DIM
NSA_KV_HEADS = 4
NSA_GROUP = NSA_HEADS // NSA_KV_HEADS
NSA_KV_WIDTH = NSA_KV_HEADS * NSA_HEAD_DIM
CMP_BLOCK = 32
CMP_STRIDE = 16
CMP_HIDDEN = 4 * NSA_HEAD_DIM
SEL_BLOCK = 64
SEL_TOP = 8
SEL_BONUS = 1.0e4
WINDOW = 512
Q_BLOCK = 128
HGRN_WIDTH = D_MIX - NSA_WIDTH
HGRN_HEAD_DIM = 128
HGRN_HEADS = HGRN_WIDTH // HGRN_HEAD_DIM
HGRN_CHUNK = 64
EPS = 1e-6
NEG_INF = -1e30
IN_SPLITS = [NSA_WIDTH] + [NSA_KV_WIDTH] * 6 + [3 * NSA_HEADS, NSA_WIDTH] + [HGRN_WIDTH] * 4
D_IN = sum(IN_SPLITS)

kernel_name = "hymba_nsa_hgrn2_layer"


def rmsnorm(x, g):
    xf = x.astype(jnp.float32)
    y = xf * lax.rsqrt(jnp.mean(xf * xf, axis=-1, keepdims=True) + EPS)
    return (y * g.astype(jnp.float32)).astype(x.dtype)


def head_rmsnorm(x, g, n_heads):
    B, S, W = x.shape
    xh = x.astype(jnp.float32).reshape(B, S, n_heads, W // n_heads)
    y = xh * lax.rsqrt(jnp.mean(xh * xh, axis=-1, keepdims=True) + EPS)
    return (y.reshape(B, S, W) * g.astype(jnp.float32)).astype(x.dtype)


def alibi_slopes(n):
    return jnp.asarray(2.0 ** (-8.0 * np.arange(1, n + 1) / n), dtype=jnp.float32)


def masked_softmax(logits, mask):
    logits = jnp.where(mask, logits, NEG_INF)
    return jax.nn.softmax(logits, axis=-1) * mask


def compress_blocks(kv, pe, w1, w2):
    B, S, G, dh = kv.shape
    n_cmp = (S - CMP_BLOCK) // CMP_STRIDE + 1
    idx = CMP_STRIDE * np.arange(n_cmp)[:, None] + np.arange(CMP_BLOCK)[None, :]
    blk = kv[:, idx] + pe[:, None, :]
    blk = blk.transpose(0, 1, 3, 2, 4).reshape(B, n_cmp, G, CMP_BLOCK * dh)
    return jax.nn.gelu(blk @ w1) @ w2


def nsa_mixer(q, k_cmp, v_cmp, k_slc, v_slc, k_win, v_win, gate,
              pe_k, w1_k, w2_k, pe_v, w1_v, w2_v):
    B, S = q.shape[:2]
    G, R, dh = NSA_KV_HEADS, NSA_GROUP, NSA_HEAD_DIM
    q = q.reshape(B, S, G, R, dh) * (dh ** -0.5)
    kv4 = lambda a: a.reshape(B, S, G, dh)
    k_cmp, v_cmp, k_slc, v_slc, k_win, v_win = map(kv4, (k_cmp, v_cmp, k_slc, v_slc, k_win, v_win))
    slopes = alibi_slopes(NSA_HEADS).reshape(G, R)[:, :, None, None]
    t = jnp.arange(S)

    kc = compress_blocks(k_cmp, pe_k, w1_k, w2_k)
    vc = compress_blocks(v_cmp, pe_v, w1_v, w2_v)
    n_cmp = kc.shape[1]
    c_end = CMP_STRIDE * jnp.arange(n_cmp) + CMP_BLOCK - 1
    dist_c = t[:, None] - c_end[None, :]
    logits = jnp.einsum('btgrd,bngd->bgrtn', q, kc).astype(jnp.float32) - slopes * dist_c
    p_cmp = masked_softmax(logits, dist_c >= 0)
    o_cmp = jnp.einsum('bgrtn,bngd->btgrd', p_cmp.astype(vc.dtype), vc)

    n_sel = S // SEL_BLOCK
    top = min(SEL_TOP, n_sel)
    cs = CMP_STRIDE * np.arange(n_cmp)[:, None]
    ss = SEL_BLOCK * np.arange(n_sel)[None, :]
    overlap = np.clip(np.minimum(cs + CMP_BLOCK, ss + SEL_BLOCK) - np.maximum(cs, ss), 0, None)
    m_map = jnp.asarray(overlap / CMP_BLOCK, dtype=jnp.float32)
    p_slc = jnp.einsum('bgtn,nj->bgtj', p_cmp.sum(axis=2), m_map)
    j = jnp.arange(n_sel)[None, :]
    cur = (t // SEL_BLOCK)[:, None]
    forced = ((j == 0) | (j == cur) | (j == cur - 1)).astype(jnp.float32)
    future = j * SEL_BLOCK > t[:, None]
    score = jnp.where(future, -1.0, p_slc + SEL_BONUS * forced)
    _, sel_idx = lax.top_k(score, top)

    k_blocks = k_slc.reshape(B, n_sel, SEL_BLOCK, G, dh).transpose(0, 3, 1, 2, 4)
    v_blocks = v_slc.reshape(B, n_sel, SEL_BLOCK, G, dh).transpose(0, 3, 1, 2, 4)
    nq = S // Q_BLOCK
    q_chunks = q.reshape(B, nq, Q_BLOCK, G, R, dh).transpose(1, 0, 3, 4, 2, 5)
    idx_chunks = sel_idx.reshape(B, G, nq, Q_BLOCK, top).transpose(2, 0, 1, 3, 4)
    gather = jax.vmap(jax.vmap(lambda blocks, ids: blocks[ids]))
    n_tok = top * SEL_BLOCK

    def sel_block(args):
        qc, ic, c = args
        kg = gather(k_blocks, ic).reshape(B, G, Q_BLOCK, n_tok, dh)
        vg = gather(v_blocks, ic).reshape(B, G, Q_BLOCK, n_tok, dh)
        pos = (ic[..., None] * SEL_BLOCK + jnp.arange(SEL_BLOCK)).reshape(B, G, Q_BLOCK, n_tok)
        t_c = c * Q_BLOCK + jnp.arange(Q_BLOCK)
        dist = (t_c[:, None] - pos)[:, :, None]
        lg = jnp.einsum('bgrqd,bgqkd->bgrqk', qc, kg).astype(jnp.float32) - slopes * dist
        p = masked_softmax(lg, dist >= 0)
        return jnp.einsum('bgrqk,bgqkd->bgrqd', p.astype(vg.dtype), vg)

    o_slc = lax.map(sel_block, (q_chunks, idx_chunks, jnp.arange(nq)))
    o_slc = o_slc.transpose(1, 0, 4, 2, 3, 5).reshape(B, S, G, R, dh)

    span = WINDOW + Q_BLOCK
    k_pad = jnp.pad(k_win, ((0, 0), (WINDOW, 0), (0, 0), (0, 0)))
    v_pad = jnp.pad(v_win, ((0, 0), (WINDOW, 0), (0, 0), (0, 0)))

    def win_block(args):
        qc, c = args
        start = c * Q_BLOCK
        kw = lax.dynamic_slice_in_dim(k_pad, start, span, axis=1)
        vw = lax.dynamic_slice_in_dim(v_pad, start, span, axis=1)
        pos = start - WINDOW + jnp.arange(span)
        t_c = start + jnp.arange(Q_BLOCK)
        dist = t_c[:, None] - pos[None, :]
        mask = (pos[None, :] >= 0) & (dist >= 0) & (dist < WINDOW)
        lg = jnp.einsum('bgrqd,bkgd->bgrqk', qc, kw).astype(jnp.float32) - slopes * dist
        p = masked_softmax(lg, mask)
        return jnp.einsum('bgrqk,bkgd->bgrqd', p.astype(vw.dtype), vw)

    o_win = lax.map(win_block, (q_chunks, jnp.arange(nq)))
    o_win = o_win.transpose(1, 0, 4, 2, 3, 5).reshape(B, S, G, R, dh)

    g = jax.nn.sigmoid(gate.reshape(B, S, G, R, 3))
    o = g[..., 0:1] * o_cmp + g[..., 1:2] * o_slc + g[..., 2:3] * o_win
    return o.reshape(B, S, NSA_WIDTH)


def hgrn2_mixer(q, f_pre, v, lb):
    out_dtype = q.dtype
    B, S, W = q.shape
    H, dh, C = HGRN_HEADS, HGRN_HEAD_DIM, HGRN_CHUNK
    nc = S // C
    qf, vf = q.astype(jnp.float32), v.astype(jnp.float32)
    f = lb + (1.0 - lb) * jax.nn.sigmoid(f_pre.astype(jnp.float32))
    k = 1.0 - f
    log_f = jnp.log(f)
    chunk = lambda a: a.reshape(B, nc, C, H, dh).transpose(1, 0, 3, 2, 4)
    causal = jnp.tril(jnp.ones((C, C), dtype=bool))[:, :, None]

    def step(state, xs):
        qc, kc, vc, lfc = xs
        b = jnp.cumsum(lfc, axis=2)
        o_inter = jnp.einsum('bhtd,bhde->bhte', qc * jnp.exp(b), state)
        decay = jnp.exp(jnp.where(causal, b[:, :, :, None, :] - b[:, :, None, :, :], -jnp.inf))
        attn = jnp.einsum('bhtd,bhsd,bhtsd->bhts', qc, kc, decay)
        o = o_inter + jnp.einsum('bhts,bhse->bhte', attn, vc)
        b_last = b[:, :, -1:, :]
        state = jnp.exp(b_last[:, :, 0, :])[..., None] * state + \
            jnp.einsum('bhsd,bhse->bhde', kc * jnp.exp(b_last - b), vc)
        return state, o

    state0 = jnp.zeros((B, H, dh, dh), dtype=jnp.float32)
    _, o = lax.scan(step, state0, (chunk(qf), chunk(k), chunk(vf), chunk(log_f)))
    return o.transpose(1, 0, 3, 2, 4).reshape(B, S, W).astype(out_dtype)


def setup_inputs(seed: int = 0) -> dict:
    key = jax.random.key(seed)
    ks = jax.random.split(key, 16)
    nrm = lambda k, shape, s: jax.random.normal(k, shape, dtype=jnp.float32) * s
    gain = lambda k, shape: 1.0 + nrm(k, shape, 0.02)
    return {
        "x": nrm(ks[0], (BATCH, SEQ, D_MODEL), 1.0),
        "norm_in": gain(ks[1], (DEPTH, D_MODEL)),
        "w_in": nrm(ks[2], (DEPTH, D_MODEL, D_IN), D_MODEL ** -0.5),
        "cmp_pe_k": nrm(ks[3], (DEPTH, CMP_BLOCK, NSA_HEAD_DIM), 0.1),
        "cmp_w1_k": nrm(ks[4], (DEPTH, CMP_BLOCK * NSA_HEAD_DIM, CMP_HIDDEN), (CMP_BLOCK * NSA_HEAD_DIM) ** -0.5),
        "cmp_w2_k": nrm(ks[5], (DEPTH, CMP_HIDDEN, NSA_HEAD_DIM), CMP_HIDDEN ** -0.5),
        "cmp_pe_v": nrm(ks[6], (DEPTH, CMP_BLOCK, NSA_HEAD_DIM), 0.1),
        "cmp_w1_v": nrm(ks[7], (DEPTH, CMP_BLOCK * NSA_HEAD_DIM, CMP_HIDDEN), (CMP_BLOCK * NSA_HEAD_DIM) ** -0.5),
        "cmp_w2_v": nrm(ks[8], (DEPTH, CMP_HIDDEN, NSA_HEAD_DIM), CMP_HIDDEN ** -0.5),
        "lower_bounds": nrm(ks[9], (DEPTH + 1, HGRN_WIDTH), 0.1),
        "nsa_out_norm": gain(ks[10], (DEPTH, NSA_WIDTH)),
        "hgrn_out_norm": gain(ks[11], (DEPTH, HGRN_WIDTH)),
        "w_out": nrm(ks[12], (DEPTH, D_MIX, D_MODEL), D_MIX ** -0.5),
        "final_norm": gain(ks[13], (D_MODEL,)),
    }


def reference(x, norm_in, w_in, cmp_pe_k, cmp_w1_k, cmp_w2_k, cmp_pe_v, cmp_w1_v, cmp_w2_v,
              lower_bounds, nsa_out_norm, hgrn_out_norm, w_out, final_norm):
    split_at = np.cumsum(IN_SPLITS)[:-1].tolist()
    lbs = jnp.cumsum(jax.nn.softmax(lower_bounds.astype(jnp.float32), axis=0), axis=0)
    for l in range(DEPTH):
        h = rmsnorm(x, norm_in[l])
        proj = h @ w_in[l]
        (q_a, k_cmp, v_cmp, k_slc, v_slc, k_win, v_win, gate_a, z_a,
         q_h, f_h, i_h, z_h) = jnp.split(proj, split_at, axis=-1)
        o_a = nsa_mixer(q_a, k_cmp, v_cmp, k_slc, v_slc, k_win, v_win, gate_a,
                        cmp_pe_k[l], cmp_w1_k[l], cmp_w2_k[l], cmp_pe_v[l], cmp_w1_v[l], cmp_w2_v[l])
        o_h = hgrn2_mixer(q_h, f_h, i_h, lbs[l])
        o_a = head_rmsnorm(o_a, nsa_out_norm[l], NSA_HEADS) * jax.nn.silu(z_a)
        o_h = head_rmsnorm(o_h, hgrn_out_norm[l], HGRN_HEADS) * jax.nn.silu(z_h)
        mixed = jnp.concatenate([o_a, o_h], axis=-1)
        x = x + mixed @ w_out[l]
    return rmsnorm(x, final_norm)
```

```python
import numpy as np
import ml_dtypes
from contextlib import ExitStack
import concourse.bass as bass
import concourse.mybir as mybir
from concourse.bass_utils import run_bass_kernel_spmd

F32 = mybir.dt.float32
BF16 = mybir.dt.bfloat16
AF = mybir.ActivationFunctionType
ALU = mybir.AluOpType
AX = mybir.AxisListType

S = 2048
D = 2048
DIN = 7728
NT = 16
KC = 16
EPS = 1e-6
OFF_QA, OFF_KC, OFF_VC, OFF_KS, OFF_VS, OFF_KW, OFF_VW = 0, 1024, 1280, 1536, 1792, 2048, 2304
OFF_GATE, OFF_ZA, OFF_QH, OFF_FH, OFF_IH, OFF_ZH = 2560, 2608, 3632, 4656, 5680, 6704
NEG = -30000.0


class Op:
    __slots__ = ("eng", "fn", "dma", "deps", "signal", "count", "slot", "target", "prev_slot_target")

    def __init__(self, eng, fn, dma):
        self.eng, self.fn, self.dma = eng, fn, dma
        self.deps = set()
        self.signal = False
        self.count = None
        self.slot = None
        self.target = None
        self.prev_slot_target = 0


class Prog:
    def __init__(self, nc, n_dma_slots=8):
        self.nc = nc
        self.ops = []
        self.last_writer = {}
        self.readers = {}
        self.n_dma_slots = n_dma_slots
        self.last_on_eng = {}
        self.dma_since_barrier = []
        self.barrier_set = None
        self.barrier_id = 0
        self.barrier_seen = {}

    def add(self, eng, fn, reads=(), writes=(), dma=False):
        i = len(self.ops)
        op = Op(eng, fn, dma)
        deps = {}
        for k in reads:
            j = self.last_writer.get(k)
            if j is not None:
                deps[j] = True
        for k in writes:
            j = self.last_writer.get(k)
            if j is not None:
                deps[j] = True
            for r in self.readers.get(k, ()):
                deps.setdefault(r, False)
        if self.barrier_set is not None and self.barrier_seen.get(eng, -1) < self.barrier_id:
            for j in self.barrier_set:
                deps.setdefault(j, True)
            self.barrier_seen[eng] = self.barrier_id
        for j, hard in deps.items():
            oj = self.ops[j]
            if not oj.dma and oj.eng == eng:
                if eng == "pe":
                    continue
            op.deps.add(j)
            if not oj.dma:
                oj.signal = True
        for k in reads:
            self.readers.setdefault(k, []).append(i)
        for k in writes:
            self.last_writer[k] = i
            self.readers[k] = []
        self.ops.append(op)
        self.last_on_eng[eng] = i
        if dma:
            self.dma_since_barrier.append(i)
        return i

    def barrier(self):
        b = set(self.last_on_eng.values()) | set(self.dma_since_barrier)
        if self.barrier_set is not None and any(
                self.barrier_seen.get(e, -1) < self.barrier_id for e in ("pe", "act", "dve", "pool", "sp")):
            b |= self.barrier_set
        self.barrier_set = b
        self.barrier_id += 1
        self.dma_since_barrier = []

    def emit(self, stack):
        nc = self.nc
        engs = {"pe": nc.tensor, "act": nc.scalar, "dve": nc.vector, "pool": nc.gpsimd, "sp": nc.sync}
        esem = {e: stack.enter_context(nc.semaphore("sem_" + e)) for e in engs}
        dsem = {}
        for q in ("sp", "pool", "act"):
            dsem[q] = [stack.enter_context(nc.semaphore("dsem_%s_%d" % (q, s))) for s in range(self.n_dma_slots)]
        cnt = {e: 0 for e in engs}
        dcount = {q: 0 for q in dsem}
        slot_total = {q: [0] * self.n_dma_slots for q in dsem}
        for op in self.ops:
            if op.dma:
                q = op.eng
                s = dcount[q] % self.n_dma_slots
                dcount[q] += 1
                op.slot = s
                op.prev_slot_target = slot_total[q][s]
                slot_total[q][s] += 16
                op.target = slot_total[q][s]
            elif op.signal:
                cnt[op.eng] += 1
                op.count = cnt[op.eng]
        seen = {e: {} for e in engs}
        nwaits = 0
        for op in self.ops:
            e = engs[op.eng]
            waits = []
            for j in op.deps:
                oj = self.ops[j]
                if oj.dma:
                    waits.append((("d", oj.eng, oj.slot), dsem[oj.eng][oj.slot], oj.target))
                else:
                    waits.append((("e", oj.eng), esem[oj.eng], oj.count))
            if op.dma and op.prev_slot_target > 0:
                waits.append((("d", op.eng, op.slot), dsem[op.eng][op.slot], op.prev_slot_target))
            best = {}
            for key, sem, val in waits:
                if val > seen[op.eng].get(key, 0) and val > best.get(key, (None, 0))[1]:
                    best[key] = (sem, val)
            for key, (sem, val) in best.items():
                e.wait_ge(sem, val)
                seen[op.eng][key] = val
                nwaits += 1
            inst = op.fn()
            if op.dma:
                inst.then_inc(dsem[op.eng][op.slot], 16)
            elif op.signal:
                inst.then_inc(esem[op.eng], 1)
        sp = nc.sync
        for q in dsem:
            for s in range(self.n_dma_slots):
                if slot_total[q][s] > seen["sp"].get(("d", q, s), 0):
                    sp.wait_ge(dsem[q][s], slot_total[q][s])
        return len(self.ops), nwaits


def bf(a):
    return np.asarray(a, dtype=np.float32).astype(ml_dtypes.bfloat16)


def build_program(stages=("A", "B", "C", "D", "F"), debug=()):
    nc = bass.Bass("TRN2", target_bir_lowering=False)
    P = Prog(nc)
    dt_in = lambda name, shape, dt=F32: nc.dram_tensor(name, list(shape), dt, kind="ExternalInput").ap()
    x = dt_in("x", [S, D])
    norm_in = dt_in("norm_in", [D])
    w_in = dt_in("w_in", [D, DIN])
    w_out = dt_in("w_out", [D, D])
    final_norm = dt_in("final_norm", [D])
    nsa_gain = dt_in("nsa_out_norm", [1024])
    hgrn_gain = dt_in("hgrn_out_norm", [1024])
    ident_d = dt_in("c_ident", [128, 128], BF16)
    lower_bounds = dt_in("lower_bounds", [2, 1024])
    cmp_pe_k = dt_in("cmp_pe_k", [32, 64]); cmp_pe_v = dt_in("cmp_pe_v", [32, 64])
    cmp_w1_k = dt_in("cmp_w1_k", [2048, 256]); cmp_w1_v = dt_in("cmp_w1_v", [2048, 256])
    cmp_w2_k = dt_in("cmp_w2_k", [256, 64]); cmp_w2_v = dt_in("cmp_w2_v", [256, 64])
    c_E = dt_in("c_E", [128, S], BF16)
    c_diag = dt_in("c_diag", [128, 2, 4, 128], BF16)
    c_cmask = dt_in("c_cmask", [128, NT, 128], BF16)
    c_mmap = dt_in("c_mmap", [128, 32], BF16)
    c_sel = dt_in("c_sel", [128, NT, 2, 32])
    c_kaug = dt_in("c_kaug", [7, S], BF16)
    c_kcaug = dt_in("c_kcaug", [7, 127], BF16)
    c_qaug = dt_in("c_qaug", [4, 7, NT, 4, 128], BF16)
    c_f32 = dt_in("c_f32", [128, 4, 128])
    out = nc.dram_tensor("out", [S, D], F32, kind="ExternalOutput").ap()
    o_scr = nc.dram_tensor("o_scr", [S, D], F32,
                           kind="ExternalOutput" if any(n == "o_scr" for n, _ in debug) else "Internal").ap()
    dbg = {}
    for name, shape in debug:
        if name == "o_scr":
            continue
        dbg[name] = nc.dram_tensor("dbg_" + name, list(shape), F32, kind="ExternalOutput").ap()

    with ExitStack() as st:
        sb = lambda name, shape, dt: st.enter_context(nc.sbuf_tensor(name, list(shape), dt))
        ps = lambda name, shape, dt: st.enter_context(nc.psum_tensor(name, list(shape), dt))

        epsb = sb("epsb", [128, 1], F32)
        P.add("dve", lambda: nc.vector.memset(epsb[:], EPS), writes=["epsb"])

        def rsqrt(o, i, scale, rkey, wkey, rkeys=None):
            np_ = o.shape[0]
            P.add("act", lambda: nc.scalar.activation(out=o, in_=i, func=AF.Ln, bias=epsb[0:np_, 0:1], scale=scale),
                  reads=(rkeys or [rkey]) + ["epsb"], writes=[wkey])
            P.add("act", lambda: nc.scalar.activation(out=o, in_=o, func=AF.Exp, scale=-0.5),
                  reads=[wkey], writes=[wkey])

        hT = sb("hT", [128, KC, S], BF16)
        wz0 = sb("wz0", [128, KC, 512], BF16)
        if "D" in stages:
            for k in range(KC):
                P.add("pool", lambda k=k: nc.gpsimd.dma_start(
                    out=wz0[:, k, :], in_=w_in[k * 128:(k + 1) * 128, OFF_ZA:OFF_ZA + 512]),
                    writes=["wz0_%d" % k], dma=True)
        ident = sb("ident", [128, 128], BF16)
        gin = sb("gin", [128, KC], F32)
        P.add("sp", lambda: nc.sync.dma_start(out=ident[:], in_=ident_d), writes=["ident"], dma=True)
        P.add("sp", lambda: nc.sync.dma_start(out=gin[:], in_=norm_in.rearrange("(k p) -> p k", p=128),
                                              allow_slow_non_contiguous=True),
              writes=["gin"], dma=True)

        def phase_A():
            with ExitStack() as sa:
                sba = lambda name, shape, dt: sa.enter_context(nc.sbuf_tensor(name, list(shape), dt))
                xt = [sba("xt%d" % i, [128, D], F32) for i in range(2)]
                xn = [sba("xn%d" % i, [128, D], BF16) for i in range(2)]
                junk = sba("junkA", [128, D], BF16)
                ssq = [sba("ssq%d" % i, [128, 1], F32) for i in range(2)]
                rstd = [sba("rstd%d" % i, [128, 1], F32) for i in range(2)]
                pt = [sa.enter_context(nc.psum_tensor("ptA%d" % i, [128, 8, 128], BF16)) for i in range(2)]
                for i in range(NT):
                    b = i % 2
                    P.add("sp", lambda i=i, b=b: nc.sync.dma_start(out=xt[b][:], in_=x[i * 128:(i + 1) * 128, :]),
                          writes=["xt%d" % b], dma=True)
                    P.add("act", lambda b=b: nc.scalar.activation(out=junk[:], in_=xt[b][:], func=AF.Square,
                                                                  accum_out=ssq[b][:]),
                          reads=["xt%d" % b], writes=["junkA", "ssq%d" % b])
                    rsqrt(rstd[b][:], ssq[b][:], 1.0 / D, "ssq%d" % b, "rstd%d" % b)
                    P.add("dve", lambda b=b: nc.vector.tensor_scalar(out=xn[b][:], in0=xt[b][:], scalar1=rstd[b][:, 0:1],
                                                                     scalar2=None, op0=ALU.mult),
                          reads=["xt%d" % b, "rstd%d" % b], writes=["xn%d" % b])
                    for half in range(2):
                        for kk in range(8):
                            k = half * 8 + kk
                            P.add("pe", lambda b=b, k=k, kk=kk, half=half: nc.tensor.transpose(
                                out=pt[half][:, kk, :], in_=xn[b][:, k * 128:(k + 1) * 128], identity=ident[:]),
                                reads=["xn%d" % b, "ident"], writes=["ptA%d" % half])
                        P.add("dve", lambda i=i, half=half: nc.vector.tensor_tensor(
                            out=hT[:, half * 8:(half + 1) * 8, i * 128:(i + 1) * 128], in0=pt[half][:],
                            in1=gin[:, half * 8:(half + 1) * 8].unsqueeze(2).to_broadcast([128, 8, 128]), op=ALU.mult),
                            reads=["ptA%d" % half, "gin"], writes=["hT_%d" % i])
        if "A" in stages:
            phase_A()
            P.barrier()
            if "hT" in dbg:
                with nc.sbuf_tensor("dbg_hT_sb", [128, KC, S], F32) as dsb:
                    P.add("dve", lambda: nc.vector.tensor_copy(out=dsb[:], in_=hT[:]),
                          reads=["hT_%d" % i for i in range(NT)], writes=["dsb"])
                    P.add("sp", lambda: nc.sync.dma_start(out=dbg["hT"].rearrange("(k p) t -> p k t", p=128), in_=dsb[:]),
                          reads=["dsb"], dma=True)
                    P.barrier()
        hT_keys = ["hT_%d" % i for i in range(NT)]

        def phase_O1():
            with nc.sbuf_tensor("ones_dbg", [128, D], F32) as ones_dbg:
                P.add("dve", lambda: nc.vector.memset(ones_dbg[:], 1.0), writes=["ones_dbg"])
                for i in range(NT):
                    P.add("sp", lambda i=i: nc.sync.dma_start(out=o_scr[i * 128:(i + 1) * 128, :], in_=ones_dbg[:]),
                          reads=["ones_dbg"], writes=["o_scr"], dma=True)
                P.barrier()
        if "O1" in stages:
            phase_O1()


        def phase_B():
            with ExitStack() as sB:
                sbb = lambda name, shape, dt: sB.enter_context(nc.sbuf_tensor(name, list(shape), dt))
                kT_aug = sbb("kT_aug", [128, 2, 4, S], BF16)
                v_aug = sbb("v_aug", [128, NT, 2, 4, 65], BF16)
                gate_sb = sbb("gate_sb", [128, NT, 48], F32)
                kcT_aug = sbb("kcT_aug", [128, 4, 127], BF16)
                vc_aug = sbb("vc_aug", [128, 4, 65], BF16)
                P.add("dve", lambda: nc.vector.memset(kT_aug[64:128, :, :, :], 0.0), writes=["kTaug_init"])
                P.add("dve", lambda: nc.vector.memset(kcT_aug[64:128, :, :], 0.0), writes=["kcTaug_init"])
                for xx in range(2):
                    for g in range(4):
                        P.add("sp", lambda xx=xx, g=g: nc.sync.dma_start(out=kT_aug[64:71, xx, g, :], in_=c_kaug),
                              reads=["kTaug_init"], writes=["kTaug_rows"], dma=True)
                        if xx == 0:
                            P.add("sp", lambda g=g: nc.sync.dma_start(out=kT_aug[96:128, 0, g, :], in_=c_E[0:32, :]),
                                  reads=["kTaug_init"], writes=["kTaug_rows"], dma=True)
                for g in range(4):
                    P.add("sp", lambda g=g: nc.sync.dma_start(out=kcT_aug[64:71, g, :], in_=c_kcaug),
                          reads=["kcTaug_init"], writes=["kcTaug_rows"], dma=True)
                P.add("dve", lambda: nc.vector.memset(v_aug[:], 1.0), writes=["v_aug_init"])
                P.add("dve", lambda: nc.vector.memset(vc_aug[:], 1.0), writes=["vc_aug_init"])

                with ExitStack() as s0:
                    sb0 = lambda name, shape, dt: s0.enter_context(nc.sbuf_tensor(name, list(shape), dt))
                    ps0 = lambda name, shape, dt: s0.enter_context(nc.psum_tensor(name, list(shape), dt))
                    slabB = sb0("slabB", [128, KC, 2, 2, 256], BF16)
                    wg = sb0("wg", [128, KC, 48], BF16)
                    kstg = [sb0("kstg%d" % i, [128, 512], BF16) for i in range(2)]
                    pp = [ps0("ppB%d" % i, [128, 512], F32) for i in range(2)]
                    pg = ps0("pgB", [128, 512], F32)
                    npj = 0
                    for k in range(KC):
                        P.add("pool", lambda k=k: nc.gpsimd.dma_start(
                            out=slabB[:, k, :, :, :].rearrange("p a b c -> p (a b c)"),
                            in_=w_in[k * 128:(k + 1) * 128, OFF_KS:OFF_KS + 1024]),
                            writes=["slabB_%d" % k], dma=True)
                        P.add("pool", lambda k=k: nc.gpsimd.dma_start(
                            out=wg[:, k, :], in_=w_in[k * 128:(k + 1) * 128, OFF_GATE:OFF_GATE + 48]),
                            writes=["wg_%d" % k], dma=True)
                    for xx in range(2):
                        for gp in range(2):
                            for tb in range(4):
                                b = npj % 2
                                npj += 1
                                for k in range(KC):
                                    P.add("pe", lambda k=k, xx=xx, gp=gp, tb=tb, b=b: nc.tensor.matmul(
                                        pp[b][:], lhsT=slabB[:, k, xx, 0, gp * 128:(gp + 1) * 128],
                                        rhs=hT[:, k, tb * 512:(tb + 1) * 512], start=(k == 0), stop=(k == KC - 1)),
                                        reads=["slabB_%d" % k] + hT_keys[tb * 4:tb * 4 + 4], writes=["ppB%d" % b])
                                P.add("act", lambda xx=xx, gp=gp, tb=tb, b=b: nc.scalar.copy(
                                    out=kT_aug[0:64, xx, 2 * gp, tb * 512:(tb + 1) * 512], in_=pp[b][0:64, :]),
                                    reads=["ppB%d" % b], writes=["kT_%d_%d" % (xx, 2 * gp)])
                                P.add("act", lambda b=b: nc.scalar.copy(out=kstg[b][64:128, :], in_=pp[b][64:128, :]),
                                      reads=["ppB%d" % b], writes=["kstg%d" % b])
                                P.add("sp", lambda xx=xx, gp=gp, tb=tb, b=b: nc.sync.dma_start(
                                    out=kT_aug[0:64, xx, 2 * gp + 1, tb * 512:(tb + 1) * 512], in_=kstg[b][64:128, :]),
                                    reads=["kstg%d" % b], writes=["kT_%d_%d" % (xx, 2 * gp + 1)], dma=True)
                    for i in range(NT):
                        b = npj % 2
                        npj += 1
                        for k in range(KC):
                            P.add("pe", lambda k=k, i=i, b=b: nc.tensor.matmul(
                                pp[b][:], lhsT=hT[:, k, i * 128:(i + 1) * 128], rhs=slabB[:, k, :, 1, :],
                                start=(k == 0), stop=(k == KC - 1)),
                                reads=["slabB_%d" % k, "hT_%d" % i], writes=["ppB%d" % b])
                        for k in range(KC):
                            P.add("pe", lambda k=k, i=i: nc.tensor.matmul(
                                pg[:, 0:48], lhsT=hT[:, k, i * 128:(i + 1) * 128], rhs=wg[:, k, :],
                                start=(k == 0), stop=(k == KC - 1)),
                                reads=["wg_%d" % k, "hT_%d" % i], writes=["pgB"])
                        for xx in range(2):
                            P.add("act", lambda xx=xx, i=i, b=b: nc.scalar.copy(
                                out=v_aug[:, i, xx, :, 0:64],
                                in_=pp[b][:, xx * 256:(xx + 1) * 256].rearrange("p (g d) -> p g d", d=64)),
                                reads=["ppB%d" % b, "v_aug_init"], writes=["v_aug_%d" % i])
                        P.add("act", lambda i=i: nc.scalar.activation(out=gate_sb[:, i, :], in_=pg[:, 0:48], func=AF.Exp,
                                                                      scale=-1.0),
                              reads=["pgB"], writes=["gate_%d" % i])
                    gkeys = ["gate_%d" % i for i in range(NT)]
                    P.add("dve", lambda: nc.vector.tensor_scalar(out=gate_sb[:], in0=gate_sb[:], scalar1=1.0, scalar2=None,
                                                                 op0=ALU.add), reads=gkeys, writes=gkeys)
                    P.add("dve", lambda: nc.vector.reciprocal(out=gate_sb[:], in_=gate_sb[:]), reads=gkeys, writes=gkeys)

                P.barrier()
                with ExitStack() as s0:
                    sb0 = lambda name, shape, dt: s0.enter_context(nc.sbuf_tensor(name, list(shape), dt))
                    ps0 = lambda name, shape, dt: s0.enter_context(nc.psum_tensor(name, list(shape), dt))
                    slabA = sb0("slabA", [128, KC, 512], BF16)
                    cmpT = sb0("cmpT", [128, 4, S], BF16)
                    w1 = [sb0("w1_%d" % kv, [128, 32, 256], BF16) for kv in range(2)]
                    w2 = [sb0("w2_%d" % kv, [128, 2, 64], BF16) for kv in range(2)]
                    peT = [sb0("peT_%d" % kv, [64, 32], BF16) for kv in range(2)]
                    bias_sb = sb0("bias_sb", [128, 4], F32)
                    hact = [sb0("hact%d" % i, [128, 2, 127], BF16) for i in range(2)]
                    gx = [sb0("gx%d" % i, [128, 127], F32) for i in range(2)]
                    gu = [sb0("gu%d" % i, [128, 127], F32) for i in range(2)]
                    ppc = [ps0("ppc%d" % i, [128, 512], F32) for i in range(2)]
                    ph = [ps0("phB%d" % i, [128, 512], F32) for i in range(2)]
                    pb_ = ps0("pbB", [128, 512], F32)
                    pc = ps0("pcB", [128, 512], F32)
                    npj = 0
                    for k in range(KC):
                        P.add("pool", lambda k=k: nc.gpsimd.dma_start(
                            out=slabA[:, k, :], in_=w_in[k * 128:(k + 1) * 128, OFF_KC:OFF_KC + 512]),
                            writes=["slabA_%d" % k], dma=True)
                    w1d = [cmp_w1_k, cmp_w1_v]
                    w2d = [cmp_w2_k, cmp_w2_v]
                    ped = [cmp_pe_k, cmp_pe_v]
                    for kv in range(2):
                        for half in range(2):
                            for lq in range(4):
                                P.add("pool", lambda kv=kv, half=half, lq=lq: nc.gpsimd.dma_start(
                                    out=w1[kv][half * 64:(half + 1) * 64, lq * 8:(lq + 1) * 8, :],
                                    in_=w1d[kv].rearrange("(l d) h -> d l h", d=64)[:, lq * 8:(lq + 1) * 8, :]),
                                    writes=["w1_%d" % kv], dma=True)
                        P.add("pool", lambda kv=kv: nc.gpsimd.dma_start(
                            out=w2[kv][:], in_=w2d[kv].rearrange("(c p) d -> p c d", p=128)),
                            writes=["w2_%d" % kv], dma=True)
                        P.add("pool", lambda kv=kv: nc.gpsimd.dma_start(
                            out=peT[kv][:], in_=ped[kv].rearrange("l d -> d l"), allow_slow_non_contiguous=True),
                            writes=["peT_%d" % kv], dma=True)
                    for cc in range(4):
                        for tb in range(4):
                            b = npj % 2
                            npj += 1
                            for k in range(KC):
                                P.add("pe", lambda k=k, cc=cc, tb=tb, b=b: nc.tensor.matmul(
                                    ppc[b][:], lhsT=slabA[:, k, cc * 128:(cc + 1) * 128], rhs=hT[:, k, tb * 512:(tb + 1) * 512],
                                    start=(k == 0), stop=(k == KC - 1)),
                                    reads=["slabA_%d" % k] + hT_keys[tb * 4:tb * 4 + 4], writes=["ppc%d" % b])
                            P.add("act", lambda cc=cc, tb=tb, b=b: nc.scalar.copy(
                                out=cmpT[:, cc, tb * 512:(tb + 1) * 512], in_=ppc[b][:]),
                                reads=["ppc%d" % b], writes=["cmpT_%d" % cc])
                    for kv in range(2):
                        for hc in range(2):
                            col = kv * 2 + hc
                            for l in range(32):
                                P.add("pe", lambda kv=kv, hc=hc, l=l, col=col: nc.tensor.matmul(
                                    pb_[:, col:col + 1], lhsT=w1[kv][0:64, l, hc * 128:(hc + 1) * 128], rhs=peT[kv][:, l:l + 1],
                                    start=(l == 0), stop=(l == 31), skip_group_check=True),
                                    reads=["w1_%d" % kv, "peT_%d" % kv], writes=["pbB"])
                    P.add("dve", lambda: nc.vector.tensor_copy(out=bias_sb[:], in_=pb_[:, 0:4]), reads=["pbB"], writes=["bias_sb"])
                    nh = 0
                    for kv in range(2):
                        for g in range(4):
                            base = 64 * (g % 2)
                            cc = kv * 2 + g // 2
                            hb = nh % 2
                            nh += 1
                            for hc in range(2):
                                for l in range(32):
                                    P.add("pe", lambda kv=kv, hc=hc, l=l, base=base, cc=cc, hb=hb: nc.tensor.matmul(
                                        ph[hb][:, hc * 128:hc * 128 + 127],
                                        lhsT=w1[kv][base:base + 64, l, hc * 128:(hc + 1) * 128],
                                        rhs=cmpT[base:base + 64, cc, l:l + 2017:16],
                                        start=(l == 0 and hc == 0), stop=(l == 31), skip_group_check=True),
                                        reads=["w1_%d" % kv, "cmpT_%d" % cc], writes=["phB%d" % hb])
                            for hc in range(2):
                                gb = hc
                                col = kv * 2 + hc
                                P.add("dve", lambda hb=hb, hc=hc, gb=gb, col=col: nc.vector.tensor_scalar(
                                    out=gx[gb][:], in0=ph[hb][:, hc * 128:hc * 128 + 127], scalar1=bias_sb[:, col:col + 1],
                                    scalar2=None, op0=ALU.add),
                                    reads=["phB%d" % hb, "bias_sb"], writes=["gx%d" % gb])
                                P.add("dve", lambda gb=gb: nc.vector.tensor_tensor(out=gu[gb][:], in0=gx[gb][:], in1=gx[gb][:],
                                                                                   op=ALU.mult),
                                      reads=["gx%d" % gb], writes=["gu%d" % gb])
                                P.add("dve", lambda gb=gb: nc.vector.tensor_scalar(out=gu[gb][:], in0=gu[gb][:], scalar1=0.044715,
                                                                                   scalar2=1.0, op0=ALU.mult, op1=ALU.add),
                                      reads=["gu%d" % gb], writes=["gu%d" % gb])
                                P.add("dve", lambda gb=gb: nc.vector.tensor_tensor(out=gu[gb][:], in0=gu[gb][:], in1=gx[gb][:],
                                                                                   op=ALU.mult),
                                      reads=["gu%d" % gb, "gx%d" % gb], writes=["gu%d" % gb])
                                P.add("act", lambda gb=gb: nc.scalar.activation(out=gu[gb][:], in_=gu[gb][:], func=AF.Exp,
                                                                                scale=-1.5957691216057308),
                                      reads=["gu%d" % gb], writes=["gu%d" % gb])
                                P.add("dve", lambda gb=gb: nc.vector.tensor_scalar(out=gu[gb][:], in0=gu[gb][:], scalar1=1.0,
                                                                                   scalar2=None, op0=ALU.add),
                                      reads=["gu%d" % gb], writes=["gu%d" % gb])
                                P.add("dve", lambda gb=gb: nc.vector.reciprocal(out=gu[gb][:], in_=gu[gb][:]),
                                      reads=["gu%d" % gb], writes=["gu%d" % gb])
                                P.add("dve", lambda gb=gb, hb=hb, hc=hc: nc.vector.tensor_tensor(
                                    out=hact[hb][:, hc, :], in0=gx[gb][:], in1=gu[gb][:], op=ALU.mult),
                                    reads=["gx%d" % gb, "gu%d" % gb], writes=["hact%d_%d" % (hb, hc)])
                            hkeys = ["hact%d_%d" % (hb, hc) for hc in range(2)]
                            if kv == 0:
                                for hc in range(2):
                                    P.add("pe", lambda hb=hb, hc=hc: nc.tensor.matmul(
                                        pc[0:64, 0:127], lhsT=w2[0][:, hc, :], rhs=hact[hb][:, hc, :],
                                        start=(hc == 0), stop=(hc == 1)),
                                        reads=hkeys + ["w2_0"], writes=["pcB"])
                                P.add("act", lambda g=g: nc.scalar.copy(out=kcT_aug[0:64, g, :], in_=pc[0:64, 0:127]),
                                      reads=["pcB"], writes=["kcT_%d" % g])
                            else:
                                for hc in range(2):
                                    P.add("pe", lambda hb=hb, hc=hc: nc.tensor.matmul(
                                        pc[0:127, 0:64], lhsT=hact[hb][:, hc, :], rhs=w2[1][:, hc, :],
                                        start=(hc == 0), stop=(hc == 1)),
                                        reads=hkeys + ["w2_1"], writes=["pcB"])
                                P.add("act", lambda g=g: nc.scalar.copy(out=vc_aug[0:127, g, 0:64], in_=pc[0:127, 0:64]),
                                      reads=["pcB", "vc_aug_init"], writes=["vc_%d" % g])
                P.barrier()
                if "kc" in dbg:
                    with nc.sbuf_tensor("dbg_kc_sb", [128, 2, 4, 127], F32) as dkc:
                        P.add("dve", lambda: nc.vector.memset(dkc[:], 0.0), writes=["dkc"])
                        P.add("dve", lambda: nc.vector.tensor_copy(out=dkc[0:71, 0, :, :], in_=kcT_aug[0:71, :, :]), writes=["dkc"])
                        P.add("dve", lambda: nc.vector.tensor_copy(out=dkc[:, 1, :, 0:65], in_=vc_aug[:]), writes=["dkc"])
                        P.add("sp", lambda: nc.sync.dma_start(out=dbg["kc"], in_=dkc[:]), reads=["dkc"], dma=True)
                        P.barrier()

                if "noB2" in stages:
                    return
                with ExitStack() as s2:
                    sb2 = lambda name, shape, dt: s2.enter_context(nc.sbuf_tensor(name, list(shape), dt))
                    ps2 = lambda name, shape, dt: s2.enter_context(nc.psum_tensor(name, list(shape), dt))
                    wqs = sb2("wqs", [128, KC, 256], BF16)
                    cdiag = sb2("cdiag", [128, 2, 4, 128], BF16)
                    ccm = sb2("ccm", [128, NT, 128], BF16)
                    cmm = sb2("cmm", [128, 32], BF16)
                    csel = sb2("csel", [128, NT, 2, 32], F32)
                    ngb = sb2("ngb", [128, 1024], F32)
                    tiny = sb2("tiny", [128, 1], F32)
                    P.add("sp", lambda: nc.sync.dma_start(out=cdiag[:], in_=c_diag), writes=["cdiag"], dma=True)
                    P.add("sp", lambda: nc.sync.dma_start(out=ccm[:], in_=c_cmask), writes=["ccm"], dma=True)
                    P.add("sp", lambda: nc.sync.dma_start(out=cmm[:], in_=c_mmap), writes=["cmm"], dma=True)
                    P.add("sp", lambda: nc.sync.dma_start(out=csel[:], in_=c_sel), writes=["csel"], dma=True)
                    P.add("sp", lambda: nc.sync.dma_start(out=ngb[:], in_=nsa_gain.partition_broadcast(128)),
                          writes=["ngb"], dma=True)
                    P.add("dve", lambda: nc.vector.memset(tiny[:], 1e-30), writes=["tiny"])
                    qTa = [sb2("qT_aug%d" % i, [128, NT, 4, 128], BF16) for i in range(2)]
                    PT = [sb2("PT%d" % i, [128, 512], BF16) for i in range(3)]
                    negselT = [sb2("negsel%d" % i, [128, 128], F32) for i in range(2)]
                    coef = sb2("coef", [128, 3, 4], F32)
                    coefC = sb2("coefC", [128, 4], F32)
                    pslc = sb2("pslc", [128, 32], F32)
                    score = sb2("score", [128, 32], F32)
                    top8 = sb2("top8", [128, 8], F32)
                    oacc = [sb2("oacc%d" % i, [128, 4, 64], F32) for i in range(3)]
                    rinvE = [sb2("rinvE%d" % i, [128, 3, 4], F32) for i in range(3)]
                    otmp = sb2("otmp", [128, 4, 64], F32)
                    ssq = [sb2("ssqB%d" % i, [128, 4], F32) for i in range(3)]
                    rsd = [sb2("rsdB%d" % i, [128, 4], F32) for i in range(3)]
                    ostage = [sb2("ostB%d" % i, [128, 256], F32) for i in range(3)]
                    idf = sb2("idfB", [128, 128], F32)
                    qstg = [sb2("qstg%d" % i, [128, 512], BF16) for i in range(2)]
                    pS = [ps2("pS%d" % i, [128, 512], F32) for i in range(3)]
                    pOc = ps2("pOc0", [128, 512], F32)
                    pOs2 = [ps2("pOs%d" % i, [128, 512], F32) for i in range(2)]
                    pOw2 = [ps2("pOw%d" % i, [128, 512], F32) for i in range(2)]
                    P.add("sp", lambda: nc.sync.dma_start(out=idf[:], in_=c_f32[:, 2, :]), writes=["idfB"], dma=True)
                    for i_ in range(2):
                        P.add("dve", lambda i_=i_: nc.vector.memset(negselT[i_][:], 0.0), writes=["negsel%d" % i_])
                    for qb in range(2):
                        P.add("dve", lambda qb=qb: nc.vector.memset(qTa[qb][64:128, :, :, :], 0.0), writes=["qaug_init%d" % qb])
                    O3 = lambda t_: t_[:, 0:260].rearrange("p (r e) -> p r e", e=65)
                    kOc = "pOc0"

                    def load_wq(g):
                        for k in range(KC):
                            P.add("pool", lambda k=k: nc.gpsimd.dma_start(
                                out=wqs[:, k, :], in_=w_in[k * 128:(k + 1) * 128, OFF_QA + g * 256:OFF_QA + (g + 1) * 256]),
                                writes=["wqs_%d" % k], dma=True)

                    def load_qaug(g):
                        qb = g % 2
                        P.add("sp", lambda: nc.sync.dma_start(out=qTa[qb][64:71, :, :, :], in_=c_qaug[g]),
                              reads=["qaug_init%d" % qb], writes=["qaug_rows%d" % qb], dma=True)

                    stream = []
                    for r in range(2):
                        for tb in range(4):
                            stream.append(("qp", 0, r, tb))
                    for g in range(4):
                        gj = []
                        gj.append(("cmp", g, 0, 0))
                        for c in range(NT):
                            gj.append(("selpe", g, c, 0))
                            if c + 1 < NT:
                                gj.append(("cmp", g, c + 1, 0))
                            gj += [("win", g, c, m) for m in range(max(0, c - 4), c + 1)]
                            gj += [("slc", g, c, m) for m in range(c + 1)]
                        if g + 1 < 4:
                            qps = [("qp", g + 1, r, tb) for r in range(2) for tb in range(4)]
                            out_ = []
                            for ji, jb in enumerate(gj):
                                out_.append(jb)
                                if ji >= 40 and (ji - 40) % 20 == 0 and qps:
                                    out_.append(qps.pop(0))
                            out_ += qps
                            gj = out_
                        stream += gj
                    n_st = len(stream)

                    def qkeys(g, c):
                        qb = g % 2
                        return ["qT%d_%d" % (qb, c // 4), "qaug_rows%d" % qb, "nsT%d_%d" % (qb, c)]

                    def emit_S(idx):
                        job = stream[idx]
                        sb_ = idx % 3
                        kind = job[0]
                        if kind == "selpe":
                            return
                        if kind == "qp":
                            _, g, r, tb = job
                            for k in range(KC):
                                P.add("pe", lambda k=k: nc.tensor.matmul(
                                    pS[sb_][:], lhsT=wqs[:, k, r * 128:(r + 1) * 128], rhs=hT[:, k, tb * 512:(tb + 1) * 512],
                                    start=(k == 0), stop=(k == KC - 1)),
                                    reads=["wqs_%d" % k] + hT_keys[tb * 4:tb * 4 + 4], writes=["pS%d" % sb_])
                            return
                        _, g, c, m = job
                        qT_aug = qTa[g % 2]
                        ms = slice(m * 128, (m + 1) * 128)
                        if kind == "cmp":
                            P.add("pe", lambda: nc.tensor.matmul(
                                pS[sb_][0:127, :], lhsT=kcT_aug[:, g, :], rhs=qT_aug[:, c, :, :], start=True, stop=False,
                                skip_group_check=True),
                                reads=["kcT_%d" % g, "kcTaug_rows"] + qkeys(g, c), writes=["pS%d" % sb_])
                            for r in range(4):
                                P.add("pe", lambda r=r: nc.tensor.matmul(
                                    pS[sb_][0:127, r * 128:(r + 1) * 128], lhsT=ident[0:127, 0:127], rhs=ccm[0:127, c, :],
                                    start=False, stop=(r == 3), skip_group_check=True),
                                    reads=["ident", "ccm"], writes=["pS%d" % sb_])
                        else:
                            xx = 0 if kind == "slc" else 1
                            extra = []
                            if m == c:
                                extra.append(0)
                            if kind == "win" and m == c - 4:
                                extra.append(1)
                            P.add("pe", lambda: nc.tensor.matmul(
                                pS[sb_][:], lhsT=kT_aug[:, xx, g, ms], rhs=qT_aug[:, c, :, :], start=True,
                                stop=(len(extra) == 0), skip_group_check=True),
                                reads=["kT_%d_%d" % (xx, g), "kTaug_rows"] + qkeys(g, c), writes=["pS%d" % sb_])
                            for ei, di in enumerate(extra):
                                last = ei == len(extra) - 1
                                P.add("pe", lambda last=last, di=di: nc.tensor.matmul(
                                    pS[sb_][:], lhsT=ident[:], rhs=cdiag[:, di, :, :], start=False, stop=last,
                                    skip_group_check=True),
                                    reads=["ident", "cdiag"], writes=["pS%d" % sb_])

                    def emit_act(idx):
                        job = stream[idx]
                        sb_ = idx % 3
                        if job[0] == "selpe":
                            return
                        if job[0] == "qp":
                            _, g, r, tb = job
                            qT_aug = qTa[g % 2]
                            sg = idx % 2
                            P.add("act", lambda: nc.scalar.activation(
                                out=qT_aug[0:64, tb * 4:(tb + 1) * 4, 2 * r, :],
                                in_=pS[sb_][0:64, :].rearrange("p (c t) -> p c t", t=128), func=AF.Copy, scale=0.125),
                                reads=["pS%d" % sb_], writes=["qT%d_%d" % (g % 2, tb)])
                            P.add("act", lambda: nc.scalar.activation(
                                out=qstg[sg][64:128, :], in_=pS[sb_][64:128, :], func=AF.Copy, scale=0.125),
                                reads=["pS%d" % sb_], writes=["qstg%d" % sg])
                            P.add("sp", lambda: nc.sync.dma_start(
                                out=qT_aug[0:64, tb * 4:(tb + 1) * 4, 2 * r + 1, :],
                                in_=qstg[sg][64:128, :].rearrange("p (c t) -> p c t", t=128)),
                                reads=["qstg%d" % sg], writes=["qT%d_%d" % (g % 2, tb)], dma=True)
                            return
                        np_ = 127 if job[0] == "cmp" else 128
                        P.add("act", lambda: nc.scalar.activation(out=PT[sb_][0:np_, :], in_=pS[sb_][0:np_, :], func=AF.Exp),
                              reads=["pS%d" % sb_], writes=["PT%d" % sb_])

                    def emit_PV(idx):
                        job = stream[idx]
                        pb_i = idx % 3
                        kind = job[0]
                        if kind in ("qp", "selpe"):
                            return
                        _, g, c, m = job
                        if kind == "cmp":
                            for r in range(4):
                                P.add("pe", lambda r=r: nc.tensor.matmul(
                                    O3(pOc)[:, r, :], lhsT=PT[pb_i][0:127, r * 128:(r + 1) * 128], rhs=vc_aug[0:127, g, :],
                                    start=(r == 0), stop=True, skip_group_check=True),
                                    reads=["PT%d" % pb_i, "vc_%d" % g, "vc_aug_init"], writes=[kOc])
                            for r in range(4):
                                P.add("pe", lambda r=r: nc.tensor.matmul(
                                    pOc[:, 260 + r * 32:260 + (r + 1) * 32], lhsT=PT[pb_i][0:127, r * 128:(r + 1) * 128],
                                    rhs=cmm[0:127, :], start=False, stop=True, skip_group_check=True),
                                    reads=["PT%d" % pb_i, "cmm"], writes=[kOc])
                        else:
                            xx = 0 if kind == "slc" else 1
                            pO = pOs2[c % 2] if kind == "slc" else pOw2[c % 2]
                            key = ("pOs%d" if kind == "slc" else "pOw%d") % (c % 2)
                            m0 = 0 if kind == "slc" else max(0, c - 4)
                            for r in range(4):
                                P.add("pe", lambda r=r: nc.tensor.matmul(
                                    O3(pO)[:, r, :], lhsT=PT[pb_i][:, r * 128:(r + 1) * 128], rhs=v_aug[:, m, xx, g, :],
                                    start=(m == m0 and r == 0), stop=(m == c), skip_group_check=True),
                                    reads=["PT%d" % pb_i, "v_aug_%d" % m], writes=[key])

                    def emit_select(g, c):
                        ob = (g * NT + c) % 3
                        rk = "rinvE%d_0" % ob
                        P.add("dve", lambda: nc.vector.tensor_scalar(
                            out=rinvE[ob][:, 0, :], in0=O3(pOc)[:, :, 64], scalar1=tiny[:, 0:1], scalar2=None, op0=ALU.add),
                            reads=[kOc, "tiny"], writes=[rk])
                        P.add("dve", lambda: nc.vector.reciprocal(out=rinvE[ob][:, 0, :], in_=rinvE[ob][:, 0, :]),
                              reads=[rk], writes=[rk])
                        for r in range(4):
                            if r == 0:
                                P.add("dve", lambda: nc.vector.tensor_scalar(
                                    out=pslc[:], in0=pOc[:, 260:292], scalar1=rinvE[ob][:, 0, 0:1], scalar2=None, op0=ALU.mult),
                                    reads=[kOc, rk], writes=["pslc"])
                            else:
                                P.add("dve", lambda r=r: nc.vector.scalar_tensor_tensor(
                                    out=pslc[:], in0=pOc[:, 260 + r * 32:292 + r * 32], scalar=rinvE[ob][:, 0, r:r + 1],
                                    in1=pslc[:], op0=ALU.mult, op1=ALU.add),
                                    reads=[kOc, rk, "pslc"], writes=["pslc"])
                        P.add("dve", lambda: nc.vector.tensor_tensor(out=score[:], in0=pslc[:], in1=csel[:, c, 0, :], op=ALU.mult),
                              reads=["pslc", "csel"], writes=["score"])
                        P.add("dve", lambda: nc.vector.tensor_tensor(out=score[:], in0=score[:], in1=csel[:, c, 1, :], op=ALU.add),
                              reads=["score", "csel"], writes=["score"])
                        P.add("dve", lambda: nc.vector.max(out=top8[:], in_=score[:]), reads=["score"], writes=["top8"])
                        P.add("dve", lambda: nc.vector.tensor_scalar(
                            out=negselT[c % 2][:, 96:128], in0=score[:], scalar1=top8[:, 7:8], scalar2=-1.0,
                            op0=ALU.is_ge, op1=ALU.add),
                            reads=["score", "top8"], writes=["negsel%d" % (c % 2)])
                        gsl0 = gate_sb[:, c, g * 12:(g + 1) * 12].rearrange("p (r x) -> p x r", x=3)[:, 0, :]
                        P.add("dve", lambda: nc.vector.tensor_tensor(out=coefC[:], in0=rinvE[ob][:, 0, :], in1=gsl0, op=ALU.mult),
                              reads=[rk, "gate_%d" % c], writes=["coefC"])
                        P.add("dve", lambda: nc.vector.tensor_tensor(
                            out=oacc[ob][:], in0=O3(pOc)[:, :, 0:64], in1=coefC[:].unsqueeze(2).to_broadcast([128, 4, 64]),
                            op=ALU.mult),
                            reads=[kOc, "coefC"], writes=["oacc%d" % ob])

                    def emit_selpe(g, c):
                        qT_aug = qTa[g % 2]
                        pOs = pOs2[c % 2]
                        kOs = "pOs%d" % (c % 2)
                        P.add("pe", lambda: nc.tensor.transpose(out=pOs[:, 260:388], in_=negselT[c % 2][:], identity=idf[:]),
                              reads=["negsel%d" % (c % 2), "idfB"], writes=[kOs])
                        P.add("dve", lambda: nc.vector.tensor_copy(
                            out=qT_aug[96:128, c, :, :], in_=pOs[96:128, 260:388].unsqueeze(1).to_broadcast([32, 4, 128])),
                            reads=[kOs, "qaug_init%d" % (g % 2)], writes=["nsT%d_%d" % (g % 2, c)])

                    def epi_part1(g, c):
                        ob = (g * NT + c) % 3
                        pOs, pOw = pOs2[c % 2], pOw2[c % 2]
                        kOs, kOw = "pOs%d" % (c % 2), "pOw%d" % (c % 2)
                        for bi, pO, key in ((1, pOs, kOs), (2, pOw, kOw)):
                            P.add("dve", lambda bi=bi, pO=pO: nc.vector.reciprocal(out=rinvE[ob][:, bi, :], in_=O3(pO)[:, :, 64]),
                                  reads=[key], writes=["rinvE%d_%d" % (ob, bi)])
                        gsl = gate_sb[:, c, g * 12:(g + 1) * 12].rearrange("p (r x) -> p x r", x=3)
                        P.add("dve", lambda: nc.vector.tensor_tensor(out=coef[:], in0=rinvE[ob][:], in1=gsl, op=ALU.mult),
                              reads=["rinvE%d_%d" % (ob, bi) for bi in range(3)] + ["gate_%d" % c], writes=["coef"])
                        for bi, pO, key in ((1, pOs, kOs), (2, pOw, kOw)):
                            P.add("dve", lambda bi=bi, pO=pO: nc.vector.tensor_tensor(
                                out=otmp[:], in0=O3(pO)[:, :, 0:64],
                                in1=coef[:, bi, :].unsqueeze(2).to_broadcast([128, 4, 64]), op=ALU.mult),
                                reads=[key, "coef"], writes=["otmp"])
                            P.add("dve", lambda: nc.vector.tensor_tensor(out=oacc[ob][:], in0=oacc[ob][:], in1=otmp[:], op=ALU.add),
                                  reads=["oacc%d" % ob, "otmp"], writes=["oacc%d" % ob])
                        P.add("dve", lambda: nc.vector.tensor_tensor(out=otmp[:], in0=oacc[ob][:], in1=oacc[ob][:], op=ALU.mult),
                              reads=["oacc%d" % ob], writes=["otmp"])
                        P.add("dve", lambda: nc.vector.tensor_reduce(out=ssq[ob][:], in_=otmp[:], axis=AX.X, op=ALU.add),
                              reads=["otmp"], writes=["ssqB%d" % ob])

                    def epi_tail(g, c):
                        ob = (g * NT + c) % 3
                        rsqrt(rsd[ob][:], ssq[ob][:], 1.0 / 64, "ssqB%d" % ob, "rsdB%d" % ob)
                        P.add("dve", lambda: nc.vector.tensor_tensor(
                            out=oacc[ob][:], in0=oacc[ob][:], in1=rsd[ob][:].unsqueeze(2).to_broadcast([128, 4, 64]), op=ALU.mult),
                            reads=["oacc%d" % ob, "rsdB%d" % ob], writes=["oacc%d" % ob])
                        P.add("dve", lambda: nc.vector.tensor_tensor(
                            out=ostage[ob][:], in0=oacc[ob][:].rearrange("p r d -> p (r d)"), in1=ngb[:, g * 256:(g + 1) * 256],
                            op=ALU.mult),
                            reads=["oacc%d" % ob, "ngb"], writes=["ostB%d" % ob])
                        P.add("sp", lambda: nc.sync.dma_start(
                            out=o_scr[c * 128:(c + 1) * 128, g * 256:(g + 1) * 256], in_=ostage[ob][:]),
                            reads=["ostB%d" % ob], writes=["o_scr"], dma=True)

                    load_wq(0)
                    load_qaug(0)
                    sel_done = set()
                    deferred = []
                    pend_p1, pend_tail = [], []
                    s_emitted = set()

                    def try_S(idx):
                        if idx >= n_st or idx in s_emitted:
                            return
                        job = stream[idx]
                        if job[0] == "slc" and (job[1], job[2]) not in sel_done:
                            deferred.append(idx)
                            return
                        s_emitted.add(idx)
                        emit_S(idx)

                    try_S(0)
                    try_S(1)
                    for idx, job in enumerate(stream):
                        try_S(idx + 2)
                        if idx not in s_emitted:
                            s_emitted.add(idx)
                            if idx in deferred:
                                deferred.remove(idx)
                            emit_S(idx)
                        emit_act(idx)
                        emit_PV(idx)
                        kind = job[0]
                        if kind == "qp":
                            _, g_, r_, tb_ = job
                            if r_ == 1 and tb_ == 3 and g_ + 1 < 4:
                                load_wq(g_ + 1)
                            continue
                        _, g, c, m = job
                        if kind == "selpe":
                            emit_selpe(g, c)
                            sel_done.add((g, c))
                            for d_ in list(deferred):
                                deferred.remove(d_)
                                try_S(d_)
                            continue
                        if kind == "cmp":
                            if c == 2 and g + 1 < 4:
                                load_qaug(g + 1)
                            seq = g * NT + c
                            while pend_tail and pend_tail[0][3] <= seq - 3:
                                gt, ct, _, _ = pend_tail.pop(0)
                                epi_tail(gt, ct)
                            emit_select(g, c)
                            while pend_p1:
                                epi_part1(*pend_p1.pop(0))
                        for pt_ in pend_tail:
                            pt_[2] -= 1
                        while pend_tail and pend_tail[0][2] <= 0 and (pend_tail[0][0], pend_tail[0][1]) not in pend_p1:
                            gt, ct, _, _ = pend_tail.pop(0)
                            epi_tail(gt, ct)
                        if kind == "slc" and m == c:
                            pend_p1.append((g, c))
                            pend_tail.append([g, c, 24, g * NT + c])
                    while pend_p1:
                        epi_part1(*pend_p1.pop(0))
                    while pend_tail:
                        gt, ct, _, _ = pend_tail.pop(0)
                        epi_tail(gt, ct)
        if "B" in stages:
            phase_B()
            P.barrier()

        HB = 2

        def phase_C():
            with ExitStack() as sc:
                sbc = lambda name, shape, dt: sc.enter_context(nc.sbuf_tensor(name, list(shape), dt))
                psc = lambda name, shape, dt: sc.enter_context(nc.psum_tensor(name, list(shape), dt))
                cf = sbc("cf", [128, 4, 128], F32)
                P.add("sp", lambda: nc.sync.dma_start(out=cf[:], in_=c_f32), writes=["cf"], dma=True)
                U2, L2, IDF = cf[:, 0, :], cf[:, 1, :], cf[:, 2, :]
                lbb = sbc("lbb", [128, 1024], F32)
                oml = sbc("oml", [128, 1024], F32)
                hgb = sbc("hgb", [128, 1024], F32)
                P.add("sp", lambda: nc.sync.dma_start(out=lbb[:], in_=lower_bounds[0].partition_broadcast(128)),
                      writes=["lbb"], dma=True)
                P.add("sp", lambda: nc.sync.dma_start(out=oml[:], in_=lower_bounds[1].partition_broadcast(128)),
                      writes=["oml"], dma=True)
                P.add("sp", lambda: nc.sync.dma_start(out=hgb[:], in_=hgrn_gain.partition_broadcast(128)),
                      writes=["hgb"], dma=True)
                P.add("dve", lambda: nc.vector.tensor_tensor(out=oml[:], in0=oml[:], in1=lbb[:], op=ALU.subtract),
                      reads=["lbb", "oml"], writes=["oml"])
                P.add("act", lambda: nc.scalar.activation(out=oml[:], in_=oml[:], func=AF.Exp), reads=["oml"], writes=["oml"])
                P.add("dve", lambda: nc.vector.tensor_scalar(out=oml[:], in0=oml[:], scalar1=1.0, scalar2=None, op0=ALU.add),
                      reads=["oml"], writes=["oml"])
                P.add("dve", lambda: nc.vector.reciprocal(out=lbb[:], in_=oml[:]), reads=["oml"], writes=["lbb"])
                P.add("dve", lambda: nc.vector.tensor_scalar(out=oml[:], in0=lbb[:], scalar1=-1.0, scalar2=1.0,
                                                             op0=ALU.mult, op1=ALU.add), reads=["lbb"], writes=["oml"])

                W = HB * 128
                wq = sbc("wq", [128, KC, W], BF16)
                wfi = sbc("wfi", [128, KC, 2, W], BF16)
                qT = sbc("qTh", [128, HB, S], BF16)
                logf = sbc("logf", [128, NT, W], F32)
                kk = sbc("kk", [128, NT, W], F32)
                vv = sbc("vv", [128, NT, W], BF16)
                S32 = sbc("S32", [128, HB, 128], F32)
                Sbf = sbc("Sbf", [128, HB, 128], BF16)
                tmpe = [sbc("tmpe%d" % i, [128, W], F32) for i in range(2)]
                tmpf = [sbc("tmpf%d" % i, [128, W], F32) for i in range(2)]
                ebT = [sbc("ebT%d" % i, [128, HB, 128], F32) for i in range(2)]
                enbT = [sbc("enbT%d" % i, [128, HB, 128], F32) for i in range(2)]
                erev = [sbc("erev%d" % i, [128, HB, 128], F32) for i in range(2)]
                qbT = [sbc("qbT%d" % i, [128, HB, 128], BF16) for i in range(2)]
                kbT = [sbc("kbT%d" % i, [128, HB, 128], BF16) for i in range(2)]
                kd = [sbc("kd%d" % i, [128, HB, 128], BF16) for i in range(2)]
                ATm = [sbc("ATm%d" % i, [128, HB, 128], BF16) for i in range(2)]
                ost = [sbc("ost%d" % i, [128, W], F32) for i in range(2)]
                junk = sbc("junkC", [128, 128], BF16)
                ssq = [sbc("ssqC%d" % i, [128, HB], F32) for i in range(2)]
                rsd = [sbc("rsdC%d" % i, [128, HB], F32) for i in range(2)]
                pproj = [psc("ppC%d" % i, [128, 512], F32) for i in range(2)]
                pAb = [psc("pA_%d" % hd, [128, 4, 128], F32) for hd in range(HB)]
                pA = [pAb, pAb]
                pOb = [psc("pO_%d" % par, [128, HB, 128], F32) for par in range(2)]
                pSb = [psc("pS_%d" % hd, [128, 128], F32) for hd in range(HB)]

                npj = 0
                for h0 in range(0, 8, HB):
                    for k in range(KC):
                        P.add("pool", lambda k=k, h0=h0: nc.gpsimd.dma_start(
                            out=wq[:, k, :], in_=w_in[k * 128:(k + 1) * 128, OFF_QH + h0 * 128:OFF_QH + h0 * 128 + W]),
                            writes=["wq_%d" % k], dma=True)
                        P.add("pool", lambda k=k, h0=h0: nc.gpsimd.dma_start(
                            out=wfi[:, k, 0, :], in_=w_in[k * 128:(k + 1) * 128, OFF_FH + h0 * 128:OFF_FH + h0 * 128 + W]),
                            writes=["wf_%d" % k], dma=True)
                        P.add("pool", lambda k=k, h0=h0: nc.gpsimd.dma_start(
                            out=wfi[:, k, 1, :], in_=w_in[k * 128:(k + 1) * 128, OFF_IH + h0 * 128:OFF_IH + h0 * 128 + W]),
                            writes=["wi_%d" % k], dma=True)
                    for hd in range(HB):
                        for tb in range(4):
                            pp = npj % 2
                            npj += 1
                            for k in range(KC):
                                P.add("pe", lambda k=k, hd=hd, tb=tb, pp=pp: nc.tensor.matmul(
                                    pproj[pp][:], lhsT=wq[:, k, hd * 128:(hd + 1) * 128], rhs=hT[:, k, tb * 512:(tb + 1) * 512],
                                    start=(k == 0), stop=(k == KC - 1)),
                                    reads=["wq_%d" % k] + hT_keys[tb * 4:tb * 4 + 4], writes=["ppC%d" % pp])
                            P.add("act", lambda hd=hd, tb=tb, pp=pp: nc.scalar.copy(
                                out=qT[:, hd, tb * 512:(tb + 1) * 512], in_=pproj[pp][:]),
                                reads=["ppC%d" % pp], writes=["qTh_%d_%d" % (hd, tb)])
                    for i in range(NT):
                        pp = npj % 2
                        npj += 1
                        b = i % 2
                        for k in range(KC):
                            P.add("pe", lambda k=k, i=i, pp=pp: nc.tensor.matmul(
                                pproj[pp][:], lhsT=hT[:, k, i * 128:(i + 1) * 128], rhs=wfi[:, k, :, :],
                                start=(k == 0), stop=(k == KC - 1)),
                                reads=["wf_%d" % k, "wi_%d" % k, "hT_%d" % i], writes=["ppC%d" % pp])
                        P.add("act", lambda pp=pp, b=b: nc.scalar.activation(out=tmpe[b][:], in_=pproj[pp][:, 0:W],
                                                                             func=AF.Exp, scale=-1.0),
                              reads=["ppC%d" % pp], writes=["tmpe%d" % b])
                        P.add("act", lambda pp=pp, i=i: nc.scalar.copy(out=vv[:, i, :], in_=pproj[pp][:, W:2 * W]),
                              reads=["ppC%d" % pp], writes=["vv_%d" % i])
                        P.add("dve", lambda b=b: nc.vector.tensor_scalar(out=tmpe[b][:], in0=tmpe[b][:], scalar1=1.0,
                                                                         scalar2=None, op0=ALU.add),
                              reads=["tmpe%d" % b], writes=["tmpe%d" % b])
                        P.add("dve", lambda b=b: nc.vector.reciprocal(out=tmpe[b][:], in_=tmpe[b][:]),
                              reads=["tmpe%d" % b], writes=["tmpe%d" % b])
                        P.add("dve", lambda b=b, h0=h0: nc.vector.tensor_tensor(
                            out=tmpf[b][:], in0=tmpe[b][:], in1=oml[:, h0 * 128:h0 * 128 + W], op=ALU.mult),
                            reads=["tmpe%d" % b, "oml"], writes=["tmpf%d" % b])
                        P.add("dve", lambda b=b, h0=h0: nc.vector.tensor_tensor(
                            out=tmpf[b][:], in0=tmpf[b][:], in1=lbb[:, h0 * 128:h0 * 128 + W], op=ALU.add),
                            reads=["tmpf%d" % b, "lbb"], writes=["tmpf%d" % b])
                        P.add("act", lambda b=b, i=i: nc.scalar.activation(out=logf[:, i, :], in_=tmpf[b][:], func=AF.Ln),
                              reads=["tmpf%d" % b], writes=["logf_%d" % i])
                        P.add("dve", lambda b=b, i=i: nc.vector.tensor_scalar(
                            out=kk[:, i, :], in0=tmpf[b][:], scalar1=-1.0, scalar2=1.0, op0=ALU.mult, op1=ALU.add),
                            reads=["tmpf%d" % b], writes=["kk_%d" % i])
                    P.add("dve", lambda: nc.vector.memset(S32[:], 0.0), writes=["S32_%d" % hd for hd in range(HB)])
                    P.add("dve", lambda: nc.vector.memset(Sbf[:], 0.0), writes=["Sbf_%d" % hd for hd in range(HB)])
                    for i in range(NT):
                        par = i % 2
                        tb = i // 4
                        hs = lambda hd: slice(hd * 128, (hd + 1) * 128)
                        for hd in range(HB):
                            P.add("pe", lambda i=i, hd=hd, par=par: nc.tensor.matmul(
                                pA[par][hd][:, 0, :], lhsT=logf[:, i, hs(hd)], rhs=U2, start=True, stop=True),
                                reads=["logf_%d" % i, "cf"], writes=["pA_%d" % hd])
                            P.add("pe", lambda i=i, hd=hd, par=par: nc.tensor.matmul(
                                pA[par][hd][:, 1, :], lhsT=L2, rhs=logf[:, i, hs(hd)], start=True, stop=True),
                                reads=["logf_%d" % i, "cf"], writes=["pA_%d" % hd])
                            P.add("pe", lambda i=i, hd=hd, par=par: nc.tensor.transpose(
                                out=pA[par][hd][:, 2, :], in_=kk[:, i, hs(hd)], identity=IDF),
                                reads=["kk_%d" % i, "cf"], writes=["pA_%d" % hd])
                        for hd in range(HB):
                            P.add("act", lambda hd=hd, par=par: nc.scalar.activation(
                                out=ebT[par][:, hd, :], in_=pA[par][hd][:, 0, :], func=AF.Exp),
                                reads=["pA_%d" % hd], writes=["ebT%d_%d" % (par, hd)])
                            P.add("act", lambda hd=hd, par=par: nc.scalar.activation(
                                out=enbT[par][:, hd, :], in_=pA[par][hd][:, 0, :], func=AF.Exp, scale=-1.0),
                                reads=["pA_%d" % hd], writes=["enbT%d_%d" % (par, hd)])
                            P.add("act", lambda hd=hd, par=par: nc.scalar.activation(
                                out=erev[par][:, hd, :], in_=pA[par][hd][:, 1, :], func=AF.Exp),
                                reads=["pA_%d" % hd], writes=["erev%d_%d" % (par, hd)])
                        for hd in range(HB):
                            P.add("dve", lambda i=i, hd=hd, par=par: nc.vector.tensor_tensor(
                                out=qbT[par][:, hd, :], in0=qT[:, hd, i * 128:(i + 1) * 128], in1=ebT[par][:, hd, :], op=ALU.mult),
                                reads=["qTh_%d_%d" % (hd, tb), "ebT%d_%d" % (par, hd)], writes=["qbT%d_%d" % (par, hd)])
                            P.add("dve", lambda hd=hd, par=par: nc.vector.tensor_tensor(
                                out=kbT[par][:, hd, :], in0=pA[par][hd][:, 2, :], in1=enbT[par][:, hd, :], op=ALU.mult),
                                reads=["pA_%d" % hd, "enbT%d_%d" % (par, hd)], writes=["kbT%d_%d" % (par, hd)])
                            P.add("dve", lambda i=i, hd=hd, par=par: nc.vector.tensor_tensor(
                                out=kd[par][:, hd, :], in0=kk[:, i, hs(hd)], in1=erev[par][:, hd, :], op=ALU.mult),
                                reads=["kk_%d" % i, "erev%d_%d" % (par, hd)], writes=["kd%d_%d" % (par, hd)])
                        for hd in range(HB):
                            P.add("pe", lambda hd=hd, par=par: nc.tensor.matmul(
                                pA[par][hd][:, 3, :], lhsT=kbT[par][:, hd, :], rhs=qbT[par][:, hd, :], start=True, stop=True),
                                reads=["kbT%d_%d" % (par, hd), "qbT%d_%d" % (par, hd)], writes=["pA_%d" % hd])
                        for hd in range(HB):
                            P.add("dve", lambda hd=hd, par=par: nc.vector.tensor_tensor(
                                out=ATm[par][:, hd, :], in0=pA[par][hd][:, 3, :], in1=U2, op=ALU.mult),
                                reads=["pA_%d" % hd, "cf"], writes=["ATm%d_%d" % (par, hd)])
                        for ch in range(2):
                            cs = slice(ch * 64, (ch + 1) * 64)
                            for hd in range(HB):
                                if ch == 0:
                                    P.add("pe", lambda i=i, hd=hd, par=par: nc.tensor.matmul(
                                        pOb[par][:, hd, :], lhsT=ATm[par][:, hd, :], rhs=vv[:, i, hs(hd)],
                                        start=(hd == 0), stop=False, skip_group_check=True),
                                        reads=["ATm%d_%d" % (par, hd), "vv_%d" % i], writes=["pO_%d" % par])
                                P.add("pe", lambda hd=hd, par=par, cs=cs, ch=ch: nc.tensor.matmul(
                                    pOb[par][cs, hd, :], lhsT=qbT[par][:, hd, cs], rhs=Sbf[:, hd, :],
                                    start=False, stop=(ch == 1), skip_group_check=True),
                                    reads=["qbT%d_%d" % (par, hd), "Sbf_%d" % hd], writes=["pO_%d" % par])
                                P.add("pe", lambda i=i, hd=hd, par=par, cs=cs: nc.tensor.matmul(
                                    pSb[hd][:], lhsT=kd[par][cs, hd, :], rhs=vv[cs, i, hs(hd)],
                                    start=True, stop=True),
                                    reads=["kd%d_%d" % (par, hd), "vv_%d" % i], writes=["pS_%d" % hd])
                            for hd in range(HB):
                                col = ch * 64 + 63
                                P.add("dve", lambda hd=hd, par=par, col=col: nc.vector.scalar_tensor_tensor(
                                    out=S32[:, hd, :], in0=S32[:, hd, :], scalar=ebT[par][:, hd, col:col + 1],
                                    in1=pSb[hd][:], op0=ALU.mult, op1=ALU.add),
                                    reads=["S32_%d" % hd, "ebT%d_%d" % (par, hd), "pS_%d" % hd], writes=["S32_%d" % hd])
                                P.add("act", lambda hd=hd: nc.scalar.copy(out=Sbf[:, hd, :], in_=S32[:, hd, :]),
                                      reads=["S32_%d" % hd], writes=["Sbf_%d" % hd])
                        for hd in range(HB):
                            P.add("act", lambda hd=hd, par=par: nc.scalar.activation(
                                out=junk[:], in_=pOb[par][:, hd, :], func=AF.Square, accum_out=ssq[par][:, hd:hd + 1]),
                                reads=["pO_%d" % par], writes=["junkC", "ssqC%d_%d" % (par, hd)])
                        rsqrt(rsd[par][:], ssq[par][:], 1.0 / 128, "ssqC%d" % par, "rsdC%d" % par,
                              rkeys=["ssqC%d_%d" % (par, hd) for hd in range(HB)])
                        for hd in range(HB):
                            P.add("dve", lambda hd=hd, par=par, h0=h0: nc.vector.scalar_tensor_tensor(
                                out=ost[par][:, hs(hd)], in0=pOb[par][:, hd, :], scalar=rsd[par][:, hd:hd + 1],
                                in1=hgb[:, (h0 + hd) * 128:(h0 + hd + 1) * 128], op0=ALU.mult, op1=ALU.mult),
                                reads=["pO_%d" % par, "rsdC%d" % par, "hgb"], writes=["ost%d" % par])
                        P.add("sp", lambda i=i, par=par, h0=h0: nc.sync.dma_start(
                            out=o_scr[i * 128:(i + 1) * 128, 1024 + h0 * 128:1024 + h0 * 128 + W], in_=ost[par][:]),
                            reads=["ost%d" % par], writes=["o_scr"], dma=True)
        if "C" in stages:
            phase_C()
            P.barrier()

        mixT = st.enter_context(nc.sbuf_tensor("mixT", [128, KC, S], BF16))

        def phase_D():
            with ExitStack() as sdd:
                sbd = lambda name, shape, dt: sdd.enter_context(nc.sbuf_tensor(name, list(shape), dt))
                wz = [wz0, sbd("wz1", [128, KC, 512], BF16)]
                ot = [sbd("ot%d" % i, [128, 512], F32) for i in range(2)]
                sz = [sbd("sz%d" % i, [128, 512], F32) for i in range(2)]
                mx = [sbd("mx%d" % i, [128, 512], BF16) for i in range(2)]
                pz = [sdd.enter_context(nc.psum_tensor("pz%d" % i, [128, 512], F32)) for i in range(2)]
                pt = [sdd.enter_context(nc.psum_tensor("ptD%d" % i, [128, 4, 128], BF16)) for i in range(2)]
                zcols = [OFF_ZA, OFF_ZA + 512, OFF_ZH, OFF_ZH + 512]
                n = 0
                pend_tr = []
                for zb in range(4):
                    wb = zb % 2
                    for k in range(KC):
                        if zb == 0:
                            break
                        P.add("pool", lambda k=k, wb=wb, zb=zb: nc.gpsimd.dma_start(
                            out=wz[wb][:, k, :], in_=w_in[k * 128:(k + 1) * 128, zcols[zb]:zcols[zb] + 512]),
                            writes=["wz%d_%d" % (wb, k)], dma=True)
                    for i in range(NT):
                        b = n % 2
                        n += 1
                        P.add("sp", lambda i=i, b=b, zb=zb: nc.sync.dma_start(
                            out=ot[b][:], in_=o_scr[i * 128:(i + 1) * 128, zb * 512:(zb + 1) * 512]),
                            reads=["o_scr"], writes=["ot%d" % b], dma=True)
                        for k in range(KC):
                            P.add("pe", lambda i=i, b=b, k=k, wb=wb: nc.tensor.matmul(
                                pz[b][:], lhsT=hT[:, k, i * 128:(i + 1) * 128], rhs=wz[wb][:, k, :],
                                start=(k == 0), stop=(k == KC - 1)),
                                reads=["hT_%d" % i, "wz%d_%d" % (wb, k)], writes=["pz%d" % b])
                        while pend_tr:
                            pend_tr.pop(0)()
                        P.add("act", lambda b=b: nc.scalar.activation(out=sz[b][:], in_=pz[b][:], func=AF.Silu),
                              reads=["pz%d" % b], writes=["sz%d" % b])
                        P.add("dve", lambda b=b: nc.vector.tensor_tensor(out=mx[b][:], in0=sz[b][:], in1=ot[b][:],
                                                                         op=ALU.mult),
                              reads=["sz%d" % b, "ot%d" % b], writes=["mx%d" % b])
                        def tr(b=b, i=i, zb=zb):
                            for kk in range(4):
                                P.add("pe", lambda kk=kk: nc.tensor.transpose(
                                    out=pt[b][:, kk, :], in_=mx[b][:, kk * 128:(kk + 1) * 128], identity=ident[:]),
                                    reads=["mx%d" % b, "ident"], writes=["ptD%d" % b])
                            P.add("dve", lambda: nc.vector.tensor_copy(
                                out=mixT[:, zb * 4:(zb + 1) * 4, i * 128:(i + 1) * 128], in_=pt[b][:]),
                                reads=["ptD%d" % b], writes=["mixT_%d" % i])
                        pend_tr.append(tr)
                while pend_tr:
                    pend_tr.pop(0)()
        if "D" in stages:
            phase_D()
            P.barrier()

        def phase_F():
            with ExitStack() as sf:
                sbf = lambda name, shape, dt: sf.enter_context(nc.sbuf_tensor(name, list(shape), dt))
                wo = hT
                fg = sbf("fg", [128, D], F32)
                xt = [sbf("xtF%d" % i, [128, D], F32) for i in range(2)]
                rt = [sbf("rtF%d" % i, [128, D], F32) for i in range(2)]
                yo = [sbf("yoF%d" % i, [128, D], F32) for i in range(2)]
                junk = sbf("junkF", [128, D], BF16)
                ssq = [sbf("ssqF%d" % i, [128, 1], F32) for i in range(2)]
                rstd = [sbf("rstdF%d" % i, [128, 1], F32) for i in range(2)]
                py = [sf.enter_context(nc.psum_tensor("py%d" % i, [128, 512], F32)) for i in range(8)]
                for k in range(KC):
                    P.add("pool", lambda k=k: nc.gpsimd.dma_start(out=wo[:, k, :], in_=w_out[k * 128:(k + 1) * 128, :]),
                          writes=["wo_%d" % k], dma=True)
                P.add("sp", lambda: nc.sync.dma_start(out=fg[:], in_=final_norm.partition_broadcast(128)),
                      writes=["fg"], dma=True)
                for i in range(NT):
                    b = i % 2
                    P.add("sp", lambda i=i, b=b: nc.sync.dma_start(out=xt[b][:], in_=x[i * 128:(i + 1) * 128, :]),
                          writes=["xtF%d" % b], dma=True)
                    for k in range(KC):
                        for nb in range(4):
                            pb = b * 4 + nb
                            P.add("pe", lambda i=i, k=k, nb=nb, pb=pb: nc.tensor.matmul(
                                py[pb][:], lhsT=mixT[:, k, i * 128:(i + 1) * 128], rhs=wo[:, k, nb * 512:(nb + 1) * 512],
                                start=(k == 0), stop=(k == KC - 1)),
                                reads=["mixT_%d" % i, "wo_%d" % k], writes=["py%d" % pb])
                    for nb in range(4):
                        pb = b * 4 + nb
                        P.add("dve", lambda b=b, nb=nb, pb=pb: nc.vector.tensor_tensor(
                            out=rt[b][:, nb * 512:(nb + 1) * 512], in0=py[pb][:], in1=xt[b][:, nb * 512:(nb + 1) * 512],
                            op=ALU.add),
                            reads=["py%d" % pb, "xtF%d" % b], writes=["rtF%d_%d" % (b, nb)])
                    rkeys = ["rtF%d_%d" % (b, nb) for nb in range(4)]
                    P.add("act", lambda b=b: nc.scalar.activation(out=junk[:], in_=rt[b][:], func=AF.Square,
                                                                  accum_out=ssq[b][:]),
                          reads=rkeys, writes=["junkF", "ssqF%d" % b])
                    rsqrt(rstd[b][:], ssq[b][:], 1.0 / D, "ssqF%d" % b, "rstdF%d" % b)
                    P.add("dve", lambda b=b: nc.vector.scalar_tensor_tensor(
                        out=yo[b][:], in0=rt[b][:], scalar=rstd[b][:, 0:1], in1=fg[:], op0=ALU.mult, op1=ALU.mult),
                        reads=rkeys + ["rstdF%d" % b, "fg"], writes=["yoF%d" % b])
                    P.add("sp", lambda i=i, b=b: nc.sync.dma_start(out=out[i * 128:(i + 1) * 128, :], in_=yo[b][:]),
                          reads=["yoF%d" % b], dma=True)
        if "F" in stages:
            phase_F()
        nops, nwaits = P.emit(st)
    return nc, (nops, nwaits)


def make_consts():
    c = {}
    c["c_ident"] = bf(np.eye(128))
    blk = np.arange(128) // 64
    same = blk[:, None] == blk[None, :]
    ii = np.arange(128)
    U2 = (same & (ii[:, None] <= ii[None, :])).astype(np.float32)
    L2 = (same & (ii[:, None] > ii[None, :])).astype(np.float32)
    cb = np.zeros((128, 128), np.float32)
    cb[64:, 0] = -80.0
    cb[:64, 1] = -80.0
    c["c_f32"] = np.ascontiguousarray(np.stack([U2, L2, np.eye(128, dtype=np.float32), cb], axis=1))

    pos = np.arange(S)
    E = np.zeros((128, S), np.float32)
    E[pos // 64, pos] = -NEG
    c["c_E"] = bf(E)
    kl = np.arange(128)[:, None]
    tl = np.arange(128)[None, :]
    diag = np.where(kl <= tl, 0.0, NEG)
    far = np.where(tl < kl, 0.0, NEG)
    dd = np.stack([diag, far], axis=0)[:, None, :, :].repeat(4, axis=1)
    c["c_diag"] = bf(np.ascontiguousarray(dd.transpose(2, 0, 1, 3)))
    n = np.arange(128)[:, None, None]
    cch = np.arange(NT)[None, :, None]
    tt = np.arange(128)[None, None, :]
    c["c_cmask"] = bf(np.where(16 * n + 31 <= 128 * cch + tt, 0.0, NEG))
    cs_ = 16 * np.arange(127)[:, None]
    ss_ = 64 * np.arange(32)[None, :]
    ov = np.clip(np.minimum(cs_ + 32, ss_ + 64) - np.maximum(cs_, ss_), 0, None) / 32.0
    mm = np.zeros((128, 32), np.float32)
    mm[:127] = ov
    c["c_mmap"] = bf(mm)
    t_abs = (128 * np.arange(NT)[None, :, None] + np.arange(128)[:, None, None])
    j = np.arange(32)[None, None, :]
    cur = t_abs // 64
    forced = ((j == 0) | (j == cur) | (j == cur - 1)).astype(np.float32)
    future = (j * 64 > t_abs).astype(np.float32)
    c["c_sel"] = np.ascontiguousarray(np.stack([1.0 - future, 1e4 * forced * (1.0 - future) - future], axis=2).astype(np.float32))

    def split_rows(p):
        a = (p // 128) * 128
        b_ = p % 128
        one = np.ones_like(p)
        return np.stack([a, a, b_, b_, one, one, one], axis=0).astype(np.float32)
    c["c_kaug"] = bf(split_rows(pos))
    c["c_kcaug"] = bf(split_rows(16 * np.arange(127) + 31))
    qa = np.zeros((4, 7, 4, S), np.float32)
    for g in range(4):
        for r in range(4):
            sl = np.float32(2.0 ** (-(4 * g + r + 1) / 2.0))
            s_hi = np.float32(bf(sl))
            s_lo = np.float32(bf(sl - s_hi))
            sp_ = np.float64(s_hi) + np.float64(s_lo)
            st = sp_ * pos.astype(np.float64)
            st1 = bf(st).astype(np.float64)
            st2 = bf(st - st1).astype(np.float64)
            st3 = bf(st - st1 - st2).astype(np.float64)
            qa[g, 0, r] = s_hi; qa[g, 1, r] = s_lo; qa[g, 2, r] = s_hi; qa[g, 3, r] = s_lo
            qa[g, 4, r] = -st1; qa[g, 5, r] = -st2; qa[g, 6, r] = -st3
    c["c_qaug"] = bf(np.ascontiguousarray(qa.reshape(4, 7, 4, NT, 128).transpose(0, 1, 3, 2, 4)))
    return c


_CACHE = {}


def kernel(**inputs):
    if "nc" not in _CACHE:
        _CACHE["nc"] = build_program()[0]
    nc = _CACHE["nc"]
    consts = make_consts()
    x = np.asarray(inputs["x"], dtype=np.float32)
    B = x.shape[0]
    shared = {
        "norm_in": np.ascontiguousarray(np.asarray(inputs["norm_in"], np.float32)[0]),
        "w_in": np.ascontiguousarray(np.asarray(inputs["w_in"], np.float32)[0]),
        "w_out": np.ascontiguousarray(np.asarray(inputs["w_out"], np.float32)[0]),
        "final_norm": np.ascontiguousarray(np.asarray(inputs["final_norm"], np.float32)),
        "nsa_out_norm": np.ascontiguousarray(np.asarray(inputs["nsa_out_norm"], np.float32)[0]),
        "hgrn_out_norm": np.ascontiguousarray(np.asarray(inputs["hgrn_out_norm"], np.float32)[0]),
        "lower_bounds": np.ascontiguousarray(np.asarray(inputs["lower_bounds"], np.float32)),
    }
    for nm in ("cmp_pe_k", "cmp_pe_v", "cmp_w1_k", "cmp_w1_v", "cmp_w2_k", "cmp_w2_v"):
        shared[nm] = np.ascontiguousarray(np.asarray(inputs[nm], np.float32)[0])
    shared.update(consts)
    in_maps = []
    for b in range(B):
        m = dict(shared)
        m["x"] = np.ascontiguousarray(x[b])
        in_maps.append(m)
    res = run_bass_kernel_spmd(nc, in_maps, core_ids=list(range(B)))
    return np.stack([np.asarray(r["out"], dtype=np.float32) for r in res.results], axis=0)
```

```python
import numpy as np
import ml_dtypes
from contextlib import ExitStack
import concourse.bass as bass
import concourse.mybir as mybir
from concourse.bass_utils import run_bass_kernel_spmd

F32 = mybir.dt.float32
BF16 = mybir.dt.bfloat16
AF = mybir.ActivationFunctionType
ALU = mybir.AluOpType
AX = mybir.AxisListType

S = 2048
D = 2048
DIN = 7728
NT = 16
KC = 16
EPS = 1e-6
OFF_QA, OFF_KC, OFF_VC, OFF_KS, OFF_VS, OFF_KW, OFF_VW = 0, 1024, 1280, 1536, 1792, 2048, 2304
OFF_GATE, OFF_ZA, OFF_QH, OFF_FH, OFF_IH, OFF_ZH = 2560, 2608, 3632, 4656, 5680, 6704
NEG = -30000.0


class Op:
    __slots__ = ("eng", "fn", "dma", "deps", "signal", "count", "slot", "target", "prev_slot_target")

    def __init__(self, eng, fn, dma):
        self.eng, self.fn, self.dma = eng, fn, dma
        self.deps = set()
        self.signal = False
        self.count = None
        self.slot = None
        self.target = None
        self.prev_slot_target = 0


class Prog:
    def __init__(self, nc, n_dma_slots=8):
        self.nc = nc
        self.ops = []
        self.last_writer = {}
        self.readers = {}
        self.n_dma_slots = n_dma_slots
        self.last_on_eng = {}
        self.dma_since_barrier = []
        self.barrier_set = None
        self.barrier_id = 0
        self.barrier_seen = {}

    def add(self, eng, fn, reads=(), writes=(), dma=False):
        i = len(self.ops)
        op = Op(eng, fn, dma)
        deps = {}
        for k in reads:
            j = self.last_writer.get(k)
            if j is not None:
                deps[j] = True
        for k in writes:
            j = self.last_writer.get(k)
            if j is not None:
                deps[j] = True
            for r in self.readers.get(k, ()):
                deps.setdefault(r, False)
        if self.barrier_set is not None and self.barrier_seen.get(eng, -1) < self.barrier_id:
            for j in self.barrier_set:
                deps.setdefault(j, True)
            self.barrier_seen[eng] = self.barrier_id
        for j, hard in deps.items():
            oj = self.ops[j]
            if not oj.dma and oj.eng == eng:
                if eng == "pe":
                    continue
            op.deps.add(j)
            if not oj.dma:
                oj.signal = True
        for k in reads:
            self.readers.setdefault(k, []).append(i)
        for k in writes:
            self.last_writer[k] = i
            self.readers[k] = []
        self.ops.append(op)
        self.last_on_eng[eng] = i
        if dma:
            self.dma_since_barrier.append(i)
        return i

    def barrier(self):
        b = set(self.last_on_eng.values()) | set(self.dma_since_barrier)
        if self.barrier_set is not None and any(
                self.barrier_seen.get(e, -1) < self.barrier_id for e in ("pe", "act", "dve", "pool", "sp")):
            b |= self.barrier_set
        self.barrier_set = b
        self.barrier_id += 1
        self.dma_since_barrier = []

    def emit(self, stack):
        nc = self.nc
        engs = {"pe": nc.tensor, "act": nc.scalar, "dve": nc.vector, "pool": nc.gpsimd, "sp": nc.sync}
        esem = {e: stack.enter_context(nc.semaphore("sem_" + e)) for e in engs}
        dsem = {}
        for q in ("sp", "pool", "act"):
            dsem[q] = [stack.enter_context(nc.semaphore("dsem_%s_%d" % (q, s))) for s in range(self.n_dma_slots)]
        cnt = {e: 0 for e in engs}
        dcount = {q: 0 for q in dsem}
        slot_total = {q: [0] * self.n_dma_slots for q in dsem}
        for op in self.ops:
            if op.dma:
                q = op.eng
                s = dcount[q] % self.n_dma_slots
                dcount[q] += 1
                op.slot = s
                op.prev_slot_target = slot_total[q][s]
                slot_total[q][s] += 16
                op.target = slot_total[q][s]
            elif op.signal:
                cnt[op.eng] += 1
                op.count = cnt[op.eng]
        seen = {e: {} for e in engs}
        nwaits = 0
        for op in self.ops:
            e = engs[op.eng]
            waits = []
            for j in op.deps:
                oj = self.ops[j]
                if oj.dma:
                    waits.append((("d", oj.eng, oj.slot), dsem[oj.eng][oj.slot], oj.target))
                else:
                    waits.append((("e", oj.eng), esem[oj.eng], oj.count))
            if op.dma and op.prev_slot_target > 0:
                waits.append((("d", op.eng, op.slot), dsem[op.eng][op.slot], op.prev_slot_target))
            best = {}
            for key, sem, val in waits:
                if val > seen[op.eng].get(key, 0) and val > best.get(key, (None, 0))[1]:
                    best[key] = (sem, val)
            for key, (sem, val) in best.items():
                e.wait_ge(sem, val)
                seen[op.eng][key] = val
                nwaits += 1
            inst = op.fn()
            if op.dma:
                inst.then_inc(dsem[op.eng][op.slot], 16)
            elif op.signal:
                inst.then_inc(esem[op.eng], 1)
        sp = nc.sync
        for q in dsem:
            for s in range(self.n_dma_slots):
                if slot_total[q][s] > seen["sp"].get(("d", q, s), 0):
                    sp.wait_ge(dsem[q][s], slot_total[q][s])
        return len(self.ops), nwaits


def bf(a):
    return np.asarray(a, dtype=np.float32).astype(ml_dtypes.bfloat16)


def build_program(stages=("A", "B", "C", "D", "F"), debug=()):
    nc = bass.Bass("TRN2", target_bir_lowering=False)
    P = Prog(nc)
    dt_in = lambda name, shape, dt=F32: nc.dram_tensor(name, list(shape), dt, kind="ExternalInput").ap()
    x = dt_in("x", [S, D])
    norm_in = dt_in("norm_in", [D])
    w_in = dt_in("w_in", [D, DIN])
    w_out = dt_in("w_out", [D, D])
    final_norm = dt_in("final_norm", [D])
    nsa_gain = dt_in("nsa_out_norm", [1024])
    hgrn_gain = dt_in("hgrn_out_norm", [1024])
    ident_d = dt_in("c_ident", [128, 128], BF16)
    lower_bounds = dt_in("lower_bounds", [2, 1024])
    cmp_pe_k = dt_in("cmp_pe_k", [32, 64]); cmp_pe_v = dt_in("cmp_pe_v", [32, 64])
    cmp_w1_k = dt_in("cmp_w1_k", [2048, 256]); cmp_w1_v = dt_in("cmp_w1_v", [2048, 256])
    cmp_w2_k = dt_in("cmp_w2_k", [256, 64]); cmp_w2_v = dt_in("cmp_w2_v", [256, 64])
    c_E = dt_in("c_E", [128, S], BF16)
    c_diag = dt_in("c_diag", [128, 2, 4, 128], BF16)
    c_cmask = dt_in("c_cmask", [128, NT, 128], BF16)
    c_mmap = dt_in("c_mmap", [128, 32], BF16)
    c_sel = dt_in("c_sel", [128, NT, 2, 32])
    c_kaug = dt_in("c_kaug", [7, S], BF16)
    c_kcaug = dt_in("c_kcaug", [7, 127], BF16)
    c_qaug = dt_in("c_qaug", [4, 7, NT, 4, 128], BF16)
    c_f32 = dt_in("c_f32", [128, 4, 128])
    out = nc.dram_tensor("out", [S, D], F32, kind="ExternalOutput").ap()
    o_scr = nc.dram_tensor("o_scr", [S, D], F32,
                           kind="ExternalOutput" if any(n == "o_scr" for n, _ in debug) else "Internal").ap()
    dbg = {}
    for name, shape in debug:
        if name == "o_scr":
            continue
        dbg[name] = nc.dram_tensor("dbg_" + name, list(shape), F32, kind="ExternalOutput").ap()

    with ExitStack() as st:
        sb = lambda name, shape, dt: st.enter_context(nc.sbuf_tensor(name, list(shape), dt))
        ps = lambda name, shape, dt: st.enter_context(nc.psum_tensor(name, list(shape), dt))

        epsb = sb("epsb", [128, 1], F32)
        P.add("dve", lambda: nc.vector.memset(epsb[:], EPS), writes=["epsb"])

        def rsqrt(o, i, scale, rkey, wkey, rkeys=None):
            np_ = o.shape[0]
            P.add("act", lambda: nc.scalar.activation(out=o, in_=i, func=AF.Ln, bias=epsb[0:np_, 0:1], scale=scale),
                  reads=(rkeys or [rkey]) + ["epsb"], writes=[wkey])
            P.add("act", lambda: nc.scalar.activation(out=o, in_=o, func=AF.Exp, scale=-0.5),
                  reads=[wkey], writes=[wkey])

        hT = sb("hT", [128, KC, S], BF16)
        wz0 = sb("wz0", [128, KC, 512], BF16)
        if "D" in stages:
            for k in range(KC):
                P.add("pool", lambda k=k: nc.gpsimd.dma_start(
                    out=wz0[:, k, :], in_=w_in[k * 128:(k + 1) * 128, OFF_ZA:OFF_ZA + 512]),
                    writes=["wz0_%d" % k], dma=True)
        ident = sb("ident", [128, 128], BF16)
        gin = sb("gin", [128, KC], F32)
        P.add("sp", lambda: nc.sync.dma_start(out=ident[:], in_=ident_d), writes=["ident"], dma=True)
        P.add("sp", lambda: nc.sync.dma_start(out=gin[:], in_=norm_in.rearrange("(k p) -> p k", p=128),
                                              allow_slow_non_contiguous=True),
              writes=["gin"], dma=True)

        def phase_A():
            with ExitStack() as sa:
                sba = lambda name, shape, dt: sa.enter_context(nc.sbuf_tensor(name, list(shape), dt))
                xt = [sba("xt%d" % i, [128, D], F32) for i in range(2)]
                xn = [sba("xn%d" % i, [128, D], BF16) for i in range(2)]
                junk = sba("junkA", [128, D], BF16)
                ssq = [sba("ssq%d" % i, [128, 1], F32) for i in range(2)]
                rstd = [sba("rstd%d" % i, [128, 1], F32) for i in range(2)]
                pt = [sa.enter_context(nc.psum_tensor("ptA%d" % i, [128, 8, 128], BF16)) for i in range(2)]
                for i in range(NT):
                    b = i % 2
                    P.add("sp", lambda i=i, b=b: nc.sync.dma_start(out=xt[b][:], in_=x[i * 128:(i + 1) * 128, :]),
                          writes=["xt%d" % b], dma=True)
                    P.add("act", lambda b=b: nc.scalar.activation(out=junk[:], in_=xt[b][:], func=AF.Square,
                                                                  accum_out=ssq[b][:]),
                          reads=["xt%d" % b], writes=["junkA", "ssq%d" % b])
                    rsqrt(rstd[b][:], ssq[b][:], 1.0 / D, "ssq%d" % b, "rstd%d" % b)
                    P.add("dve", lambda b=b: nc.vector.tensor_scalar(out=xn[b][:], in0=xt[b][:], scalar1=rstd[b][:, 0:1],
                                                                     scalar2=None, op0=ALU.mult),
                          reads=["xt%d" % b, "rstd%d" % b], writes=["xn%d" % b])
                    for half in range(2):
                        for kk in range(8):
                            k = half * 8 + kk
                            P.add("pe", lambda b=b, k=k, kk=kk, half=half: nc.tensor.transpose(
                                out=pt[half][:, kk, :], in_=xn[b][:, k * 128:(k + 1) * 128], identity=ident[:]),
                                reads=["xn%d" % b, "ident"], writes=["ptA%d" % half])
                        P.add("dve", lambda i=i, half=half: nc.vector.tensor_tensor(
                            out=hT[:, half * 8:(half + 1) * 8, i * 128:(i + 1) * 128], in0=pt[half][:],
                            in1=gin[:, half * 8:(half + 1) * 8].unsqueeze(2).to_broadcast([128, 8, 128]), op=ALU.mult),
                            reads=["ptA%d" % half, "gin"], writes=["hT_%d" % i])
        if "A" in stages:
            phase_A()
            P.barrier()
            if "hT" in dbg:
                with nc.sbuf_tensor("dbg_hT_sb", [128, KC, S], F32) as dsb:
                    P.add("dve", lambda: nc.vector.tensor_copy(out=dsb[:], in_=hT[:]),
                          reads=["hT_%d" % i for i in range(NT)], writes=["dsb"])
                    P.add("sp", lambda: nc.sync.dma_start(out=dbg["hT"].rearrange("(k p) t -> p k t", p=128), in_=dsb[:]),
                          reads=["dsb"], dma=True)
                    P.barrier()
        hT_keys = ["hT_%d" % i for i in range(NT)]

        def phase_O1():
            with nc.sbuf_tensor("ones_dbg", [128, D], F32) as ones_dbg:
                P.add("dve", lambda: nc.vector.memset(ones_dbg[:], 1.0), writes=["ones_dbg"])
                for i in range(NT):
                    P.add("sp", lambda i=i: nc.sync.dma_start(out=o_scr[i * 128:(i + 1) * 128, :], in_=ones_dbg[:]),
                          reads=["ones_dbg"], writes=["o_scr"], dma=True)
                P.barrier()
        if "O1" in stages:
            phase_O1()


        def phase_B():
            with ExitStack() as sB:
                sbb = lambda name, shape, dt: sB.enter_context(nc.sbuf_tensor(name, list(shape), dt))
                kT_aug = sbb("kT_aug", [128, 2, 4, S], BF16)
                v_aug = sbb("v_aug", [128, NT, 2, 4, 65], BF16)
                gate_sb = sbb("gate_sb", [128, NT, 48], F32)
                kcT_aug = sbb("kcT_aug", [128, 4, 127], BF16)
                vc_aug = sbb("vc_aug", [128, 4, 65], BF16)
                P.add("dve", lambda: nc.vector.memset(kT_aug[64:128, :, :, :], 0.0), writes=["kTaug_init"])
                P.add("dve", lambda: nc.vector.memset(kcT_aug[64:128, :, :], 0.0), writes=["kcTaug_init"])
                for xx in range(2):
                    for g in range(4):
                        P.add("sp", lambda xx=xx, g=g: nc.sync.dma_start(out=kT_aug[64:71, xx, g, :], in_=c_kaug),
                              reads=["kTaug_init"], writes=["kTaug_rows"], dma=True)
                        if xx == 0:
                            P.add("sp", lambda g=g: nc.sync.dma_start(out=kT_aug[96:128, 0, g, :], in_=c_E[0:32, :]),
                                  reads=["kTaug_init"], writes=["kTaug_rows"], dma=True)
                for g in range(4):
                    P.add("sp", lambda g=g: nc.sync.dma_start(out=kcT_aug[64:71, g, :], in_=c_kcaug),
                          reads=["kcTaug_init"], writes=["kcTaug_rows"], dma=True)
                P.add("dve", lambda: nc.vector.memset(v_aug[:], 1.0), writes=["v_aug_init"])
                P.add("dve", lambda: nc.vector.memset(vc_aug[:], 1.0), writes=["vc_aug_init"])

                with ExitStack() as s0:
                    sb0 = lambda name, shape, dt: s0.enter_context(nc.sbuf_tensor(name, list(shape), dt))
                    ps0 = lambda name, shape, dt: s0.enter_context(nc.psum_tensor(name, list(shape), dt))
                    slabB = sb0("slabB", [128, KC, 2, 2, 256], BF16)
                    wg = sb0("wg", [128, KC, 48], BF16)
                    kstg = [sb0("kstg%d" % i, [128, 512], BF16) for i in range(2)]
                    pp = [ps0("ppB%d" % i, [128, 512], F32) for i in range(2)]
                    pg = ps0("pgB", [128, 512], F32)
                    npj = 0
                    for k in range(KC):
                        P.add("pool", lambda k=k: nc.gpsimd.dma_start(
                            out=slabB[:, k, :, :, :].rearrange("p a b c -> p (a b c)"),
                            in_=w_in[k * 128:(k + 1) * 128, OFF_KS:OFF_KS + 1024]),
                            writes=["slabB_%d" % k], dma=True)
                        P.add("pool", lambda k=k: nc.gpsimd.dma_start(
                            out=wg[:, k, :], in_=w_in[k * 128:(k + 1) * 128, OFF_GATE:OFF_GATE + 48]),
                            writes=["wg_%d" % k], dma=True)
                    for xx in range(2):
                        for gp in range(2):
                            for tb in range(4):
                                b = npj % 2
                                npj += 1
                                for k in range(KC):
                                    P.add("pe", lambda k=k, xx=xx, gp=gp, tb=tb, b=b: nc.tensor.matmul(
                                        pp[b][:], lhsT=slabB[:, k, xx, 0, gp * 128:(gp + 1) * 128],
                                        rhs=hT[:, k, tb * 512:(tb + 1) * 512], start=(k == 0), stop=(k == KC - 1)),
                                        reads=["slabB_%d" % k] + hT_keys[tb * 4:tb * 4 + 4], writes=["ppB%d" % b])
                                P.add("act", lambda xx=xx, gp=gp, tb=tb, b=b: nc.scalar.copy(
                                    out=kT_aug[0:64, xx, 2 * gp, tb * 512:(tb + 1) * 512], in_=pp[b][0:64, :]),
                                    reads=["ppB%d" % b], writes=["kT_%d_%d" % (xx, 2 * gp)])
                                P.add("act", lambda b=b: nc.scalar.copy(out=kstg[b][64:128, :], in_=pp[b][64:128, :]),
                                      reads=["ppB%d" % b], writes=["kstg%d" % b])
                                P.add("sp", lambda xx=xx, gp=gp, tb=tb, b=b: nc.sync.dma_start(
                                    out=kT_aug[0:64, xx, 2 * gp + 1, tb * 512:(tb + 1) * 512], in_=kstg[b][64:128, :]),
                                    reads=["kstg%d" % b], writes=["kT_%d_%d" % (xx, 2 * gp + 1)], dma=True)
                    for i in range(NT):
                        b = npj % 2
                        npj += 1
                        for k in range(KC):
                            P.add("pe", lambda k=k, i=i, b=b: nc.tensor.matmul(
                                pp[b][:], lhsT=hT[:, k, i * 128:(i + 1) * 128], rhs=slabB[:, k, :, 1, :],
                                start=(k == 0), stop=(k == KC - 1)),
                                reads=["slabB_%d" % k, "hT_%d" % i], writes=["ppB%d" % b])
                        for k in range(KC):
                            P.add("pe", lambda k=k, i=i: nc.tensor.matmul(
                                pg[:, 0:48], lhsT=hT[:, k, i * 128:(i + 1) * 128], rhs=wg[:, k, :],
                                start=(k == 0), stop=(k == KC - 1)),
                                reads=["wg_%d" % k, "hT_%d" % i], writes=["pgB"])
                        for xx in range(2):
                            P.add("act", lambda xx=xx, i=i, b=b: nc.scalar.copy(
                                out=v_aug[:, i, xx, :, 0:64],
                                in_=pp[b][:, xx * 256:(xx + 1) * 256].rearrange("p (g d) -> p g d", d=64)),
                                reads=["ppB%d" % b, "v_aug_init"], writes=["v_aug_%d" % i])
                        P.add("act", lambda i=i: nc.scalar.activation(out=gate_sb[:, i, :], in_=pg[:, 0:48], func=AF.Exp,
                                                                      scale=-1.0),
                              reads=["pgB"], writes=["gate_%d" % i])
                    gkeys = ["gate_%d" % i for i in range(NT)]
                    P.add("dve", lambda: nc.vector.tensor_scalar(out=gate_sb[:], in0=gate_sb[:], scalar1=1.0, scalar2=None,
                                                                 op0=ALU.add), reads=gkeys, writes=gkeys)
                    P.add("dve", lambda: nc.vector.reciprocal(out=gate_sb[:], in_=gate_sb[:]), reads=gkeys, writes=gkeys)

                P.barrier()
                with ExitStack() as s0:
                    sb0 = lambda name, shape, dt: s0.enter_context(nc.sbuf_tensor(name, list(shape), dt))
                    ps0 = lambda name, shape, dt: s0.enter_context(nc.psum_tensor(name, list(shape), dt))
                    slabA = sb0("slabA", [128, KC, 512], BF16)
                    cmpT = sb0("cmpT", [128, 4, S], BF16)
                    w1 = [sb0("w1_%d" % kv, [128, 32, 256], BF16) for kv in range(2)]
                    w2 = [sb0("w2_%d" % kv, [128, 2, 64], BF16) for kv in range(2)]
                    peT = [sb0("peT_%d" % kv, [64, 32], BF16) for kv in range(2)]
                    bias_sb = sb0("bias_sb", [128, 4], F32)
                    hact = [sb0("hact%d" % i, [128, 2, 127], BF16) for i in range(2)]
                    gx = [sb0("gx%d" % i, [128, 127], F32) for i in range(2)]
                    gu = [sb0("gu%d" % i, [128, 127], F32) for i in range(2)]
                    ppc = [ps0("ppc%d" % i, [128, 512], F32) for i in range(2)]
                    ph = [ps0("phB%d" % i, [128, 512], F32) for i in range(2)]
                    pb_ = ps0("pbB", [128, 512], F32)
                    pc = ps0("pcB", [128, 512], F32)
                    npj = 0
                    for k in range(KC):
                        P.add("pool", lambda k=k: nc.gpsimd.dma_start(
                            out=slabA[:, k, :], in_=w_in[k * 128:(k + 1) * 128, OFF_KC:OFF_KC + 512]),
                            writes=["slabA_%d" % k], dma=True)
                    w1d = [cmp_w1_k, cmp_w1_v]
                    w2d = [cmp_w2_k, cmp_w2_v]
                    ped = [cmp_pe_k, cmp_pe_v]
                    for kv in range(2):
                        for half in range(2):
                            for lq in range(4):
                                P.add("pool", lambda kv=kv, half=half, lq=lq: nc.gpsimd.dma_start(
                                    out=w1[kv][half * 64:(half + 1) * 64, lq * 8:(lq + 1) * 8, :],
                                    in_=w1d[kv].rearrange("(l d) h -> d l h", d=64)[:, lq * 8:(lq + 1) * 8, :]),
                                    writes=["w1_%d" % kv], dma=True)
                        P.add("pool", lambda kv=kv: nc.gpsimd.dma_start(
                            out=w2[kv][:], in_=w2d[kv].rearrange("(c p) d -> p c d", p=128)),
                            writes=["w2_%d" % kv], dma=True)
                        P.add("pool", lambda kv=kv: nc.gpsimd.dma_start(
                            out=peT[kv][:], in_=ped[kv].rearrange("l d -> d l"), allow_slow_non_contiguous=True),
                            writes=["peT_%d" % kv], dma=True)
                    for cc in range(4):
                        for tb in range(4):
                            b = npj % 2
                            npj += 1
                            for k in range(KC):
                                P.add("pe", lambda k=k, cc=cc, tb=tb, b=b: nc.tensor.matmul(
                                    ppc[b][:], lhsT=slabA[:, k, cc * 128:(cc + 1) * 128], rhs=hT[:, k, tb * 512:(tb + 1) * 512],
                                    start=(k == 0), stop=(k == KC - 1)),
                                    reads=["slabA_%d" % k] + hT_keys[tb * 4:tb * 4 + 4], writes=["ppc%d" % b])
                            P.add("act", lambda cc=cc, tb=tb, b=b: nc.scalar.copy(
                                out=cmpT[:, cc, tb * 512:(tb + 1) * 512], in_=ppc[b][:]),
                                reads=["ppc%d" % b], writes=["cmpT_%d" % cc])
                    for kv in range(2):
                        for hc in range(2):
                            col = kv * 2 + hc
                            for l in range(32):
                                P.add("pe", lambda kv=kv, hc=hc, l=l, col=col: nc.tensor.matmul(
                                    pb_[:, col:col + 1], lhsT=w1[kv][0:64, l, hc * 128:(hc + 1) * 128], rhs=peT[kv][:, l:l + 1],
                                    start=(l == 0), stop=(l == 31), skip_group_check=True),
                                    reads=["w1_%d" % kv, "peT_%d" % kv], writes=["pbB"])
                    P.add("dve", lambda: nc.vector.tensor_copy(out=bias_sb[:], in_=pb_[:, 0:4]), reads=["pbB"], writes=["bias_sb"])
                    nh = 0
                    for kv in range(2):
                        for g in range(4):
                            base = 64 * (g % 2)
                            cc = kv * 2 + g // 2
                            hb = nh % 2
                            nh += 1
                            for hc in range(2):
                                for l in range(32):
                                    P.add("pe", lambda kv=kv, hc=hc, l=l, base=base, cc=cc, hb=hb: nc.tensor.matmul(
                                        ph[hb][:, hc * 128:hc * 128 + 127],
                                        lhsT=w1[kv][base:base + 64, l, hc * 128:(hc + 1) * 128],
                                        rhs=cmpT[base:base + 64, cc, l:l + 2017:16],
                                        start=(l == 0 and hc == 0), stop=(l == 31), skip_group_check=True),
                                        reads=["w1_%d" % kv, "cmpT_%d" % cc], writes=["phB%d" % hb])
                            for hc in range(2):
                                gb = hc
                                col = kv * 2 + hc
                                P.add("dve", lambda hb=hb, hc=hc, gb=gb, col=col: nc.vector.tensor_scalar(
                                    out=gx[gb][:], in0=ph[hb][:, hc * 128:hc * 128 + 127], scalar1=bias_sb[:, col:col + 1],
                                    scalar2=None, op0=ALU.add),
                                    reads=["phB%d" % hb, "bias_sb"], writes=["gx%d" % gb])
                                P.add("dve", lambda gb=gb: nc.vector.tensor_tensor(out=gu[gb][:], in0=gx[gb][:], in1=gx[gb][:],
                                                                                   op=ALU.mult),
                                      reads=["gx%d" % gb], writes=["gu%d" % gb])
                                P.add("dve", lambda gb=gb: nc.vector.tensor_scalar(out=gu[gb][:], in0=gu[gb][:], scalar1=0.044715,
                                                                                   scalar2=1.0, op0=ALU.mult, op1=ALU.add),
                                      reads=["gu%d" % gb], writes=["gu%d" % gb])
                                P.add("dve", lambda gb=gb: nc.vector.tensor_tensor(out=gu[gb][:], in0=gu[gb][:], in1=gx[gb][:],
                                                                                   op=ALU.mult),
                                      reads=["gu%d" % gb, "gx%d" % gb], writes=["gu%d" % gb])
                                P.add("act", lambda gb=gb: nc.scalar.activation(out=gu[gb][:], in_=gu[gb][:], func=AF.Exp,
                                                                                scale=-1.5957691216057308),
                                      reads=["gu%d" % gb], writes=["gu%d" % gb])
                                P.add("dve", lambda gb=gb: nc.vector.tensor_scalar(out=gu[gb][:], in0=gu[gb][:], scalar1=1.0,
                                                                                   scalar2=None, op0=ALU.add),
                                      reads=["gu%d" % gb], writes=["gu%d" % gb])
                                P.add("dve", lambda gb=gb: nc.vector.reciprocal(out=gu[gb][:], in_=gu[gb][:]),
                                      reads=["gu%d" % gb], writes=["gu%d" % gb])
                                P.add("dve", lambda gb=gb, hb=hb, hc=hc: nc.vector.tensor_tensor(
                                    out=hact[hb][:, hc, :], in0=gx[gb][:], in1=gu[gb][:], op=ALU.mult),
                                    reads=["gx%d" % gb, "gu%d" % gb], writes=["hact%d_%d" % (hb, hc)])
                            hkeys = ["hact%d_%d" % (hb, hc) for hc in range(2)]
                            if kv == 0:
                                for hc in range(2):
                                    P.add("pe", lambda hb=hb, hc=hc: nc.tensor.matmul(
                                        pc[0:64, 0:127], lhsT=w2[0][:, hc, :], rhs=hact[hb][:, hc, :],
                                        start=(hc == 0), stop=(hc == 1)),
                                        reads=hkeys + ["w2_0"], writes=["pcB"])
                                P.add("act", lambda g=g: nc.scalar.copy(out=kcT_aug[0:64, g, :], in_=pc[0:64, 0:127]),
                                      reads=["pcB"], writes=["kcT_%d" % g])
                            else:
                                for hc in range(2):
                                    P.add("pe", lambda hb=hb, hc=hc: nc.tensor.matmul(
                                        pc[0:127, 0:64], lhsT=hact[hb][:, hc, :], rhs=w2[1][:, hc, :],
                                        start=(hc == 0), stop=(hc == 1)),
                                        reads=hkeys + ["w2_1"], writes=["pcB"])
                                P.add("act", lambda g=g: nc.scalar.copy(out=vc_aug[0:127, g, 0:64], in_=pc[0:127, 0:64]),
                                      reads=["pcB", "vc_aug_init"], writes=["vc_%d" % g])
                P.barrier()
                if "kc" in dbg:
                    with nc.sbuf_tensor("dbg_kc_sb", [128, 2, 4, 127], F32) as dkc:
                        P.add("dve", lambda: nc.vector.memset(dkc[:], 0.0), writes=["dkc"])
                        P.add("dve", lambda: nc.vector.tensor_copy(out=dkc[0:71, 0, :, :], in_=kcT_aug[0:71, :, :]), writes=["dkc"])
                        P.add("dve", lambda: nc.vector.tensor_copy(out=dkc[:, 1, :, 0:65], in_=vc_aug[:]), writes=["dkc"])
                        P.add("sp", lambda: nc.sync.dma_start(out=dbg["kc"], in_=dkc[:]), reads=["dkc"], dma=True)
                        P.barrier()

                if "noB2" in stages:
                    return
                with ExitStack() as s2:
                    sb2 = lambda name, shape, dt: s2.enter_context(nc.sbuf_tensor(name, list(shape), dt))
                    ps2 = lambda name, shape, dt: s2.enter_context(nc.psum_tensor(name, list(shape), dt))
                    wqs = sb2("wqs", [128, KC, 256], BF16)
                    cdiag = sb2("cdiag", [128, 2, 4, 128], BF16)
                    ccm = sb2("ccm", [128, NT, 128], BF16)
                    cmm = sb2("cmm", [128, 32], BF16)
                    csel = sb2("csel", [128, NT, 2, 32], F32)
                    ngb = sb2("ngb", [128, 1024], F32)
                    tiny = sb2("tiny", [128, 1], F32)
                    P.add("sp", lambda: nc.sync.dma_start(out=cdiag[:], in_=c_diag), writes=["cdiag"], dma=True)
                    P.add("sp", lambda: nc.sync.dma_start(out=ccm[:], in_=c_cmask), writes=["ccm"], dma=True)
                    P.add("sp", lambda: nc.sync.dma_start(out=cmm[:], in_=c_mmap), writes=["cmm"], dma=True)
                    P.add("sp", lambda: nc.sync.dma_start(out=csel[:], in_=c_sel), writes=["csel"], dma=True)
                    P.add("sp", lambda: nc.sync.dma_start(out=ngb[:], in_=nsa_gain.partition_broadcast(128)),
                          writes=["ngb"], dma=True)
                    P.add("dve", lambda: nc.vector.memset(tiny[:], 1e-30), writes=["tiny"])
                    qTa = [sb2("qT_aug%d" % i, [128, NT, 4, 128], BF16) for i in range(2)]
                    PT = [sb2("PT%d" % i, [128, 512], BF16) for i in range(3)]
                    negselT = [sb2("negsel%d" % i, [128, 128], F32) for i in range(2)]
                    coef = sb2("coef", [128, 3, 4], F32)
                    coefC = sb2("coefC", [128, 4], F32)
                    pslc = sb2("pslc", [128, 32], F32)
                    score = sb2("score", [128, 32], F32)
                    top8 = sb2("top8", [128, 8], F32)
                    oacc = [sb2("oacc%d" % i, [128, 4, 64], F32) for i in range(3)]
                    rinvE = [sb2("rinvE%d" % i, [128, 3, 4], F32) for i in range(3)]
                    otmp = sb2("otmp", [128, 4, 64], F32)
                    ssq = [sb2("ssqB%d" % i, [128, 4], F32) for i in range(3)]
                    rsd = [sb2("rsdB%d" % i, [128, 4], F32) for i in range(3)]
                    ostage = [sb2("ostB%d" % i, [128, 256], F32) for i in range(3)]
                    idf = sb2("idfB", [128, 128], F32)
                    qstg = [sb2("qstg%d" % i, [128, 512], BF16) for i in range(2)]
                    pS = [ps2("pS%d" % i, [128, 512], F32) for i in range(3)]
                    pOc = ps2("pOc0", [128, 512], F32)
                    pOs2 = [ps2("pOs%d" % i, [128, 512], F32) for i in range(2)]
                    pOw2 = [ps2("pOw%d" % i, [128, 512], F32) for i in range(2)]
                    P.add("sp", lambda: nc.sync.dma_start(out=idf[:], in_=c_f32[:, 2, :]), writes=["idfB"], dma=True)
                    for i_ in range(2):
                        P.add("dve", lambda i_=i_: nc.vector.memset(negselT[i_][:], 0.0), writes=["negsel%d" % i_])
                    for qb in range(2):
                        P.add("dve", lambda qb=qb: nc.vector.memset(qTa[qb][64:128, :, :, :], 0.0), writes=["qaug_init%d" % qb])
                    O3 = lambda t_: t_[:, 0:260].rearrange("p (r e) -> p r e", e=65)
                    kOc = "pOc0"

                    def load_wq(g):
                        for k in range(KC):
                            P.add("pool", lambda k=k: nc.gpsimd.dma_start(
                                out=wqs[:, k, :], in_=w_in[k * 128:(k + 1) * 128, OFF_QA + g * 256:OFF_QA + (g + 1) * 256]),
                                writes=["wqs_%d" % k], dma=True)

                    def load_qaug(g):
                        qb = g % 2
                        P.add("sp", lambda: nc.sync.dma_start(out=qTa[qb][64:71, :, :, :], in_=c_qaug[g]),
                              reads=["qaug_init%d" % qb], writes=["qaug_rows%d" % qb], dma=True)

                    stream = []
                    for r in range(2):
                        for tb in range(4):
                            stream.append(("qp", 0, r, tb))
                    for g in range(4):
                        gj = []
                        gj.append(("cmp", g, 0, 0))
                        for c in range(NT):
                            gj.append(("selpe", g, c, 0))
                            if c + 1 < NT:
                                gj.append(("cmp", g, c + 1, 0))
                            gj += [("win", g, c, m) for m in range(max(0, c - 4), c + 1)]
                            gj += [("slc", g, c, m) for m in range(c + 1)]
                        if g + 1 < 4:
                            qps = [("qp", g + 1, r, tb) for r in range(2) for tb in range(4)]
                            out_ = []
                            for ji, jb in enumerate(gj):
                                out_.append(jb)
                                if ji >= 40 and (ji - 40) % 20 == 0 and qps:
                                    out_.append(qps.pop(0))
                            out_ += qps
                            gj = out_
                        stream += gj
                    n_st = len(stream)

                    def qkeys(g, c):
                        qb = g % 2
                        return ["qT%d_%d" % (qb, c // 4), "qaug_rows%d" % qb, "nsT%d_%d" % (qb, c)]

                    def emit_S(idx):
                        job = stream[idx]
                        sb_ = idx % 3
                        kind = job[0]
                        if kind == "selpe":
                            return
                        if kind == "qp":
                            _, g, r, tb = job
                            for k in range(KC):
                                P.add("pe", lambda k=k: nc.tensor.matmul(
                                    pS[sb_][:], lhsT=wqs[:, k, r * 128:(r + 1) * 128], rhs=hT[:, k, tb * 512:(tb + 1) * 512],
                                    start=(k == 0), stop=(k == KC - 1)),
                                    reads=["wqs_%d" % k] + hT_keys[tb * 4:tb * 4 + 4], writes=["pS%d" % sb_])
                            return
                        _, g, c, m = job
                        qT_aug = qTa[g % 2]
                        ms = slice(m * 128, (m + 1) * 128)
                        if kind == "cmp":
                            P.add("pe", lambda: nc.tensor.matmul(
                                pS[sb_][0:127, :], lhsT=kcT_aug[:, g, :], rhs=qT_aug[:, c, :, :], start=True, stop=False,
                                skip_group_check=True),
                                reads=["kcT_%d" % g, "kcTaug_rows"] + qkeys(g, c), writes=["pS%d" % sb_])
                            for r in range(4):
                                P.add("pe", lambda r=r: nc.tensor.matmul(
                                    pS[sb_][0:127, r * 128:(r + 1) * 128], lhsT=ident[0:127, 0:127], rhs=ccm[0:127, c, :],
                                    start=False, stop=(r == 3), skip_group_check=True),
                                    reads=["ident", "ccm"], writes=["pS%d" % sb_])
                        else:
                            xx = 0 if kind == "slc" else 1
                            extra = []
                            if m == c:
                                extra.append(0)
                            if kind == "win" and m == c - 4:
                                extra.append(1)
                            P.add("pe", lambda: nc.tensor.matmul(
                                pS[sb_][:], lhsT=kT_aug[:, xx, g, ms], rhs=qT_aug[:, c, :, :], start=True,
                                stop=(len(extra) == 0), skip_group_check=True),
                                reads=["kT_%d_%d" % (xx, g), "kTaug_rows"] + qkeys(g, c), writes=["pS%d" % sb_])
                            for ei, di in enumerate(extra):
                                last = ei == len(extra) - 1
                                P.add("pe", lambda last=last, di=di: nc.tensor.matmul(
                                    pS[sb_][:], lhsT=ident[:], rhs=cdiag[:, di, :, :], start=False, stop=last,
                                    skip_group_check=True),
                                    reads=["ident", "cdiag"], writes=["pS%d" % sb_])

                    def emit_act(idx):
                        job = stream[idx]
                        sb_ = idx % 3
                        if job[0] == "selpe":
                            return
                        if job[0] == "qp":
                            _, g, r, tb = job
                            qT_aug = qTa[g % 2]
                            sg = idx % 2
                            P.add("act", lambda: nc.scalar.activation(
                                out=qT_aug[0:64, tb * 4:(tb + 1) * 4, 2 * r, :],
                                in_=pS[sb_][0:64, :].rearrange("p (c t) -> p c t", t=128), func=AF.Copy, scale=0.125),
                                reads=["pS%d" % sb_], writes=["qT%d_%d" % (g % 2, tb)])
                            P.add("act", lambda: nc.scalar.activation(
                                out=qstg[sg][64:128, :], in_=pS[sb_][64:128, :], func=AF.Copy, scale=0.125),
                                reads=["pS%d" % sb_], writes=["qstg%d" % sg])
                            P.add("sp", lambda: nc.sync.dma_start(
                                out=qT_aug[0:64, tb * 4:(tb + 1) * 4, 2 * r + 1, :],
                                in_=qstg[sg][64:128, :].rearrange("p (c t) -> p c t", t=128)),
                                reads=["qstg%d" % sg], writes=["qT%d_%d" % (g % 2, tb)], dma=True)
                            return
                        np_ = 127 if job[0] == "cmp" else 128
                        P.add("act", lambda: nc.scalar.activation(out=PT[sb_][0:np_, :], in_=pS[sb_][0:np_, :], func=AF.Exp),
                              reads=["pS%d" % sb_], writes=["PT%d" % sb_])

                    def emit_PV(idx):
                        job = stream[idx]
                        pb_i = idx % 3
                        kind = job[0]
                        if kind in ("qp", "selpe"):
                            return
                        _, g, c, m = job
                        if kind == "cmp":
                            for r in range(4):
                                P.add("pe", lambda r=r: nc.tensor.matmul(
                                    O3(pOc)[:, r, :], lhsT=PT[pb_i][0:127, r * 128:(r + 1) * 128], rhs=vc_aug[0:127, g, :],
                                    start=(r == 0), stop=True, skip_group_check=True),
                                    reads=["PT%d" % pb_i, "vc_%d" % g, "vc_aug_init"], writes=[kOc])
                            for r in range(4):
                                P.add("pe", lambda r=r: nc.tensor.matmul(
                                    pOc[:, 260 + r * 32:260 + (r + 1) * 32], lhsT=PT[pb_i][0:127, r * 128:(r + 1) * 128],
                                    rhs=cmm[0:127, :], start=False, stop=True, skip_group_check=True),
                                    reads=["PT%d" % pb_i, "cmm"], writes=[kOc])
                        else:
                            xx = 0 if kind == "slc" else 1
                            pO = pOs2[c % 2] if kind == "slc" else pOw2[c % 2]
                            key = ("pOs%d" if kind == "slc" else "pOw%d") % (c % 2)
                            m0 = 0 if kind == "slc" else max(0, c - 4)
                            for r in range(4):
                                P.add("pe", lambda r=r: nc.tensor.matmul(
                                    O3(pO)[:, r, :], lhsT=PT[pb_i][:, r * 128:(r + 1) * 128], rhs=v_aug[:, m, xx, g, :],
                                    start=(m == m0 and r == 0), stop=(m == c), skip_group_check=True),
                                    reads=["PT%d" % pb_i, "v_aug_%d" % m], writes=[key])

                    def emit_select(g, c):
                        ob = (g * NT + c) % 3
                        rk = "rinvE%d_0" % ob
                        P.add("dve", lambda: nc.vector.tensor_scalar(
                            out=rinvE[ob][:, 0, :], in0=O3(pOc)[:, :, 64], scalar1=tiny[:, 0:1], scalar2=None, op0=ALU.add),
                            reads=[kOc, "tiny"], writes=[rk])
                        P.add("dve", lambda: nc.vector.reciprocal(out=rinvE[ob][:, 0, :], in_=rinvE[ob][:, 0, :]),
                              reads=[rk], writes=[rk])
                        for r in range(4):
                            if r == 0:
                                P.add("dve", lambda: nc.vector.tensor_scalar(
                                    out=pslc[:], in0=pOc[:, 260:292], scalar1=rinvE[ob][:, 0, 0:1], scalar2=None, op0=ALU.mult),
                                    reads=[kOc, rk], writes=["pslc"])
                            else:
                                P.add("dve", lambda r=r: nc.vector.scalar_tensor_tensor(
                                    out=pslc[:], in0=pOc[:, 260 + r * 32:292 + r * 32], scalar=rinvE[ob][:, 0, r:r + 1],
                                    in1=pslc[:], op0=ALU.mult, op1=ALU.add),
                                    reads=[kOc, rk, "pslc"], writes=["pslc"])
                        P.add("dve", lambda: nc.vector.tensor_tensor(out=score[:], in0=pslc[:], in1=csel[:, c, 0, :], op=ALU.mult),
                              reads=["pslc", "csel"], writes=["score"])
                        P.add("dve", lambda: nc.vector.tensor_tensor(out=score[:], in0=score[:], in1=csel[:, c, 1, :], op=ALU.add),
                              reads=["score", "csel"], writes=["score"])
                        P.add("dve", lambda: nc.vector.max(out=top8[:], in_=score[:]), reads=["score"], writes=["top8"])
                        P.add("dve", lambda: nc.vector.tensor_scalar(
                            out=negselT[c % 2][:, 96:128], in0=score[:], scalar1=top8[:, 7:8], scalar2=-1.0,
                            op0=ALU.is_ge, op1=ALU.add),
                            reads=["score", "top8"], writes=["negsel%d" % (c % 2)])
                        gsl0 = gate_sb[:, c, g * 12:(g + 1) * 12].rearrange("p (r x) -> p x r", x=3)[:, 0, :]
                        P.add("dve", lambda: nc.vector.tensor_tensor(out=coefC[:], in0=rinvE[ob][:, 0, :], in1=gsl0, op=ALU.mult),
                              reads=[rk, "gate_%d" % c], writes=["coefC"])
                        P.add("dve", lambda: nc.vector.tensor_tensor(
                            out=oacc[ob][:], in0=O3(pOc)[:, :, 0:64], in1=coefC[:].unsqueeze(2).to_broadcast([128, 4, 64]),
                            op=ALU.mult),
                            reads=[kOc, "coefC"], writes=["oacc%d" % ob])

                    def emit_selpe(g, c):
                        qT_aug = qTa[g % 2]
                        pOs = pOs2[c % 2]
                        kOs = "pOs%d" % (c % 2)
                        P.add("pe", lambda: nc.tensor.transpose(out=pOs[:, 260:388], in_=negselT[c % 2][:], identity=idf[:]),
                              reads=["negsel%d" % (c % 2), "idfB"], writes=[kOs])
                        P.add("dve", lambda: nc.vector.tensor_copy(
                            out=qT_aug[96:128, c, :, :], in_=pOs[96:128, 260:388].unsqueeze(1).to_broadcast([32, 4, 128])),
                            reads=[kOs, "qaug_init%d" % (g % 2)], writes=["nsT%d_%d" % (g % 2, c)])

                    def epi_part1(g, c):
                        ob = (g * NT + c) % 3
                        pOs, pOw = pOs2[c % 2], pOw2[c % 2]
                        kOs, kOw = "pOs%d" % (c % 2), "pOw%d" % (c % 2)
                        for bi, pO, key in ((1, pOs, kOs), (2, pOw, kOw)):
                            P.add("dve", lambda bi=bi, pO=pO: nc.vector.reciprocal(out=rinvE[ob][:, bi, :], in_=O3(pO)[:, :, 64]),
                                  reads=[key], writes=["rinvE%d_%d" % (ob, bi)])
                        gsl = gate_sb[:, c, g * 12:(g + 1) * 12].rearrange("p (r x) -> p x r", x=3)
                        P.add("dve", lambda: nc.vector.tensor_tensor(out=coef[:], in0=rinvE[ob][:], in1=gsl, op=ALU.mult),
                              reads=["rinvE%d_%d" % (ob, bi) for bi in range(3)] + ["gate_%d" % c], writes=["coef"])
                        for bi, pO, key in ((1, pOs, kOs), (2, pOw, kOw)):
                            P.add("dve", lambda bi=bi, pO=pO: nc.vector.tensor_tensor(
                                out=otmp[:], in0=O3(pO)[:, :, 0:64],
                                in1=coef[:, bi, :].unsqueeze(2).to_broadcast([128, 4, 64]), op=ALU.mult),
                                reads=[key, "coef"], writes=["otmp"])
                            P.add("dve", lambda: nc.vector.tensor_tensor(out=oacc[ob][:], in0=oacc[ob][:], in1=otmp[:], op=ALU.add),
                                  reads=["oacc%d" % ob, "otmp"], writes=["oacc%d" % ob])
                        P.add("dve", lambda: nc.vector.tensor_tensor(out=otmp[:], in0=oacc[ob][:], in1=oacc[ob][:], op=ALU.mult),
                              reads=["oacc%d" % ob], writes=["otmp"])
                        P.add("dve", lambda: nc.vector.tensor_reduce(out=ssq[ob][:], in_=otmp[:], axis=AX.X, op=ALU.add),
                              reads=["otmp"], writes=["ssqB%d" % ob])

                    def epi_tail(g, c):
                        ob = (g * NT + c) % 3
                        rsqrt(rsd[ob][:], ssq[ob][:], 1.0 / 64, "ssqB%d" % ob, "rsdB%d" % ob)
                        P.add("dve", lambda: nc.vector.tensor_tensor(
                            out=oacc[ob][:], in0=oacc[ob][:], in1=rsd[ob][:].unsqueeze(2).to_broadcast([128, 4, 64]), op=ALU.mult),
                            reads=["oacc%d" % ob, "rsdB%d" % ob], writes=["oacc%d" % ob])
                        P.add("dve", lambda: nc.vector.tensor_tensor(
                            out=ostage[ob][:], in0=oacc[ob][:].rearrange("p r d -> p (r d)"), in1=ngb[:, g * 256:(g + 1) * 256],
                            op=ALU.mult),
                            reads=["oacc%d" % ob, "ngb"], writes=["ostB%d" % ob])
                        P.add("sp", lambda: nc.sync.dma_start(
                            out=o_scr[c * 128:(c + 1) * 128, g * 256:(g + 1) * 256], in_=ostage[ob][:]),
                            reads=["ostB%d" % ob], writes=["o_scr"], dma=True)

                    load_wq(0)
                    load_qaug(0)
                    sel_done = set()
                    deferred = []
                    pend_p1, pend_tail = [], []
                    s_emitted = set()

                    def try_S(idx):
                        if idx >= n_st or idx in s_emitted:
                            return
                        job = stream[idx]
                        if job[0] == "slc" and (job[1], job[2]) not in sel_done:
                            deferred.append(idx)
                            return
                        s_emitted.add(idx)
                        emit_S(idx)

                    try_S(0)
                    try_S(1)
                    for idx, job in enumerate(stream):
                        try_S(idx + 2)
                        if idx not in s_emitted:
                            s_emitted.add(idx)
                            if idx in deferred:
                                deferred.remove(idx)
                            emit_S(idx)
                        emit_act(idx)
                        emit_PV(idx)
                        kind = job[0]
                        if kind == "qp":
                            _, g_, r_, tb_ = job
                            if r_ == 1 and tb_ == 3 and g_ + 1 < 4:
                                load_wq(g_ + 1)
                            continue
                        _, g, c, m = job
                        if kind == "selpe":
                            emit_selpe(g, c)
                            sel_done.add((g, c))
                            for d_ in list(deferred):
                                deferred.remove(d_)
                                try_S(d_)
                            continue
                        if kind == "cmp":
                            if c == 2 and g + 1 < 4:
                                load_qaug(g + 1)
                            seq = g * NT + c
                            while pend_tail and pend_tail[0][3] <= seq - 3:
                                gt, ct, _, _ = pend_tail.pop(0)
                                epi_tail(gt, ct)
                            emit_select(g, c)
                            while pend_p1:
                                epi_part1(*pend_p1.pop(0))
                        for pt_ in pend_tail:
                            pt_[2] -= 1
                        while pend_tail and pend_tail[0][2] <= 0 and (pend_tail[0][0], pend_tail[0][1]) not in pend_p1:
                            gt, ct, _, _ = pend_tail.pop(0)
                            epi_tail(gt, ct)
                        if kind == "slc" and m == c:
                            pend_p1.append((g, c))
                            pend_tail.append([g, c, 24, g * NT + c])
                    while pend_p1:
                        epi_part1(*pend_p1.pop(0))
                    while pend_tail:
                        gt, ct, _, _ = pend_tail.pop(0)
                        epi_tail(gt, ct)
        if "B" in stages:
            phase_B()
            P.barrier()

        HB = 2

        def phase_C():
            with ExitStack() as sc:
                sbc = lambda name, shape, dt: sc.enter_context(nc.sbuf_tensor(name, list(shape), dt))
                psc = lambda name, shape, dt: sc.enter_context(nc.psum_tensor(name, list(shape), dt))
                cf = sbc("cf", [128, 4, 128], F32)
                P.add("sp", lambda: nc.sync.dma_start(out=cf[:], in_=c_f32), writes=["cf"], dma=True)
                U2, L2, IDF = cf[:, 0, :], cf[:, 1, :], cf[:, 2, :]
                lbb = sbc("lbb", [128, 1024], F32)
                oml = sbc("oml", [128, 1024], F32)
                hgb = sbc("hgb", [128, 1024], F32)
                P.add("sp", lambda: nc.sync.dma_start(out=lbb[:], in_=lower_bounds[0].partition_broadcast(128)),
                      writes=["lbb"], dma=True)
                P.add("sp", lambda: nc.sync.dma_start(out=oml[:], in_=lower_bounds[1].partition_broadcast(128)),
                      writes=["oml"], dma=True)
                P.add("sp", lambda: nc.sync.dma_start(out=hgb[:], in_=hgrn_gain.partition_broadcast(128)),
                      writes=["hgb"], dma=True)
                P.add("dve", lambda: nc.vector.tensor_tensor(out=oml[:], in0=oml[:], in1=lbb[:], op=ALU.subtract),
                      reads=["lbb", "oml"], writes=["oml"])
                P.add("act", lambda: nc.scalar.activation(out=oml[:], in_=oml[:], func=AF.Exp), reads=["oml"], writes=["oml"])
                P.add("dve", lambda: nc.vector.tensor_scalar(out=oml[:], in0=oml[:], scalar1=1.0, scalar2=None, op0=ALU.add),
                      reads=["oml"], writes=["oml"])
                P.add("dve", lambda: nc.vector.reciprocal(out=lbb[:], in_=oml[:]), reads=["oml"], writes=["lbb"])
                P.add("dve", lambda: nc.vector.tensor_scalar(out=oml[:], in0=lbb[:], scalar1=-1.0, scalar2=1.0,
                                                             op0=ALU.mult, op1=ALU.add), reads=["lbb"], writes=["oml"])

                W = HB * 128
                wq = sbc("wq", [128, KC, W], BF16)
                wfi = sbc("wfi", [128, KC, 2, W], BF16)
                qT = sbc("qTh", [128, HB, S], BF16)
                logf = sbc("logf", [128, NT, W], F32)
                kk = sbc("kk", [128, NT, W], F32)
                vv = sbc("vv", [128, NT, W], BF16)
                S32 = sbc("S32", [128, HB, 128], F32)
                Sbf = sbc("Sbf", [128, HB, 128], BF16)
                tmpe = [sbc("tmpe%d" % i, [128, W], F32) for i in range(2)]
                tmpf = [sbc("tmpf%d" % i, [128, W], F32) for i in range(2)]
                ebT = [sbc("ebT%d" % i, [128, HB, 128], F32) for i in range(2)]
                enbT = [sbc("enbT%d" % i, [128, HB, 128], F32) for i in range(2)]
                erev = [sbc("erev%d" % i, [128, HB, 128], F32) for i in range(2)]
                qbz = [sbc("qbz%d" % i, [128, HB, 2, 128], BF16) for i in range(2)]
                for i_ in range(2):
                    P.add("dve", lambda i_=i_: nc.vector.memset(qbz[i_][:], 0.0), writes=["qbz_init"])
                qv = lambda par, hd: qbz[par][:, hd, :, :].rearrange("p a (b t) -> p (a b) t", t=64)[:, 0:4:3, :]
                kbT = [sbc("kbT%d" % i, [128, HB, 128], BF16) for i in range(2)]
                kd = [sbc("kd%d" % i, [128, HB, 2, 128], BF16) for i in range(2)]
                ATm = [sbc("ATm%d" % i, [128, HB, 128], BF16) for i in range(2)]
                ost = [sbc("ost%d" % i, [128, W], F32) for i in range(2)]
                junk = sbc("junkC", [128, 128], BF16)
                ssq = [sbc("ssqC%d" % i, [128, HB], F32) for i in range(2)]
                rsd = [sbc("rsdC%d" % i, [128, HB], F32) for i in range(2)]
                pA = [[psc("pA%d_%d" % (par, hd), [128, 4, 128], F32) for hd in range(HB)] for par in range(2)]
                pOb = [psc("pO_%d" % par, [128, 4, 128], F32) for par in range(2)]
                pproj = [pOb[i_][:, :, :].rearrange("p a b -> p (a b)") for i_ in range(2)]
                pSf = [psc("pS_%d" % hd, [128, 4, 128], F32) for hd in range(HB)]
                pSb = [t[:, 0, :] for t in pSf]

                npj = 0
                for h0 in range(0, 8, HB):
                    for k in range(KC):
                        P.add("pool", lambda k=k, h0=h0: nc.gpsimd.dma_start(
                            out=wq[:, k, :], in_=w_in[k * 128:(k + 1) * 128, OFF_QH + h0 * 128:OFF_QH + h0 * 128 + W]),
                            writes=["wq_%d" % k], dma=True)
                        P.add("pool", lambda k=k, h0=h0: nc.gpsimd.dma_start(
                            out=wfi[:, k, 0, :], in_=w_in[k * 128:(k + 1) * 128, OFF_FH + h0 * 128:OFF_FH + h0 * 128 + W]),
                            writes=["wf_%d" % k], dma=True)
                        P.add("pool", lambda k=k, h0=h0: nc.gpsimd.dma_start(
                            out=wfi[:, k, 1, :], in_=w_in[k * 128:(k + 1) * 128, OFF_IH + h0 * 128:OFF_IH + h0 * 128 + W]),
                            writes=["wi_%d" % k], dma=True)
                    for hd in range(HB):
                        for tb in range(4):
                            pp = npj % 2
                            npj += 1
                            for k in range(KC):
                                P.add("pe", lambda k=k, hd=hd, tb=tb, pp=pp: nc.tensor.matmul(
                                    pproj[pp], lhsT=wq[:, k, hd * 128:(hd + 1) * 128], rhs=hT[:, k, tb * 512:(tb + 1) * 512],
                                    start=(k == 0), stop=(k == KC - 1)),
                                    reads=["wq_%d" % k] + hT_keys[tb * 4:tb * 4 + 4], writes=["pO_%d" % pp])
                            P.add("act", lambda hd=hd, tb=tb, pp=pp: nc.scalar.copy(
                                out=qT[:, hd, tb * 512:(tb + 1) * 512], in_=pproj[pp]),
                                reads=["pO_%d" % pp], writes=["qTh_%d_%d" % (hd, tb)])
                    for i in range(NT):
                        pp = npj % 2
                        npj += 1
                        b = i % 2
                        for k in range(KC):
                            P.add("pe", lambda k=k, i=i, pp=pp: nc.tensor.matmul(
                                pproj[pp], lhsT=hT[:, k, i * 128:(i + 1) * 128], rhs=wfi[:, k, :, :],
                                start=(k == 0), stop=(k == KC - 1)),
                                reads=["wf_%d" % k, "wi_%d" % k, "hT_%d" % i], writes=["pO_%d" % pp])
                        P.add("act", lambda pp=pp, b=b: nc.scalar.activation(out=tmpe[b][:], in_=pproj[pp][:, 0:W],
                                                                             func=AF.Exp, scale=-1.0),
                              reads=["pO_%d" % pp], writes=["tmpe%d" % b])
                        P.add("act", lambda pp=pp, i=i: nc.scalar.copy(out=vv[:, i, :], in_=pproj[pp][:, W:2 * W]),
                              reads=["pO_%d" % pp], writes=["vv_%d" % i])
                        P.add("dve", lambda b=b: nc.vector.tensor_scalar(out=tmpe[b][:], in0=tmpe[b][:], scalar1=1.0,
                                                                         scalar2=None, op0=ALU.add),
                              reads=["tmpe%d" % b], writes=["tmpe%d" % b])
                        P.add("dve", lambda b=b: nc.vector.reciprocal(out=tmpe[b][:], in_=tmpe[b][:]),
                              reads=["tmpe%d" % b], writes=["tmpe%d" % b])
                        P.add("dve", lambda b=b, h0=h0: nc.vector.tensor_tensor(
                            out=tmpf[b][:], in0=tmpe[b][:], in1=oml[:, h0 * 128:h0 * 128 + W], op=ALU.mult),
                            reads=["tmpe%d" % b, "oml"], writes=["tmpf%d" % b])
                        P.add("dve", lambda b=b, h0=h0: nc.vector.tensor_tensor(
                            out=tmpf[b][:], in0=tmpf[b][:], in1=lbb[:, h0 * 128:h0 * 128 + W], op=ALU.add),
                            reads=["tmpf%d" % b, "lbb"], writes=["tmpf%d" % b])
                        P.add("act", lambda b=b, i=i: nc.scalar.activation(out=logf[:, i, :], in_=tmpf[b][:], func=AF.Ln),
                              reads=["tmpf%d" % b], writes=["logf_%d" % i])
                        P.add("dve", lambda b=b, i=i: nc.vector.tensor_scalar(
                            out=kk[:, i, :], in0=tmpf[b][:], scalar1=-1.0, scalar2=1.0, op0=ALU.mult, op1=ALU.add),
                            reads=["tmpf%d" % b], writes=["kk_%d" % i])
                    P.add("dve", lambda: nc.vector.memset(S32[:], 0.0), writes=["S32_%d" % hd for hd in range(HB)])
                    P.add("dve", lambda: nc.vector.memset(Sbf[:], 0.0), writes=["Sbf_%d" % hd for hd in range(HB)])
                    pend_epi = []
                    def front(i):
                            par = i % 2
                            tb = i // 4
                            hs = lambda hd: slice(hd * 128, (hd + 1) * 128)
                            for hd in range(HB):
                                P.add("pe", lambda i=i, hd=hd, par=par: nc.tensor.matmul(
                                    pA[par][hd][:, 0, :], lhsT=logf[:, i, hs(hd)], rhs=U2, start=True, stop=True),
                                    reads=["logf_%d" % i, "cf"], writes=["pA%d_%d" % (par, hd)])
                                P.add("pe", lambda i=i, hd=hd, par=par: nc.tensor.matmul(
                                    pA[par][hd][:, 1, :], lhsT=L2, rhs=logf[:, i, hs(hd)], start=True, stop=True),
                                    reads=["logf_%d" % i, "cf"], writes=["pA%d_%d" % (par, hd)])
                                P.add("pe", lambda i=i, hd=hd, par=par: nc.tensor.transpose(
                                    out=pA[par][hd][:, 2, :], in_=kk[:, i, hs(hd)], identity=IDF),
                                    reads=["kk_%d" % i, "cf"], writes=["pA%d_%d" % (par, hd)])
                            for hd in range(HB):
                                P.add("act", lambda hd=hd, par=par: nc.scalar.activation(
                                    out=ebT[par][:, hd, :], in_=pA[par][hd][:, 0, :], func=AF.Exp),
                                    reads=["pA%d_%d" % (par, hd)], writes=["ebT%d_%d" % (par, hd)])
                                P.add("act", lambda hd=hd, par=par: nc.scalar.activation(
                                    out=enbT[par][:, hd, :], in_=pA[par][hd][:, 0, :], func=AF.Exp, scale=-1.0),
                                    reads=["pA%d_%d" % (par, hd)], writes=["enbT%d_%d" % (par, hd)])
                                P.add("act", lambda hd=hd, par=par: nc.scalar.activation(
                                    out=erev[par][:, hd, :], in_=pA[par][hd][:, 1, :], func=AF.Exp),
                                    reads=["pA%d_%d" % (par, hd)], writes=["erev%d_%d" % (par, hd)])
                            for hd in range(HB):
                                P.add("dve", lambda i=i, hd=hd, par=par: nc.vector.tensor_tensor(
                                    out=qv(par, hd), in0=qT[:, hd, i * 128:(i + 1) * 128].rearrange("p (a t) -> p a t", t=64),
                                    in1=ebT[par][:, hd, :].rearrange("p (a t) -> p a t", t=64), op=ALU.mult),
                                    reads=["qTh_%d_%d" % (hd, tb), "ebT%d_%d" % (par, hd), "qbz_init"],
                                    writes=["qbT%d_%d" % (par, hd)])
                                P.add("dve", lambda hd=hd, par=par: nc.vector.tensor_tensor(
                                    out=kbT[par][:, hd, :], in0=pA[par][hd][:, 2, :], in1=enbT[par][:, hd, :], op=ALU.mult),
                                    reads=["pA%d_%d" % (par, hd), "enbT%d_%d" % (par, hd)], writes=["kbT%d_%d" % (par, hd)])
                                for ch_ in range(2):
                                    P.add("dve", lambda i=i, hd=hd, par=par, ch_=ch_: nc.vector.scalar_tensor_tensor(
                                        out=kd[par][:, hd, ch_, :], in0=kk[:, i, hs(hd)], scalar=cf[:, 3, 2 + ch_:3 + ch_],
                                        in1=erev[par][:, hd, :], op0=ALU.mult, op1=ALU.mult),
                                        reads=["kk_%d" % i, "erev%d_%d" % (par, hd), "cf"],
                                        writes=["kd%d_%d_%d" % (par, hd, ch_)])
                            for hd in range(HB):
                                P.add("pe", lambda hd=hd, par=par: nc.tensor.matmul(
                                    pA[par][hd][:, 3, :], lhsT=kbT[par][:, hd, :], rhs=qv(par, hd), start=True, stop=True),
                                    reads=["kbT%d_%d" % (par, hd), "qbT%d_%d" % (par, hd)], writes=["pA%d_%d" % (par, hd)])
                            for hd in range(HB):
                                P.add("dve", lambda hd=hd, par=par: nc.vector.tensor_tensor(
                                    out=ATm[par][:, hd, :], in0=pA[par][hd][:, 3, :], in1=U2, op=ALU.mult),
                                    reads=["pA%d_%d" % (par, hd), "cf"], writes=["ATm%d_%d" % (par, hd)])
                            while pend_epi:
                                pend_epi.pop(0)()

                    def back(i, chs):
                            par = i % 2
                            tb = i // 4
                            hs = lambda hd: slice(hd * 128, (hd + 1) * 128)
                            for ch in chs:
                                cs = slice(ch * 64, (ch + 1) * 64)
                                for hd in range(HB):
                                    if ch == 0:
                                        P.add("pe", lambda i=i, hd=hd, par=par: nc.tensor.matmul(
                                            pOb[par][:, hd, :], lhsT=ATm[par][:, hd, :], rhs=vv[:, i, hs(hd)],
                                            start=(hd == 0), stop=False, skip_group_check=True),
                                            reads=["ATm%d_%d" % (par, hd), "vv_%d" % i], writes=["pO_%d" % par])
                                    P.add("pe", lambda hd=hd, par=par, cs=cs, ch=ch: nc.tensor.matmul(
                                        pOb[par][:, hd, :], lhsT=qbz[par][:, hd, ch, :], rhs=Sbf[:, hd, :],
                                        start=False, stop=(ch == 1), skip_group_check=True),
                                        reads=["qbT%d_%d" % (par, hd), "Sbf_%d" % hd], writes=["pO_%d" % par])
                                    P.add("pe", lambda i=i, hd=hd, par=par, ch=ch: nc.tensor.matmul(
                                        pSb[hd], lhsT=kd[par][:, hd, ch, :], rhs=vv[:, i, hs(hd)],
                                        start=True, stop=True),
                                        reads=["kd%d_%d_%d" % (par, hd, ch), "vv_%d" % i], writes=["pS_%d" % hd])
                                for hd in range(HB):
                                    col = ch * 64 + 63
                                    P.add("dve", lambda hd=hd, par=par, col=col: nc.vector.scalar_tensor_tensor(
                                        out=Sbf[:, hd, :], in0=S32[:, hd, :], scalar=ebT[par][:, hd, col:col + 1],
                                        in1=pSb[hd], op0=ALU.mult, op1=ALU.add),
                                        reads=["S32_%d" % hd, "ebT%d_%d" % (par, hd), "pS_%d" % hd], writes=["Sbf_%d" % hd])
                                for hd in range(HB):
                                    col = ch * 64 + 63
                                    P.add("dve", lambda hd=hd, par=par, col=col: nc.vector.scalar_tensor_tensor(
                                        out=S32[:, hd, :], in0=S32[:, hd, :], scalar=ebT[par][:, hd, col:col + 1],
                                        in1=pSb[hd], op0=ALU.mult, op1=ALU.add),
                                        reads=["S32_%d" % hd, "ebT%d_%d" % (par, hd), "pS_%d" % hd], writes=["S32_%d" % hd])
                            if 1 not in chs:
                                return
                            def epi(i=i, par=par, h0=h0):
                                for hd in range(HB):
                                    P.add("act", lambda hd=hd, par=par: nc.scalar.activation(
                                        out=junk[:], in_=pOb[par][:, hd, :], func=AF.Square, accum_out=ssq[par][:, hd:hd + 1]),
                                        reads=["pO_%d" % par], writes=["junkC", "ssqC%d_%d" % (par, hd)])
                                rsqrt(rsd[par][:], ssq[par][:], 1.0 / 128, "ssqC%d" % par, "rsdC%d" % par,
                                      rkeys=["ssqC%d_%d" % (par, hd) for hd in range(HB)])
                                for hd in range(HB):
                                    P.add("dve", lambda hd=hd, par=par, h0=h0: nc.vector.scalar_tensor_tensor(
                                        out=ost[par][:, hs(hd)], in0=pOb[par][:, hd, :], scalar=rsd[par][:, hd:hd + 1],
                                        in1=hgb[:, (h0 + hd) * 128:(h0 + hd + 1) * 128], op0=ALU.mult, op1=ALU.mult),
                                        reads=["pO_%d" % par, "rsdC%d" % par, "hgb"], writes=["ost%d" % par])
                                P.add("sp", lambda i=i, par=par, h0=h0: nc.sync.dma_start(
                                    out=o_scr[i * 128:(i + 1) * 128, 1024 + h0 * 128:1024 + h0 * 128 + W], in_=ost[par][:]),
                                    reads=["ost%d" % par], writes=["o_scr"], dma=True)
                            pend_epi.append(epi)

                    front(0)
                    for i in range(NT):
                        back(i, (0,))
                        if i + 1 < NT:
                            front(i + 1)
                        back(i, (1,))
                    while pend_epi:
                        pend_epi.pop(0)()
        if "C" in stages:
            phase_C()
            P.barrier()

        mixT = st.enter_context(nc.sbuf_tensor("mixT", [128, KC, S], BF16))

        def phase_D():
            with ExitStack() as sdd:
                sbd = lambda name, shape, dt: sdd.enter_context(nc.sbuf_tensor(name, list(shape), dt))
                wz = [wz0, sbd("wz1", [128, KC, 512], BF16)]
                ot = [sbd("ot%d" % i, [128, 512], F32) for i in range(2)]
                sz = [sbd("sz%d" % i, [128, 512], F32) for i in range(2)]
                mx = [sbd("mx%d" % i, [128, 512], BF16) for i in range(2)]
                pz = [sdd.enter_context(nc.psum_tensor("pz%d" % i, [128, 512], F32)) for i in range(2)]
                pt = [sdd.enter_context(nc.psum_tensor("ptD%d" % i, [128, 4, 128], BF16)) for i in range(2)]
                zcols = [OFF_ZA, OFF_ZA + 512, OFF_ZH, OFF_ZH + 512]
                n = 0
                pend_tr = []
                for zb in range(4):
                    wb = zb % 2
                    for k in range(KC):
                        if zb == 0:
                            break
                        P.add("pool", lambda k=k, wb=wb, zb=zb: nc.gpsimd.dma_start(
                            out=wz[wb][:, k, :], in_=w_in[k * 128:(k + 1) * 128, zcols[zb]:zcols[zb] + 512]),
                            writes=["wz%d_%d" % (wb, k)], dma=True)
                    for i in range(NT):
                        b = n % 2
                        n += 1
                        P.add("sp", lambda i=i, b=b, zb=zb: nc.sync.dma_start(
                            out=ot[b][:], in_=o_scr[i * 128:(i + 1) * 128, zb * 512:(zb + 1) * 512]),
                            reads=["o_scr"], writes=["ot%d" % b], dma=True)
                        for k in range(KC):
                            P.add("pe", lambda i=i, b=b, k=k, wb=wb: nc.tensor.matmul(
                                pz[b][:], lhsT=hT[:, k, i * 128:(i + 1) * 128], rhs=wz[wb][:, k, :],
                                start=(k == 0), stop=(k == KC - 1)),
                                reads=["hT_%d" % i, "wz%d_%d" % (wb, k)], writes=["pz%d" % b])
                        while pend_tr:
                            pend_tr.pop(0)()
                        P.add("act", lambda b=b: nc.scalar.activation(out=sz[b][:], in_=pz[b][:], func=AF.Silu),
                              reads=["pz%d" % b], writes=["sz%d" % b])
                        P.add("dve", lambda b=b: nc.vector.tensor_tensor(out=mx[b][:], in0=sz[b][:], in1=ot[b][:],
                                                                         op=ALU.mult),
                              reads=["sz%d" % b, "ot%d" % b], writes=["mx%d" % b])
                        def tr(b=b, i=i, zb=zb):
                            for kk in range(4):
                                P.add("pe", lambda kk=kk: nc.tensor.transpose(
                                    out=pt[b][:, kk, :], in_=mx[b][:, kk * 128:(kk + 1) * 128], identity=ident[:]),
                                    reads=["mx%d" % b, "ident"], writes=["ptD%d" % b])
                            P.add("dve", lambda: nc.vector.tensor_copy(
                                out=mixT[:, zb * 4:(zb + 1) * 4, i * 128:(i + 1) * 128], in_=pt[b][:]),
                                reads=["ptD%d" % b], writes=["mixT_%d" % i])
                        pend_tr.append(tr)
                while pend_tr:
                    pend_tr.pop(0)()
        if "D" in stages:
            phase_D()
            P.barrier()

        def phase_F():
            with ExitStack() as sf:
                sbf = lambda name, shape, dt: sf.enter_context(nc.sbuf_tensor(name, list(shape), dt))
                wo = hT
                fg = sbf("fg", [128, D], F32)
                xt = [sbf("xtF%d" % i, [128, D], F32) for i in range(2)]
                rt = [sbf("rtF%d" % i, [128, D], F32) for i in range(2)]
                yo = [sbf("yoF%d" % i, [128, D], F32) for i in range(2)]
                junk = sbf("junkF", [128, D], BF16)
                ssq = [sbf("ssqF%d" % i, [128, 1], F32) for i in range(2)]
                rstd = [sbf("rstdF%d" % i, [128, 1], F32) for i in range(2)]
                py = [sf.enter_context(nc.psum_tensor("py%d" % i, [128, 512], F32)) for i in range(8)]
                for k in range(KC):
                    P.add("pool", lambda k=k: nc.gpsimd.dma_start(out=wo[:, k, :], in_=w_out[k * 128:(k + 1) * 128, :]),
                          writes=["wo_%d" % k], dma=True)
                P.add("sp", lambda: nc.sync.dma_start(out=fg[:], in_=final_norm.partition_broadcast(128)),
                      writes=["fg"], dma=True)
                for i in range(NT):
                    b = i % 2
                    P.add("sp", lambda i=i, b=b: nc.sync.dma_start(out=xt[b][:], in_=x[i * 128:(i + 1) * 128, :]),
                          writes=["xtF%d" % b], dma=True)
                    for k in range(KC):
                        for nb in range(4):
                            pb = b * 4 + nb
                            P.add("pe", lambda i=i, k=k, nb=nb, pb=pb: nc.tensor.matmul(
                                py[pb][:], lhsT=mixT[:, k, i * 128:(i + 1) * 128], rhs=wo[:, k, nb * 512:(nb + 1) * 512],
                                start=(k == 0), stop=(k == KC - 1)),
                                reads=["mixT_%d" % i, "wo_%d" % k], writes=["py%d" % pb])
                    for nb in range(4):
                        pb = b * 4 + nb
                        P.add("dve", lambda b=b, nb=nb, pb=pb: nc.vector.tensor_tensor(
                            out=rt[b][:, nb * 512:(nb + 1) * 512], in0=py[pb][:], in1=xt[b][:, nb * 512:(nb + 1) * 512],
                            op=ALU.add),
                            reads=["py%d" % pb, "xtF%d" % b], writes=["rtF%d_%d" % (b, nb)])
                    rkeys = ["rtF%d_%d" % (b, nb) for nb in range(4)]
                    P.add("act", lambda b=b: nc.scalar.activation(out=junk[:], in_=rt[b][:], func=AF.Square,
                                                                  accum_out=ssq[b][:]),
                          reads=rkeys, writes=["junkF", "ssqF%d" % b])
                    rsqrt(rstd[b][:], ssq[b][:], 1.0 / D, "ssqF%d" % b, "rstdF%d" % b)
                    P.add("dve", lambda b=b: nc.vector.scalar_tensor_tensor(
                        out=yo[b][:], in0=rt[b][:], scalar=rstd[b][:, 0:1], in1=fg[:], op0=ALU.mult, op1=ALU.mult),
                        reads=rkeys + ["rstdF%d" % b, "fg"], writes=["yoF%d" % b])
                    P.add("sp", lambda i=i, b=b: nc.sync.dma_start(out=out[i * 128:(i + 1) * 128, :], in_=yo[b][:]),
                          reads=["yoF%d" % b], dma=True)
        if "F" in stages:
            phase_F()
        nops, nwaits = P.emit(st)
    return nc, (nops, nwaits)


def make_consts():
    c = {}
    c["c_ident"] = bf(np.eye(128))
    blk = np.arange(128) // 64
    same = blk[:, None] == blk[None, :]
    ii = np.arange(128)
    U2 = (same & (ii[:, None] <= ii[None, :])).astype(np.float32)
    L2 = (same & (ii[:, None] > ii[None, :])).astype(np.float32)
    cb = np.zeros((128, 128), np.float32)
    cb[64:, 0] = -80.0
    cb[:64, 1] = -80.0
    cb[:64, 2] = 1.0
    cb[64:, 3] = 1.0
    c["c_f32"] = np.ascontiguousarray(np.stack([U2, L2, np.eye(128, dtype=np.float32), cb], axis=1))

    pos = np.arange(S)
    E = np.zeros((128, S), np.float32)
    E[pos // 64, pos] = -NEG
    c["c_E"] = bf(E)
    kl = np.arange(128)[:, None]
    tl = np.arange(128)[None, :]
    diag = np.where(kl <= tl, 0.0, NEG)
    far = np.where(tl < kl, 0.0, NEG)
    dd = np.stack([diag, far], axis=0)[:, None, :, :].repeat(4, axis=1)
    c["c_diag"] = bf(np.ascontiguousarray(dd.transpose(2, 0, 1, 3)))
    n = np.arange(128)[:, None, None]
    cch = np.arange(NT)[None, :, None]
    tt = np.arange(128)[None, None, :]
    c["c_cmask"] = bf(np.where(16 * n + 31 <= 128 * cch + tt, 0.0, NEG))
    cs_ = 16 * np.arange(127)[:, None]
    ss_ = 64 * np.arange(32)[None, :]
    ov = np.clip(np.minimum(cs_ + 32, ss_ + 64) - np.maximum(cs_, ss_), 0, None) / 32.0
    mm = np.zeros((128, 32), np.float32)
    mm[:127] = ov
    c["c_mmap"] = bf(mm)
    t_abs = (128 * np.arange(NT)[None, :, None] + np.arange(128)[:, None, None])
    j = np.arange(32)[None, None, :]
    cur = t_abs // 64
    forced = ((j == 0) | (j == cur) | (j == cur - 1)).astype(np.float32)
    future = (j * 64 > t_abs).astype(np.float32)
    c["c_sel"] = np.ascontiguousarray(np.stack([1.0 - future, 1e4 * forced * (1.0 - future) - future], axis=2).astype(np.float32))

    def split_rows(p):
        a = (p // 128) * 128
        b_ = p % 128
        one = np.ones_like(p)
        return np.stack([a, a, b_, b_, one, one, one], axis=0).astype(np.float32)
    c["c_kaug"] = bf(split_rows(pos))
    c["c_kcaug"] = bf(split_rows(16 * np.arange(127) + 31))
    qa = np.zeros((4, 7, 4, S), np.float32)
    for g in range(4):
        for r in range(4):
            sl = np.float32(2.0 ** (-(4 * g + r + 1) / 2.0))
            s_hi = np.float32(bf(sl))
            s_lo = np.float32(bf(sl - s_hi))
            sp_ = np.float64(s_hi) + np.float64(s_lo)
            st = sp_ * pos.astype(np.float64)
            st1 = bf(st).astype(np.float64)
            st2 = bf(st - st1).astype(np.float64)
            st3 = bf(st - st1 - st2).astype(np.float64)
            qa[g, 0, r] = s_hi; qa[g, 1, r] = s_lo; qa[g, 2, r] = s_hi; qa[g, 3, r] = s_lo
            qa[g, 4, r] = -st1; qa[g, 5, r] = -st2; qa[g, 6, r] = -st3
    c["c_qaug"] = bf(np.ascontiguousarray(qa.reshape(4, 7, 4, NT, 128).transpose(0, 1, 3, 2, 4)))
    return c


_CACHE = {}


def kernel(**inputs):
    if "nc" not in _CACHE:
        _CACHE["nc"] = build_program()[0]
    nc = _CACHE["nc"]
    consts = make_consts()
    x = np.asarray(inputs["x"], dtype=np.float32)
    B = x.shape[0]
    shared = {
        "norm_in": np.ascontiguousarray(np.asarray(inputs["norm_in"], np.float32)[0]),
        "w_in": np.ascontiguousarray(np.asarray(inputs["w_in"], np.float32)[0]),
        "w_out": np.ascontiguousarray(np.asarray(inputs["w_out"], np.float32)[0]),
        "final_norm": np.ascontiguousarray(np.asarray(inputs["final_norm"], np.float32)),
        "nsa_out_norm": np.ascontiguousarray(np.asarray(inputs["nsa_out_norm"], np.float32)[0]),
        "hgrn_out_norm": np.ascontiguousarray(np.asarray(inputs["hgrn_out_norm"], np.float32)[0]),
        "lower_bounds": np.ascontiguousarray(np.asarray(inputs["lower_bounds"], np.float32)),
    }
    for nm in ("cmp_pe_k", "cmp_pe_v", "cmp_w1_k", "cmp_w1_v", "cmp_w2_k", "cmp_w2_v"):
        shared[nm] = np.ascontiguousarray(np.asarray(inputs[nm], np.float32)[0])
    shared.update(consts)
    in_maps = []
    for b in range(B):
        m = dict(shared)
        m["x"] = np.ascontiguousarray(x[b])
        in_maps.append(m)
    res = run_bass_kernel_spmd(nc, in_maps, core_ids=list(range(B)))
    return np.stack([np.asarray(r["out"], dtype=np.float32) for r in res.results], axis=0)
```

```python
import numpy as np
import ml_dtypes
from contextlib import ExitStack
import concourse.bass as bass
import concourse.mybir as mybir
from concourse.bass_utils import run_bass_kernel_spmd

F32 = mybir.dt.float32
BF16 = mybir.dt.bfloat16
AF = mybir.ActivationFunctionType
ALU = mybir.AluOpType
AX = mybir.AxisListType

S = 2048
D = 2048
DIN = 7728
NT = 16
KC = 16
EPS = 1e-6
OFF_QA, OFF_KC, OFF_VC, OFF_KS, OFF_VS, OFF_KW, OFF_VW = 0, 1024, 1280, 1536, 1792, 2048, 2304
OFF_GATE, OFF_ZA, OFF_QH, OFF_FH, OFF_IH, OFF_ZH = 2560, 2608, 3632, 4656, 5680, 6704
NEG = -30000.0


class Op:
    __slots__ = ("eng", "fn", "dma", "deps", "signal", "count", "slot", "target", "prev_slot_target")

    def __init__(self, eng, fn, dma):
        self.eng, self.fn, self.dma = eng, fn, dma
        self.deps = set()
        self.signal = False
        self.count = None
        self.slot = None
        self.target = None
        self.prev_slot_target = 0


class Prog:
    def __init__(self, nc, n_dma_slots=8):
        self.nc = nc
        self.ops = []
        self.last_writer = {}
        self.readers = {}
        self.n_dma_slots = n_dma_slots
        self.last_on_eng = {}
        self.dma_since_barrier = []
        self.barrier_set = None
        self.barrier_id = 0
        self.barrier_seen = {}

    def add(self, eng, fn, reads=(), writes=(), dma=False):
        i = len(self.ops)
        op = Op(eng, fn, dma)
        deps = {}
        for k in reads:
            j = self.last_writer.get(k)
            if j is not None:
                deps[j] = True
        for k in writes:
            j = self.last_writer.get(k)
            if j is not None:
                deps[j] = True
            for r in self.readers.get(k, ()):
                deps.setdefault(r, False)
        if self.barrier_set is not None and self.barrier_seen.get(eng, -1) < self.barrier_id:
            for j in self.barrier_set:
                deps.setdefault(j, True)
            self.barrier_seen[eng] = self.barrier_id
        for j, hard in deps.items():
            oj = self.ops[j]
            if not oj.dma and oj.eng == eng:
                if eng == "pe":
                    continue
            op.deps.add(j)
            if not oj.dma:
                oj.signal = True
        for k in reads:
            self.readers.setdefault(k, []).append(i)
        for k in writes:
            self.last_writer[k] = i
            self.readers[k] = []
        self.ops.append(op)
        self.last_on_eng[eng] = i
        if dma:
            self.dma_since_barrier.append(i)
        return i

    def barrier(self):
        b = set(self.last_on_eng.values()) | set(self.dma_since_barrier)
        if self.barrier_set is not None and any(
                self.barrier_seen.get(e, -1) < self.barrier_id for e in ("pe", "act", "dve", "pool", "sp")):
            b |= self.barrier_set
        self.barrier_set = b
        self.barrier_id += 1
        self.dma_since_barrier = []

    def emit(self, stack):
        nc = self.nc
        engs = {"pe": nc.tensor, "act": nc.scalar, "dve": nc.vector, "pool": nc.gpsimd, "sp": nc.sync}
        esem = {e: stack.enter_context(nc.semaphore("sem_" + e)) for e in engs}
        dsem = {}
        for q in ("sp", "pool", "act"):
            dsem[q] = [stack.enter_context(nc.semaphore("dsem_%s_%d" % (q, s))) for s in range(self.n_dma_slots)]
        cnt = {e: 0 for e in engs}
        dcount = {q: 0 for q in dsem}
        slot_total = {q: [0] * self.n_dma_slots for q in dsem}
        for op in self.ops:
            if op.dma:
                q = op.eng
                s = dcount[q] % self.n_dma_slots
                dcount[q] += 1
                op.slot = s
                op.prev_slot_target = slot_total[q][s]
                slot_total[q][s] += 16
                op.target = slot_total[q][s]
            elif op.signal:
                cnt[op.eng] += 1
                op.count = cnt[op.eng]
        seen = {e: {} for e in engs}
        nwaits = 0
        for op in self.ops:
            e = engs[op.eng]
            waits = []
            for j in op.deps:
                oj = self.ops[j]
                if oj.dma:
                    waits.append((("d", oj.eng, oj.slot), dsem[oj.eng][oj.slot], oj.target))
                else:
                    waits.append((("e", oj.eng), esem[oj.eng], oj.count))
            if op.dma and op.prev_slot_target > 0:
                waits.append((("d", op.eng, op.slot), dsem[op.eng][op.slot], op.prev_slot_target))
            best = {}
            for key, sem, val in waits:
                if val > seen[op.eng].get(key, 0) and val > best.get(key, (None, 0))[1]:
                    best[key] = (sem, val)
            for key, (sem, val) in best.items():
                e.wait_ge(sem, val)
                seen[op.eng][key] = val
                nwaits += 1
            inst = op.fn()
            if op.dma:
                inst.then_inc(dsem[op.eng][op.slot], 16)
            elif op.signal:
                inst.then_inc(esem[op.eng], 1)
        sp = nc.sync
        for q in dsem:
            for s in range(self.n_dma_slots):
                if slot_total[q][s] > seen["sp"].get(("d", q, s), 0):
                    sp.wait_ge(dsem[q][s], slot_total[q][s])
        return len(self.ops), nwaits


def bf(a):
    return np.asarray(a, dtype=np.float32).astype(ml_dtypes.bfloat16)


def build_program(stages=("A", "B", "C", "D", "F"), debug=()):
    nc = bass.Bass("TRN2", target_bir_lowering=False)
    P = Prog(nc)
    dt_in = lambda name, shape, dt=F32: nc.dram_tensor(name, list(shape), dt, kind="ExternalInput").ap()
    x = dt_in("x", [S, D])
    norm_in = dt_in("norm_in", [D])
    w_in = dt_in("w_in", [D, DIN])
    w_out = dt_in("w_out", [D, D])
    final_norm = dt_in("final_norm", [D])
    nsa_gain = dt_in("nsa_out_norm", [1024])
    hgrn_gain = dt_in("hgrn_out_norm", [1024])
    ident_d = dt_in("c_ident", [128, 128], BF16)
    lower_bounds = dt_in("lower_bounds", [2, 1024])
    cmp_pe_k = dt_in("cmp_pe_k", [32, 64]); cmp_pe_v = dt_in("cmp_pe_v", [32, 64])
    cmp_w1_k = dt_in("cmp_w1_k", [2048, 256]); cmp_w1_v = dt_in("cmp_w1_v", [2048, 256])
    cmp_w2_k = dt_in("cmp_w2_k", [256, 64]); cmp_w2_v = dt_in("cmp_w2_v", [256, 64])
    c_E = dt_in("c_E", [128, S], BF16)
    c_diag = dt_in("c_diag", [128, 2, 4, 128], BF16)
    c_cmask = dt_in("c_cmask", [128, NT, 128], BF16)
    c_mmap = dt_in("c_mmap", [128, 32], BF16)
    c_sel = dt_in("c_sel", [128, NT, 2, 32])
    c_kaug = dt_in("c_kaug", [7, S], BF16)
    c_kcaug = dt_in("c_kcaug", [7, 127], BF16)
    c_qaug = dt_in("c_qaug", [4, 7, NT, 4, 128], BF16)
    c_f32 = dt_in("c_f32", [128, 4, 128])
    out = nc.dram_tensor("out", [S, D], F32, kind="ExternalOutput").ap()
    o_scr = nc.dram_tensor("o_scr", [S, D], F32,
                           kind="ExternalOutput" if any(n == "o_scr" for n, _ in debug) else "Internal").ap()
    dbg = {}
    for name, shape in debug:
        if name == "o_scr":
            continue
        dbg[name] = nc.dram_tensor("dbg_" + name, list(shape), F32, kind="ExternalOutput").ap()

    with ExitStack() as st:
        sb = lambda name, shape, dt: st.enter_context(nc.sbuf_tensor(name, list(shape), dt))
        ps = lambda name, shape, dt: st.enter_context(nc.psum_tensor(name, list(shape), dt))

        epsb = sb("epsb", [128, 1], F32)
        P.add("dve", lambda: nc.vector.memset(epsb[:], EPS), writes=["epsb"])

        def rsqrt(o, i, scale, rkey, wkey, rkeys=None):
            np_ = o.shape[0]
            P.add("act", lambda: nc.scalar.activation(out=o, in_=i, func=AF.Ln, bias=epsb[0:np_, 0:1], scale=scale),
                  reads=(rkeys or [rkey]) + ["epsb"], writes=[wkey])
            P.add("act", lambda: nc.scalar.activation(out=o, in_=o, func=AF.Exp, scale=-0.5),
                  reads=[wkey], writes=[wkey])

        hT = sb("hT", [128, KC, S], BF16)
        wz0 = sb("wz0", [128, KC, 512], BF16)
        if "D" in stages:
            for k in range(KC):
                P.add("pool", lambda k=k: nc.gpsimd.dma_start(
                    out=wz0[:, k, :], in_=w_in[k * 128:(k + 1) * 128, OFF_ZA:OFF_ZA + 512]),
                    writes=["wz0_%d" % k], dma=True)
        ident = sb("ident", [128, 128], BF16)
        gin = sb("gin", [128, KC], F32)
        P.add("sp", lambda: nc.sync.dma_start(out=ident[:], in_=ident_d), writes=["ident"], dma=True)
        P.add("sp", lambda: nc.sync.dma_start(out=gin[:], in_=norm_in.rearrange("(k p) -> p k", p=128),
                                              allow_slow_non_contiguous=True),
              writes=["gin"], dma=True)

        def phase_A():
            with ExitStack() as sa:
                sba = lambda name, shape, dt: sa.enter_context(nc.sbuf_tensor(name, list(shape), dt))
                xt = [sba("xt%d" % i, [128, D], F32) for i in range(2)]
                xn = [sba("xn%d" % i, [128, D], BF16) for i in range(2)]
                junk = sba("junkA", [128, D], BF16)
                ssq = [sba("ssq%d" % i, [128, 1], F32) for i in range(2)]
                rstd = [sba("rstd%d" % i, [128, 1], F32) for i in range(2)]
                pt = [sa.enter_context(nc.psum_tensor("ptA%d" % i, [128, 8, 128], BF16)) for i in range(2)]
                for i in range(NT):
                    b = i % 2
                    P.add("sp", lambda i=i, b=b: nc.sync.dma_start(out=xt[b][:], in_=x[i * 128:(i + 1) * 128, :]),
                          writes=["xt%d" % b], dma=True)
                    P.add("act", lambda b=b: nc.scalar.activation(out=junk[:], in_=xt[b][:], func=AF.Square,
                                                                  accum_out=ssq[b][:]),
                          reads=["xt%d" % b], writes=["junkA", "ssq%d" % b])
                    rsqrt(rstd[b][:], ssq[b][:], 1.0 / D, "ssq%d" % b, "rstd%d" % b)
                    P.add("dve", lambda b=b: nc.vector.tensor_scalar(out=xn[b][:], in0=xt[b][:], scalar1=rstd[b][:, 0:1],
                                                                     scalar2=None, op0=ALU.mult),
                          reads=["xt%d" % b, "rstd%d" % b], writes=["xn%d" % b])
                    for half in range(2):
                        for kk in range(8):
                            k = half * 8 + kk
                            P.add("pe", lambda b=b, k=k, kk=kk, half=half: nc.tensor.transpose(
                                out=pt[half][:, kk, :], in_=xn[b][:, k * 128:(k + 1) * 128], identity=ident[:]),
                                reads=["xn%d" % b, "ident"], writes=["ptA%d" % half])
                        P.add("dve", lambda i=i, half=half: nc.vector.tensor_tensor(
                            out=hT[:, half * 8:(half + 1) * 8, i * 128:(i + 1) * 128], in0=pt[half][:],
                            in1=gin[:, half * 8:(half + 1) * 8].unsqueeze(2).to_broadcast([128, 8, 128]), op=ALU.mult),
                            reads=["ptA%d" % half, "gin"], writes=["hT_%d" % i])
        if "A" in stages:
            phase_A()
            P.barrier()
            if "hT" in dbg:
                with nc.sbuf_tensor("dbg_hT_sb", [128, KC, S], F32) as dsb:
                    P.add("dve", lambda: nc.vector.tensor_copy(out=dsb[:], in_=hT[:]),
                          reads=["hT_%d" % i for i in range(NT)], writes=["dsb"])
                    P.add("sp", lambda: nc.sync.dma_start(out=dbg["hT"].rearrange("(k p) t -> p k t", p=128), in_=dsb[:]),
                          reads=["dsb"], dma=True)
                    P.barrier()
        hT_keys = ["hT_%d" % i for i in range(NT)]

        def phase_O1():
            with nc.sbuf_tensor("ones_dbg", [128, D], F32) as ones_dbg:
                P.add("dve", lambda: nc.vector.memset(ones_dbg[:], 1.0), writes=["ones_dbg"])
                for i in range(NT):
                    P.add("sp", lambda i=i: nc.sync.dma_start(out=o_scr[i * 128:(i + 1) * 128, :], in_=ones_dbg[:]),
                          reads=["ones_dbg"], writes=["o_scr"], dma=True)
                P.barrier()
        if "O1" in stages:
            phase_O1()


        def phase_B():
            with ExitStack() as sB:
                sbb = lambda name, shape, dt: sB.enter_context(nc.sbuf_tensor(name, list(shape), dt))
                kT_aug = sbb("kT_aug", [128, 2, 4, S], BF16)
                v_aug = sbb("v_aug", [128, NT, 2, 4, 65], BF16)
                gate_sb = sbb("gate_sb", [128, NT, 48], F32)
                kcT_aug = sbb("kcT_aug", [128, 4, 127], BF16)
                vc_aug = sbb("vc_aug", [128, 4, 65], BF16)
                P.add("dve", lambda: nc.vector.memset(kT_aug[64:128, :, :, :], 0.0), writes=["kTaug_init"])
                P.add("dve", lambda: nc.vector.memset(kcT_aug[64:128, :, :], 0.0), writes=["kcTaug_init"])
                for xx in range(2):
                    for g in range(4):
                        P.add("sp", lambda xx=xx, g=g: nc.sync.dma_start(out=kT_aug[64:71, xx, g, :], in_=c_kaug),
                              reads=["kTaug_init"], writes=["kTaug_rows"], dma=True)
                        if xx == 0:
                            P.add("sp", lambda g=g: nc.sync.dma_start(out=kT_aug[96:128, 0, g, :], in_=c_E[0:32, :]),
                                  reads=["kTaug_init"], writes=["kTaug_rows"], dma=True)
                for g in range(4):
                    P.add("sp", lambda g=g: nc.sync.dma_start(out=kcT_aug[64:71, g, :], in_=c_kcaug),
                          reads=["kcTaug_init"], writes=["kcTaug_rows"], dma=True)
                P.add("dve", lambda: nc.vector.memset(v_aug[:], 1.0), writes=["v_aug_init"])
                P.add("dve", lambda: nc.vector.memset(vc_aug[:], 1.0), writes=["vc_aug_init"])

                with ExitStack() as s0:
                    sb0 = lambda name, shape, dt: s0.enter_context(nc.sbuf_tensor(name, list(shape), dt))
                    ps0 = lambda name, shape, dt: s0.enter_context(nc.psum_tensor(name, list(shape), dt))
                    slabB = sb0("slabB", [128, KC, 2, 2, 256], BF16)
                    wg = sb0("wg", [128, KC, 48], BF16)
                    kstg = [sb0("kstg%d" % i, [128, 512], BF16) for i in range(2)]
                    pp = [ps0("ppB%d" % i, [128, 512], F32) for i in range(2)]
                    pg = ps0("pgB", [128, 512], F32)
                    npj = 0
                    for k in range(KC):
                        P.add("pool", lambda k=k: nc.gpsimd.dma_start(
                            out=slabB[:, k, :, :, :].rearrange("p a b c -> p (a b c)"),
                            in_=w_in[k * 128:(k + 1) * 128, OFF_KS:OFF_KS + 1024]),
                            writes=["slabB_%d" % k], dma=True)
                        P.add("pool", lambda k=k: nc.gpsimd.dma_start(
                            out=wg[:, k, :], in_=w_in[k * 128:(k + 1) * 128, OFF_GATE:OFF_GATE + 48]),
                            writes=["wg_%d" % k], dma=True)
                    for xx in range(2):
                        for gp in range(2):
                            for tb in range(4):
                                b = npj % 2
                                npj += 1
                                for k in range(KC):
                                    P.add("pe", lambda k=k, xx=xx, gp=gp, tb=tb, b=b: nc.tensor.matmul(
                                        pp[b][:], lhsT=slabB[:, k, xx, 0, gp * 128:(gp + 1) * 128],
                                        rhs=hT[:, k, tb * 512:(tb + 1) * 512], start=(k == 0), stop=(k == KC - 1)),
                                        reads=["slabB_%d" % k] + hT_keys[tb * 4:tb * 4 + 4], writes=["ppB%d" % b])
                                P.add("act", lambda xx=xx, gp=gp, tb=tb, b=b: nc.scalar.copy(
                                    out=kT_aug[0:64, xx, 2 * gp, tb * 512:(tb + 1) * 512], in_=pp[b][0:64, :]),
                                    reads=["ppB%d" % b], writes=["kT_%d_%d" % (xx, 2 * gp)])
                                P.add("act", lambda b=b: nc.scalar.copy(out=kstg[b][64:128, :], in_=pp[b][64:128, :]),
                                      reads=["ppB%d" % b], writes=["kstg%d" % b])
                                P.add("sp", lambda xx=xx, gp=gp, tb=tb, b=b: nc.sync.dma_start(
                                    out=kT_aug[0:64, xx, 2 * gp + 1, tb * 512:(tb + 1) * 512], in_=kstg[b][64:128, :]),
                                    reads=["kstg%d" % b], writes=["kT_%d_%d" % (xx, 2 * gp + 1)], dma=True)
                    for i in range(NT):
                        b = npj % 2
                        npj += 1
                        for k in range(KC):
                            P.add("pe", lambda k=k, i=i, b=b: nc.tensor.matmul(
                                pp[b][:], lhsT=hT[:, k, i * 128:(i + 1) * 128], rhs=slabB[:, k, :, 1, :],
                                start=(k == 0), stop=(k == KC - 1)),
                                reads=["slabB_%d" % k, "hT_%d" % i], writes=["ppB%d" % b])
                        for k in range(KC):
                            P.add("pe", lambda k=k, i=i: nc.tensor.matmul(
                                pg[:, 0:48], lhsT=hT[:, k, i * 128:(i + 1) * 128], rhs=wg[:, k, :],
                                start=(k == 0), stop=(k == KC - 1)),
                                reads=["wg_%d" % k, "hT_%d" % i], writes=["pgB"])
                        for xx in range(2):
                            P.add("act", lambda xx=xx, i=i, b=b: nc.scalar.copy(
                                out=v_aug[:, i, xx, :, 0:64],
                                in_=pp[b][:, xx * 256:(xx + 1) * 256].rearrange("p (g d) -> p g d", d=64)),
                                reads=["ppB%d" % b, "v_aug_init"], writes=["v_aug_%d" % i])
                        P.add("act", lambda i=i: nc.scalar.activation(out=gate_sb[:, i, :], in_=pg[:, 0:48], func=AF.Exp,
                                                                      scale=-1.0),
                              reads=["pgB"], writes=["gate_%d" % i])
                    gkeys = ["gate_%d" % i for i in range(NT)]
                    P.add("dve", lambda: nc.vector.tensor_scalar(out=gate_sb[:], in0=gate_sb[:], scalar1=1.0, scalar2=None,
                                                                 op0=ALU.add), reads=gkeys, writes=gkeys)
                    P.add("dve", lambda: nc.vector.reciprocal(out=gate_sb[:], in_=gate_sb[:]), reads=gkeys, writes=gkeys)

                P.barrier()
                with ExitStack() as s0:
                    sb0 = lambda name, shape, dt: s0.enter_context(nc.sbuf_tensor(name, list(shape), dt))
                    ps0 = lambda name, shape, dt: s0.enter_context(nc.psum_tensor(name, list(shape), dt))
                    slabA = sb0("slabA", [128, KC, 512], BF16)
                    cmpT = sb0("cmpT", [128, 4, S], BF16)
                    w1 = [sb0("w1_%d" % kv, [128, 32, 256], BF16) for kv in range(2)]
                    w2 = [sb0("w2_%d" % kv, [128, 2, 64], BF16) for kv in range(2)]
                    peT = [sb0("peT_%d" % kv, [64, 32], BF16) for kv in range(2)]
                    bias_sb = sb0("bias_sb", [128, 4], F32)
                    hact = [sb0("hact%d" % i, [128, 2, 127], BF16) for i in range(2)]
                    gx = [sb0("gx%d" % i, [128, 127], F32) for i in range(2)]
                    gu = [sb0("gu%d" % i, [128, 127], F32) for i in range(2)]
                    ppc = [ps0("ppc%d" % i, [128, 512], F32) for i in range(2)]
                    ph = [ps0("phB%d" % i, [128, 512], F32) for i in range(2)]
                    pb_ = ps0("pbB", [128, 512], F32)
                    pc = ps0("pcB", [128, 512], F32)
                    npj = 0
                    for k in range(KC):
                        P.add("pool", lambda k=k: nc.gpsimd.dma_start(
                            out=slabA[:, k, :], in_=w_in[k * 128:(k + 1) * 128, OFF_KC:OFF_KC + 512]),
                            writes=["slabA_%d" % k], dma=True)
                    w1d = [cmp_w1_k, cmp_w1_v]
                    w2d = [cmp_w2_k, cmp_w2_v]
                    ped = [cmp_pe_k, cmp_pe_v]
                    for kv in range(2):
                        for half in range(2):
                            for lq in range(4):
                                P.add("pool", lambda kv=kv, half=half, lq=lq: nc.gpsimd.dma_start(
                                    out=w1[kv][half * 64:(half + 1) * 64, lq * 8:(lq + 1) * 8, :],
                                    in_=w1d[kv].rearrange("(l d) h -> d l h", d=64)[:, lq * 8:(lq + 1) * 8, :]),
                                    writes=["w1_%d" % kv], dma=True)
                        P.add("pool", lambda kv=kv: nc.gpsimd.dma_start(
                            out=w2[kv][:], in_=w2d[kv].rearrange("(c p) d -> p c d", p=128)),
                            writes=["w2_%d" % kv], dma=True)
                        P.add("pool", lambda kv=kv: nc.gpsimd.dma_start(
                            out=peT[kv][:], in_=ped[kv].rearrange("l d -> d l"), allow_slow_non_contiguous=True),
                            writes=["peT_%d" % kv], dma=True)
                    for cc in range(4):
                        for tb in range(4):
                            b = npj % 2
                            npj += 1
                            for k in range(KC):
                                P.add("pe", lambda k=k, cc=cc, tb=tb, b=b: nc.tensor.matmul(
                                    ppc[b][:], lhsT=slabA[:, k, cc * 128:(cc + 1) * 128], rhs=hT[:, k, tb * 512:(tb + 1) * 512],
                                    start=(k == 0), stop=(k == KC - 1)),
                                    reads=["slabA_%d" % k] + hT_keys[tb * 4:tb * 4 + 4], writes=["ppc%d" % b])
                            P.add("act", lambda cc=cc, tb=tb, b=b: nc.scalar.copy(
                                out=cmpT[:, cc, tb * 512:(tb + 1) * 512], in_=ppc[b][:]),
                                reads=["ppc%d" % b], writes=["cmpT_%d" % cc])
                    for kv in range(2):
                        for hc in range(2):
                            col = kv * 2 + hc
                            for l in range(32):
                                P.add("pe", lambda kv=kv, hc=hc, l=l, col=col: nc.tensor.matmul(
                                    pb_[:, col:col + 1], lhsT=w1[kv][0:64, l, hc * 128:(hc + 1) * 128], rhs=peT[kv][:, l:l + 1],
                                    start=(l == 0), stop=(l == 31), skip_group_check=True),
                                    reads=["w1_%d" % kv, "peT_%d" % kv], writes=["pbB"])
                    P.add("dve", lambda: nc.vector.tensor_copy(out=bias_sb[:], in_=pb_[:, 0:4]), reads=["pbB"], writes=["bias_sb"])
                    nh = 0
                    for kv in range(2):
                        for g in range(4):
                            base = 64 * (g % 2)
                            cc = kv * 2 + g // 2
                            hb = nh % 2
                            nh += 1
                            for hc in range(2):
                                for l in range(32):
                                    P.add("pe", lambda kv=kv, hc=hc, l=l, base=base, cc=cc, hb=hb: nc.tensor.matmul(
                                        ph[hb][:, hc * 128:hc * 128 + 127],
                                        lhsT=w1[kv][base:base + 64, l, hc * 128:(hc + 1) * 128],
                                        rhs=cmpT[base:base + 64, cc, l:l + 2017:16],
                                        start=(l == 0 and hc == 0), stop=(l == 31), skip_group_check=True),
                                        reads=["w1_%d" % kv, "cmpT_%d" % cc], writes=["phB%d" % hb])
                            for hc in range(2):
                                gb = hc
                                col = kv * 2 + hc
                                P.add("dve", lambda hb=hb, hc=hc, gb=gb, col=col: nc.vector.tensor_scalar(
                                    out=gx[gb][:], in0=ph[hb][:, hc * 128:hc * 128 + 127], scalar1=bias_sb[:, col:col + 1],
                                    scalar2=None, op0=ALU.add),
                                    reads=["phB%d" % hb, "bias_sb"], writes=["gx%d" % gb])
                                P.add("dve", lambda gb=gb: nc.vector.tensor_tensor(out=gu[gb][:], in0=gx[gb][:], in1=gx[gb][:],
                                                                                   op=ALU.mult),
                                      reads=["gx%d" % gb], writes=["gu%d" % gb])
                                P.add("dve", lambda gb=gb: nc.vector.tensor_scalar(out=gu[gb][:], in0=gu[gb][:], scalar1=0.044715,
                                                                                   scalar2=1.0, op0=ALU.mult, op1=ALU.add),
                                      reads=["gu%d" % gb], writes=["gu%d" % gb])
                                P.add("dve", lambda gb=gb: nc.vector.tensor_tensor(out=gu[gb][:], in0=gu[gb][:], in1=gx[gb][:],
                                                                                   op=ALU.mult),
                                      reads=["gu%d" % gb, "gx%d" % gb], writes=["gu%d" % gb])
                                P.add("act", lambda gb=gb: nc.scalar.activation(out=gu[gb][:], in_=gu[gb][:], func=AF.Exp,
                                                                                scale=-1.5957691216057308),
                                      reads=["gu%d" % gb], writes=["gu%d" % gb])
                                P.add("dve", lambda gb=gb: nc.vector.tensor_scalar(out=gu[gb][:], in0=gu[gb][:], scalar1=1.0,
                                                                                   scalar2=None, op0=ALU.add),
                                      reads=["gu%d" % gb], writes=["gu%d" % gb])
                                P.add("dve", lambda gb=gb: nc.vector.reciprocal(out=gu[gb][:], in_=gu[gb][:]),
                                      reads=["gu%d" % gb], writes=["gu%d" % gb])
                                P.add("dve", lambda gb=gb, hb=hb, hc=hc: nc.vector.tensor_tensor(
                                    out=hact[hb][:, hc, :], in0=gx[gb][:], in1=gu[gb][:], op=ALU.mult),
                                    reads=["gx%d" % gb, "gu%d" % gb], writes=["hact%d_%d" % (hb, hc)])
                            hkeys = ["hact%d_%d" % (hb, hc) for hc in range(2)]
                            if kv == 0:
                                for hc in range(2):
                                    P.add("pe", lambda hb=hb, hc=hc: nc.tensor.matmul(
                                        pc[0:64, 0:127], lhsT=w2[0][:, hc, :], rhs=hact[hb][:, hc, :],
                                        start=(hc == 0), stop=(hc == 1)),
                                        reads=hkeys + ["w2_0"], writes=["pcB"])
                                P.add("act", lambda g=g: nc.scalar.copy(out=kcT_aug[0:64, g, :], in_=pc[0:64, 0:127]),
                                      reads=["pcB"], writes=["kcT_%d" % g])
                            else:
                                for hc in range(2):
                                    P.add("pe", lambda hb=hb, hc=hc: nc.tensor.matmul(
                                        pc[0:127, 0:64], lhsT=hact[hb][:, hc, :], rhs=w2[1][:, hc, :],
                                        start=(hc == 0), stop=(hc == 1)),
                                        reads=hkeys + ["w2_1"], writes=["pcB"])
                                P.add("act", lambda g=g: nc.scalar.copy(out=vc_aug[0:127, g, 0:64], in_=pc[0:127, 0:64]),
                                      reads=["pcB", "vc_aug_init"], writes=["vc_%d" % g])
                P.barrier()
                if "kc" in dbg:
                    with nc.sbuf_tensor("dbg_kc_sb", [128, 2, 4, 127], F32) as dkc:
                        P.add("dve", lambda: nc.vector.memset(dkc[:], 0.0), writes=["dkc"])
                        P.add("dve", lambda: nc.vector.tensor_copy(out=dkc[0:71, 0, :, :], in_=kcT_aug[0:71, :, :]), writes=["dkc"])
                        P.add("dve", lambda: nc.vector.tensor_copy(out=dkc[:, 1, :, 0:65], in_=vc_aug[:]), writes=["dkc"])
                        P.add("sp", lambda: nc.sync.dma_start(out=dbg["kc"], in_=dkc[:]), reads=["dkc"], dma=True)
                        P.barrier()

                if "noB2" in stages:
                    return
                with ExitStack() as s2:
                    sb2 = lambda name, shape, dt: s2.enter_context(nc.sbuf_tensor(name, list(shape), dt))
                    ps2 = lambda name, shape, dt: s2.enter_context(nc.psum_tensor(name, list(shape), dt))
                    wqs = sb2("wqs", [128, KC, 256], BF16)
                    cdiag = sb2("cdiag", [128, 2, 4, 128], BF16)
                    ccm = sb2("ccm", [128, NT, 128], BF16)
                    cmm = sb2("cmm", [128, 32], BF16)
                    csel = sb2("csel", [128, NT, 2, 32], F32)
                    ngb = sb2("ngb", [128, 1024], F32)
                    tiny = sb2("tiny", [128, 1], F32)
                    P.add("sp", lambda: nc.sync.dma_start(out=cdiag[:], in_=c_diag), writes=["cdiag"], dma=True)
                    P.add("sp", lambda: nc.sync.dma_start(out=ccm[:], in_=c_cmask), writes=["ccm"], dma=True)
                    P.add("sp", lambda: nc.sync.dma_start(out=cmm[:], in_=c_mmap), writes=["cmm"], dma=True)
                    P.add("sp", lambda: nc.sync.dma_start(out=csel[:], in_=c_sel), writes=["csel"], dma=True)
                    P.add("sp", lambda: nc.sync.dma_start(out=ngb[:], in_=nsa_gain.partition_broadcast(128)),
                          writes=["ngb"], dma=True)
                    P.add("dve", lambda: nc.vector.memset(tiny[:], 1e-30), writes=["tiny"])
                    qTa = [sb2("qT_aug%d" % i, [128, NT, 4, 128], BF16) for i in range(2)]
                    PT = [sb2("PT%d" % i, [128, 512], BF16) for i in range(3)]
                    negselT = [sb2("negsel%d" % i, [128, 128], F32) for i in range(2)]
                    coef = sb2("coef", [128, 3, 4], F32)
                    coefC = sb2("coefC", [128, 4], F32)
                    pslc = sb2("pslc", [128, 32], F32)
                    score = sb2("score", [128, 32], F32)
                    top8 = sb2("top8", [128, 8], F32)
                    oacc = [sb2("oacc%d" % i, [128, 4, 64], F32) for i in range(3)]
                    rinvE = [sb2("rinvE%d" % i, [128, 3, 4], F32) for i in range(3)]
                    otmp = sb2("otmp", [128, 4, 64], F32)
                    ssq = [sb2("ssqB%d" % i, [128, 4], F32) for i in range(3)]
                    rsd = [sb2("rsdB%d" % i, [128, 4], F32) for i in range(3)]
                    ostage = [sb2("ostB%d" % i, [128, 256], F32) for i in range(3)]
                    idf = sb2("idfB", [128, 128], F32)
                    qstg = [sb2("qstg%d" % i, [128, 512], BF16) for i in range(2)]
                    pS = [ps2("pS%d" % i, [128, 512], F32) for i in range(3)]
                    pOc = ps2("pOc0", [128, 512], F32)
                    pOs2 = [ps2("pOs%d" % i, [128, 512], F32) for i in range(2)]
                    pOw2 = [ps2("pOw%d" % i, [128, 512], F32) for i in range(2)]
                    P.add("sp", lambda: nc.sync.dma_start(out=idf[:], in_=c_f32[:, 2, :]), writes=["idfB"], dma=True)
                    for i_ in range(2):
                        P.add("dve", lambda i_=i_: nc.vector.memset(negselT[i_][:], 0.0), writes=["negsel%d" % i_])
                    for qb in range(2):
                        P.add("dve", lambda qb=qb: nc.vector.memset(qTa[qb][64:128, :, :, :], 0.0), writes=["qaug_init%d" % qb])
                    O3 = lambda t_: t_[:, 0:260].rearrange("p (r e) -> p r e", e=65)
                    kOc = "pOc0"

                    def load_wq(g):
                        for k in range(KC):
                            P.add("pool", lambda k=k: nc.gpsimd.dma_start(
                                out=wqs[:, k, :], in_=w_in[k * 128:(k + 1) * 128, OFF_QA + g * 256:OFF_QA + (g + 1) * 256]),
                                writes=["wqs_%d" % k], dma=True)

                    def load_qaug(g):
                        qb = g % 2
                        P.add("sp", lambda: nc.sync.dma_start(out=qTa[qb][64:71, :, :, :], in_=c_qaug[g]),
                              reads=["qaug_init%d" % qb], writes=["qaug_rows%d" % qb], dma=True)

                    stream = []
                    for r in range(2):
                        for tb in range(4):
                            stream.append(("qp", 0, r, tb))
                    for g in range(4):
                        gj = []
                        gj.append(("cmp", g, 0, 0))
                        for c in range(NT):
                            gj.append(("selpe", g, c, 0))
                            if c + 1 < NT:
                                gj.append(("cmp", g, c + 1, 0))
                            gj += [("win", g, c, m) for m in range(max(0, c - 4), c + 1)]
                            gj += [("slc", g, c, m) for m in range(c + 1)]
                        if g + 1 < 4:
                            qps = [("qp", g + 1, r, tb) for r in range(2) for tb in range(4)]
                            out_ = []
                            for ji, jb in enumerate(gj):
                                out_.append(jb)
                                if ji >= 40 and (ji - 40) % 20 == 0 and qps:
                                    out_.append(qps.pop(0))
                            out_ += qps
                            gj = out_
                        stream += gj
                    n_st = len(stream)

                    def qkeys(g, c):
                        qb = g % 2
                        return ["qT%d_%d" % (qb, c // 4), "qaug_rows%d" % qb, "nsT%d_%d" % (qb, c)]

                    def emit_S(idx):
                        job = stream[idx]
                        sb_ = idx % 3
                        kind = job[0]
                        if kind == "selpe":
                            return
                        if kind == "qp":
                            _, g, r, tb = job
                            for k in range(KC):
                                P.add("pe", lambda k=k: nc.tensor.matmul(
                                    pS[sb_][:], lhsT=wqs[:, k, r * 128:(r + 1) * 128], rhs=hT[:, k, tb * 512:(tb + 1) * 512],
                                    start=(k == 0), stop=(k == KC - 1)),
                                    reads=["wqs_%d" % k] + hT_keys[tb * 4:tb * 4 + 4], writes=["pS%d" % sb_])
                            return
                        _, g, c, m = job
                        qT_aug = qTa[g % 2]
                        ms = slice(m * 128, (m + 1) * 128)
                        if kind == "cmp":
                            P.add("pe", lambda: nc.tensor.matmul(
                                pS[sb_][0:127, :], lhsT=kcT_aug[:, g, :], rhs=qT_aug[:, c, :, :], start=True, stop=False,
                                skip_group_check=True),
                                reads=["kcT_%d" % g, "kcTaug_rows"] + qkeys(g, c), writes=["pS%d" % sb_])
                            for r in range(4):
                                P.add("pe", lambda r=r: nc.tensor.matmul(
                                    pS[sb_][0:127, r * 128:(r + 1) * 128], lhsT=ident[0:127, 0:127], rhs=ccm[0:127, c, :],
                                    start=False, stop=(r == 3), skip_group_check=True),
                                    reads=["ident", "ccm"], writes=["pS%d" % sb_])
                        else:
                            xx = 0 if kind == "slc" else 1
                            extra = []
                            if m == c:
                                extra.append(0)
                            if kind == "win" and m == c - 4:
                                extra.append(1)
                            P.add("pe", lambda: nc.tensor.matmul(
                                pS[sb_][:], lhsT=kT_aug[:, xx, g, ms], rhs=qT_aug[:, c, :, :], start=True,
                                stop=(len(extra) == 0), skip_group_check=True),
                                reads=["kT_%d_%d" % (xx, g), "kTaug_rows"] + qkeys(g, c), writes=["pS%d" % sb_])
                            for ei, di in enumerate(extra):
                                last = ei == len(extra) - 1
                                P.add("pe", lambda last=last, di=di: nc.tensor.matmul(
                                    pS[sb_][:], lhsT=ident[:], rhs=cdiag[:, di, :, :], start=False, stop=last,
                                    skip_group_check=True),
                                    reads=["ident", "cdiag"], writes=["pS%d" % sb_])

                    def emit_act(idx):
                        job = stream[idx]
                        sb_ = idx % 3
                        if job[0] == "selpe":
                            return
                        if job[0] == "qp":
                            _, g, r, tb = job
                            qT_aug = qTa[g % 2]
                            sg = idx % 2
                            P.add("act", lambda: nc.scalar.activation(
                                out=qT_aug[0:64, tb * 4:(tb + 1) * 4, 2 * r, :],
                                in_=pS[sb_][0:64, :].rearrange("p (c t) -> p c t", t=128), func=AF.Copy, scale=0.125),
                                reads=["pS%d" % sb_], writes=["qT%d_%d" % (g % 2, tb)])
                            P.add("act", lambda: nc.scalar.activation(
                                out=qstg[sg][64:128, :], in_=pS[sb_][64:128, :], func=AF.Copy, scale=0.125),
                                reads=["pS%d" % sb_], writes=["qstg%d" % sg])
                            P.add("sp", lambda: nc.sync.dma_start(
                                out=qT_aug[0:64, tb * 4:(tb + 1) * 4, 2 * r + 1, :],
                                in_=qstg[sg][64:128, :].rearrange("p (c t) -> p c t", t=128)),
                                reads=["qstg%d" % sg], writes=["qT%d_%d" % (g % 2, tb)], dma=True)
                            return
                        np_ = 127 if job[0] == "cmp" else 128
                        P.add("act", lambda: nc.scalar.activation(out=PT[sb_][0:np_, :], in_=pS[sb_][0:np_, :], func=AF.Exp),
                              reads=["pS%d" % sb_], writes=["PT%d" % sb_])

                    def emit_PV(idx):
                        job = stream[idx]
                        pb_i = idx % 3
                        kind = job[0]
                        if kind in ("qp", "selpe"):
                            return
                        _, g, c, m = job
                        if kind == "cmp":
                            for r in range(4):
                                P.add("pe", lambda r=r: nc.tensor.matmul(
                                    O3(pOc)[:, r, :], lhsT=PT[pb_i][0:127, r * 128:(r + 1) * 128], rhs=vc_aug[0:127, g, :],
                                    start=(r == 0), stop=True, skip_group_check=True),
                                    reads=["PT%d" % pb_i, "vc_%d" % g, "vc_aug_init"], writes=[kOc])
                            for r in range(4):
                                P.add("pe", lambda r=r: nc.tensor.matmul(
                                    pOc[:, 260 + r * 32:260 + (r + 1) * 32], lhsT=PT[pb_i][0:127, r * 128:(r + 1) * 128],
                                    rhs=cmm[0:127, :], start=False, stop=True, skip_group_check=True),
                                    reads=["PT%d" % pb_i, "cmm"], writes=[kOc])
                        else:
                            xx = 0 if kind == "slc" else 1
                            pO = pOs2[c % 2] if kind == "slc" else pOw2[c % 2]
                            key = ("pOs%d" if kind == "slc" else "pOw%d") % (c % 2)
                            m0 = 0 if kind == "slc" else max(0, c - 4)
                            for r in range(4):
                                P.add("pe", lambda r=r: nc.tensor.matmul(
                                    O3(pO)[:, r, :], lhsT=PT[pb_i][:, r * 128:(r + 1) * 128], rhs=v_aug[:, m, xx, g, :],
                                    start=(m == m0 and r == 0), stop=(m == c), skip_group_check=True),
                                    reads=["PT%d" % pb_i, "v_aug_%d" % m], writes=[key])

                    def emit_select(g, c):
                        ob = (g * NT + c) % 3
                        rk = "rinvE%d_0" % ob
                        P.add("dve", lambda: nc.vector.tensor_scalar(
                            out=rinvE[ob][:, 0, :], in0=O3(pOc)[:, :, 64], scalar1=tiny[:, 0:1], scalar2=None, op0=ALU.add),
                            reads=[kOc, "tiny"], writes=[rk])
                        P.add("dve", lambda: nc.vector.reciprocal(out=rinvE[ob][:, 0, :], in_=rinvE[ob][:, 0, :]),
                              reads=[rk], writes=[rk])
                        for r in range(4):
                            if r == 0:
                                P.add("dve", lambda: nc.vector.tensor_scalar(
                                    out=pslc[:], in0=pOc[:, 260:292], scalar1=rinvE[ob][:, 0, 0:1], scalar2=None, op0=ALU.mult),
                                    reads=[kOc, rk], writes=["pslc"])
                            else:
                                P.add("dve", lambda r=r: nc.vector.scalar_tensor_tensor(
                                    out=pslc[:], in0=pOc[:, 260 + r * 32:292 + r * 32], scalar=rinvE[ob][:, 0, r:r + 1],
                                    in1=pslc[:], op0=ALU.mult, op1=ALU.add),
                                    reads=[kOc, rk, "pslc"], writes=["pslc"])
                        P.add("dve", lambda: nc.vector.tensor_tensor(out=score[:], in0=pslc[:], in1=csel[:, c, 0, :], op=ALU.mult),
                              reads=["pslc", "csel"], writes=["score"])
                        P.add("dve", lambda: nc.vector.tensor_tensor(out=score[:], in0=score[:], in1=csel[:, c, 1, :], op=ALU.add),
                              reads=["score", "csel"], writes=["score"])
                        P.add("dve", lambda: nc.vector.max(out=top8[:], in_=score[:]), reads=["score"], writes=["top8"])
                        P.add("dve", lambda: nc.vector.tensor_scalar(
                            out=negselT[c % 2][:, 96:128], in0=score[:], scalar1=top8[:, 7:8], scalar2=-1.0,
                            op0=ALU.is_ge, op1=ALU.add),
                            reads=["score", "top8"], writes=["negsel%d" % (c % 2)])
                        gsl0 = gate_sb[:, c, g * 12:(g + 1) * 12].rearrange("p (r x) -> p x r", x=3)[:, 0, :]
                        P.add("dve", lambda: nc.vector.tensor_tensor(out=coefC[:], in0=rinvE[ob][:, 0, :], in1=gsl0, op=ALU.mult),
                              reads=[rk, "gate_%d" % c], writes=["coefC"])
                        P.add("dve", lambda: nc.vector.tensor_tensor(
                            out=oacc[ob][:], in0=O3(pOc)[:, :, 0:64], in1=coefC[:].unsqueeze(2).to_broadcast([128, 4, 64]),
                            op=ALU.mult),
                            reads=[kOc, "coefC"], writes=["oacc%d" % ob])

                    def emit_selpe(g, c):
                        qT_aug = qTa[g % 2]
                        pOs = pOs2[c % 2]
                        kOs = "pOs%d" % (c % 2)
                        P.add("pe", lambda: nc.tensor.transpose(out=pOs[:, 260:388], in_=negselT[c % 2][:], identity=idf[:]),
                              reads=["negsel%d" % (c % 2), "idfB"], writes=[kOs])
                        P.add("dve", lambda: nc.vector.tensor_copy(
                            out=qT_aug[96:128, c, :, :], in_=pOs[96:128, 260:388].unsqueeze(1).to_broadcast([32, 4, 128])),
                            reads=[kOs, "qaug_init%d" % (g % 2)], writes=["nsT%d_%d" % (g % 2, c)])

                    def epi_part1(g, c):
                        ob = (g * NT + c) % 3
                        pOs, pOw = pOs2[c % 2], pOw2[c % 2]
                        kOs, kOw = "pOs%d" % (c % 2), "pOw%d" % (c % 2)
                        for bi, pO, key in ((1, pOs, kOs), (2, pOw, kOw)):
                            P.add("dve", lambda bi=bi, pO=pO: nc.vector.reciprocal(out=rinvE[ob][:, bi, :], in_=O3(pO)[:, :, 64]),
                                  reads=[key], writes=["rinvE%d_%d" % (ob, bi)])
                        gsl = gate_sb[:, c, g * 12:(g + 1) * 12].rearrange("p (r x) -> p x r", x=3)
                        P.add("dve", lambda: nc.vector.tensor_tensor(out=coef[:], in0=rinvE[ob][:], in1=gsl, op=ALU.mult),
                              reads=["rinvE%d_%d" % (ob, bi) for bi in range(3)] + ["gate_%d" % c], writes=["coef"])
                        for bi, pO, key in ((1, pOs, kOs), (2, pOw, kOw)):
                            P.add("dve", lambda bi=bi, pO=pO: nc.vector.tensor_tensor(
                                out=otmp[:], in0=O3(pO)[:, :, 0:64],
                                in1=coef[:, bi, :].unsqueeze(2).to_broadcast([128, 4, 64]), op=ALU.mult),
                                reads=[key, "coef"], writes=["otmp"])
                            P.add("dve", lambda: nc.vector.tensor_tensor(out=oacc[ob][:], in0=oacc[ob][:], in1=otmp[:], op=ALU.add),
                                  reads=["oacc%d" % ob, "otmp"], writes=["oacc%d" % ob])
                        P.add("dve", lambda: nc.vector.tensor_tensor(out=otmp[:], in0=oacc[ob][:], in1=oacc[ob][:], op=ALU.mult),
                              reads=["oacc%d" % ob], writes=["otmp"])
                        P.add("dve", lambda: nc.vector.tensor_reduce(out=ssq[ob][:], in_=otmp[:], axis=AX.X, op=ALU.add),
                              reads=["otmp"], writes=["ssqB%d" % ob])

                    def epi_tail(g, c):
                        ob = (g * NT + c) % 3
                        rsqrt(rsd[ob][:], ssq[ob][:], 1.0 / 64, "ssqB%d" % ob, "rsdB%d" % ob)
                        P.add("dve", lambda: nc.vector.tensor_tensor(
                            out=oacc[ob][:], in0=oacc[ob][:], in1=rsd[ob][:].unsqueeze(2).to_broadcast([128, 4, 64]), op=ALU.mult),
                            reads=["oacc%d" % ob, "rsdB%d" % ob], writes=["oacc%d" % ob])
                        P.add("dve", lambda: nc.vector.tensor_tensor(
                            out=ostage[ob][:], in0=oacc[ob][:].rearrange("p r d -> p (r d)"), in1=ngb[:, g * 256:(g + 1) * 256],
                            op=ALU.mult),
                            reads=["oacc%d" % ob, "ngb"], writes=["ostB%d" % ob])
                        P.add("sp", lambda: nc.sync.dma_start(
                            out=o_scr[c * 128:(c + 1) * 128, g * 256:(g + 1) * 256], in_=ostage[ob][:]),
                            reads=["ostB%d" % ob], writes=["o_scr"], dma=True)

                    load_wq(0)
                    load_qaug(0)
                    sel_done = set()
                    deferred = []
                    pend_p1, pend_tail = [], []
                    s_emitted = set()

                    def try_S(idx):
                        if idx >= n_st or idx in s_emitted:
                            return
                        job = stream[idx]
                        if job[0] == "slc" and (job[1], job[2]) not in sel_done:
                            deferred.append(idx)
                            return
                        s_emitted.add(idx)
                        emit_S(idx)

                    try_S(0)
                    try_S(1)
                    for idx, job in enumerate(stream):
                        try_S(idx + 2)
                        if idx not in s_emitted:
                            s_emitted.add(idx)
                            if idx in deferred:
                                deferred.remove(idx)
                            emit_S(idx)
                        emit_act(idx)
                        emit_PV(idx)
                        kind = job[0]
                        if kind == "qp":
                            _, g_, r_, tb_ = job
                            if r_ == 1 and tb_ == 3 and g_ + 1 < 4:
                                load_wq(g_ + 1)
                            continue
                        _, g, c, m = job
                        if kind == "selpe":
                            emit_selpe(g, c)
                            sel_done.add((g, c))
                            for d_ in list(deferred):
                                deferred.remove(d_)
                                try_S(d_)
                            continue
                        if kind == "cmp":
                            if c == 2 and g + 1 < 4:
                                load_qaug(g + 1)
                            seq = g * NT + c
                            while pend_tail and pend_tail[0][3] <= seq - 3:
                                gt, ct, _, _ = pend_tail.pop(0)
                                epi_tail(gt, ct)
                            emit_select(g, c)
                            while pend_p1:
                                epi_part1(*pend_p1.pop(0))
                        for pt_ in pend_tail:
                            pt_[2] -= 1
                        while pend_tail and pend_tail[0][2] <= 0 and (pend_tail[0][0], pend_tail[0][1]) not in pend_p1:
                            gt, ct, _, _ = pend_tail.pop(0)
                            epi_tail(gt, ct)
                        if kind == "slc" and m == c:
                            pend_p1.append((g, c))
                            pend_tail.append([g, c, 40, g * NT + c])
                    while pend_p1:
                        epi_part1(*pend_p1.pop(0))
                    while pend_tail:
                        gt, ct, _, _ = pend_tail.pop(0)
                        epi_tail(gt, ct)
        if "B" in stages:
            phase_B()
            P.barrier()

        HB = 2

        def phase_C():
            with ExitStack() as sc:
                sbc = lambda name, shape, dt: sc.enter_context(nc.sbuf_tensor(name, list(shape), dt))
                psc = lambda name, shape, dt: sc.enter_context(nc.psum_tensor(name, list(shape), dt))
                cf = sbc("cf", [128, 4, 128], F32)
                P.add("sp", lambda: nc.sync.dma_start(out=cf[:], in_=c_f32), writes=["cf"], dma=True)
                U2, L2, IDF = cf[:, 0, :], cf[:, 1, :], cf[:, 2, :]
                lbb = sbc("lbb", [128, 1024], F32)
                oml = sbc("oml", [128, 1024], F32)
                hgb = sbc("hgb", [128, 1024], F32)
                P.add("sp", lambda: nc.sync.dma_start(out=lbb[:], in_=lower_bounds[0].partition_broadcast(128)),
                      writes=["lbb"], dma=True)
                P.add("sp", lambda: nc.sync.dma_start(out=oml[:], in_=lower_bounds[1].partition_broadcast(128)),
                      writes=["oml"], dma=True)
                P.add("sp", lambda: nc.sync.dma_start(out=hgb[:], in_=hgrn_gain.partition_broadcast(128)),
                      writes=["hgb"], dma=True)
                P.add("dve", lambda: nc.vector.tensor_tensor(out=oml[:], in0=oml[:], in1=lbb[:], op=ALU.subtract),
                      reads=["lbb", "oml"], writes=["oml"])
                P.add("act", lambda: nc.scalar.activation(out=oml[:], in_=oml[:], func=AF.Exp), reads=["oml"], writes=["oml"])
                P.add("dve", lambda: nc.vector.tensor_scalar(out=oml[:], in0=oml[:], scalar1=1.0, scalar2=None, op0=ALU.add),
                      reads=["oml"], writes=["oml"])
                P.add("dve", lambda: nc.vector.reciprocal(out=lbb[:], in_=oml[:]), reads=["oml"], writes=["lbb"])
                P.add("dve", lambda: nc.vector.tensor_scalar(out=oml[:], in0=lbb[:], scalar1=-1.0, scalar2=1.0,
                                                             op0=ALU.mult, op1=ALU.add), reads=["lbb"], writes=["oml"])

                W = HB * 128
                wq = sbc("wq", [128, KC, W], BF16)
                wfi = sbc("wfi", [128, KC, 2, W], BF16)
                qT = sbc("qTh", [128, HB, S], BF16)
                logf = sbc("logf", [128, NT, W], F32)
                kk = sbc("kk", [128, NT, W], F32)
                vv = sbc("vv", [128, NT, W], BF16)
                S32 = sbc("S32", [128, HB, 128], F32)
                Sbf = sbc("Sbf", [128, HB, 128], BF16)
                tmpe = [sbc("tmpe%d" % i, [128, W], F32) for i in range(2)]
                tmpf = [sbc("tmpf%d" % i, [128, W], F32) for i in range(2)]
                ebT = [sbc("ebT%d" % i, [128, HB, 128], F32) for i in range(2)]
                enbT = [sbc("enbT%d" % i, [128, HB, 128], F32) for i in range(2)]
                erev = [sbc("erev%d" % i, [128, HB, 128], F32) for i in range(2)]
                qbz = [sbc("qbz%d" % i, [128, HB, 2, 128], BF16) for i in range(2)]
                for i_ in range(2):
                    P.add("dve", lambda i_=i_: nc.vector.memset(qbz[i_][:], 0.0), writes=["qbz_init"])
                qv = lambda par, hd: qbz[par][:, hd, :, :].rearrange("p a (b t) -> p (a b) t", t=64)[:, 0:4:3, :]
                kbT = [sbc("kbT%d" % i, [128, HB, 128], BF16) for i in range(2)]
                kd = [sbc("kd%d" % i, [128, HB, 2, 128], BF16) for i in range(2)]
                ATm = [sbc("ATm%d" % i, [128, HB, 128], BF16) for i in range(2)]
                ost = [sbc("ost%d" % i, [128, W], F32) for i in range(2)]
                junk = sbc("junkC", [128, 128], BF16)
                ssq = [sbc("ssqC%d" % i, [128, HB], F32) for i in range(2)]
                rsd = [sbc("rsdC%d" % i, [128, HB], F32) for i in range(2)]
                pA = [[psc("pA%d_%d" % (par, hd), [128, 4, 128], F32) for hd in range(HB)] for par in range(2)]
                pOb = [psc("pO_%d" % par, [128, 4, 128], F32) for par in range(2)]
                pproj = [pOb[i_][:, :, :].rearrange("p a b -> p (a b)") for i_ in range(2)]
                pSf = [psc("pS_%d" % hd, [128, 4, 128], F32) for hd in range(HB)]
                pSb = [t[:, 0, :] for t in pSf]

                npj = 0
                for h0 in range(0, 8, HB):
                    for k in range(KC):
                        P.add("pool", lambda k=k, h0=h0: nc.gpsimd.dma_start(
                            out=wq[:, k, :], in_=w_in[k * 128:(k + 1) * 128, OFF_QH + h0 * 128:OFF_QH + h0 * 128 + W]),
                            writes=["wq_%d" % k], dma=True)
                        P.add("pool", lambda k=k, h0=h0: nc.gpsimd.dma_start(
                            out=wfi[:, k, 0, :], in_=w_in[k * 128:(k + 1) * 128, OFF_FH + h0 * 128:OFF_FH + h0 * 128 + W]),
                            writes=["wf_%d" % k], dma=True)
                        P.add("pool", lambda k=k, h0=h0: nc.gpsimd.dma_start(
                            out=wfi[:, k, 1, :], in_=w_in[k * 128:(k + 1) * 128, OFF_IH + h0 * 128:OFF_IH + h0 * 128 + W]),
                            writes=["wi_%d" % k], dma=True)
                    for hd in range(HB):
                        for tb in range(4):
                            pp = npj % 2
                            npj += 1
                            for k in range(KC):
                                P.add("pe", lambda k=k, hd=hd, tb=tb, pp=pp: nc.tensor.matmul(
                                    pproj[pp], lhsT=wq[:, k, hd * 128:(hd + 1) * 128], rhs=hT[:, k, tb * 512:(tb + 1) * 512],
                                    start=(k == 0), stop=(k == KC - 1)),
                                    reads=["wq_%d" % k] + hT_keys[tb * 4:tb * 4 + 4], writes=["pO_%d" % pp])
                            P.add("act", lambda hd=hd, tb=tb, pp=pp: nc.scalar.copy(
                                out=qT[:, hd, tb * 512:(tb + 1) * 512], in_=pproj[pp]),
                                reads=["pO_%d" % pp], writes=["qTh_%d_%d" % (hd, tb)])
                    for i in range(NT):
                        pp = npj % 2
                        npj += 1
                        b = i % 2
                        for k in range(KC):
                            P.add("pe", lambda k=k, i=i, pp=pp: nc.tensor.matmul(
                                pproj[pp], lhsT=hT[:, k, i * 128:(i + 1) * 128], rhs=wfi[:, k, :, :],
                                start=(k == 0), stop=(k == KC - 1)),
                                reads=["wf_%d" % k, "wi_%d" % k, "hT_%d" % i], writes=["pO_%d" % pp])
                        P.add("act", lambda pp=pp, b=b: nc.scalar.activation(out=tmpe[b][:], in_=pproj[pp][:, 0:W],
                                                                             func=AF.Exp, scale=-1.0),
                              reads=["pO_%d" % pp], writes=["tmpe%d" % b])
                        P.add("act", lambda pp=pp, i=i: nc.scalar.copy(out=vv[:, i, :], in_=pproj[pp][:, W:2 * W]),
                              reads=["pO_%d" % pp], writes=["vv_%d" % i])
                        P.add("dve", lambda b=b: nc.vector.tensor_scalar(out=tmpe[b][:], in0=tmpe[b][:], scalar1=1.0,
                                                                         scalar2=None, op0=ALU.add),
                              reads=["tmpe%d" % b], writes=["tmpe%d" % b])
                        P.add("dve", lambda b=b: nc.vector.reciprocal(out=tmpe[b][:], in_=tmpe[b][:]),
                              reads=["tmpe%d" % b], writes=["tmpe%d" % b])
                        P.add("dve", lambda b=b, h0=h0: nc.vector.tensor_tensor(
                            out=tmpf[b][:], in0=tmpe[b][:], in1=oml[:, h0 * 128:h0 * 128 + W], op=ALU.mult),
                            reads=["tmpe%d" % b, "oml"], writes=["tmpf%d" % b])
                        P.add("dve", lambda b=b, h0=h0: nc.vector.tensor_tensor(
                            out=tmpf[b][:], in0=tmpf[b][:], in1=lbb[:, h0 * 128:h0 * 128 + W], op=ALU.add),
                            reads=["tmpf%d" % b, "lbb"], writes=["tmpf%d" % b])
                        P.add("act", lambda b=b, i=i: nc.scalar.activation(out=logf[:, i, :], in_=tmpf[b][:], func=AF.Ln),
                              reads=["tmpf%d" % b], writes=["logf_%d" % i])
                        P.add("dve", lambda b=b, i=i: nc.vector.tensor_scalar(
                            out=kk[:, i, :], in0=tmpf[b][:], scalar1=-1.0, scalar2=1.0, op0=ALU.mult, op1=ALU.add),
                            reads=["tmpf%d" % b], writes=["kk_%d" % i])
                    P.add("dve", lambda: nc.vector.memset(S32[:], 0.0), writes=["S32_%d" % hd for hd in range(HB)])
                    P.add("dve", lambda: nc.vector.memset(Sbf[:], 0.0), writes=["Sbf_%d" % hd for hd in range(HB)])
                    pend_epi = []
                    def front(i):
                            par = i % 2
                            tb = i // 4
                            hs = lambda hd: slice(hd * 128, (hd + 1) * 128)
                            for hd in range(HB):
                                P.add("pe", lambda i=i, hd=hd, par=par: nc.tensor.matmul(
                                    pA[par][hd][:, 0, :], lhsT=logf[:, i, hs(hd)], rhs=U2, start=True, stop=True),
                                    reads=["logf_%d" % i, "cf"], writes=["pA%d_%d" % (par, hd)])
                                P.add("pe", lambda i=i, hd=hd, par=par: nc.tensor.matmul(
                                    pA[par][hd][:, 1, :], lhsT=L2, rhs=logf[:, i, hs(hd)], start=True, stop=True),
                                    reads=["logf_%d" % i, "cf"], writes=["pA%d_%d" % (par, hd)])
                                P.add("pe", lambda i=i, hd=hd, par=par: nc.tensor.transpose(
                                    out=pA[par][hd][:, 2, :], in_=kk[:, i, hs(hd)], identity=IDF),
                                    reads=["kk_%d" % i, "cf"], writes=["pA%d_%d" % (par, hd)])
                            for hd in range(HB):
                                P.add("act", lambda hd=hd, par=par: nc.scalar.activation(
                                    out=ebT[par][:, hd, :], in_=pA[par][hd][:, 0, :], func=AF.Exp),
                                    reads=["pA%d_%d" % (par, hd)], writes=["ebT%d_%d" % (par, hd)])
                                P.add("act", lambda hd=hd, par=par: nc.scalar.activation(
                                    out=enbT[par][:, hd, :], in_=pA[par][hd][:, 0, :], func=AF.Exp, scale=-1.0),
                                    reads=["pA%d_%d" % (par, hd)], writes=["enbT%d_%d" % (par, hd)])
                                P.add("act", lambda hd=hd, par=par: nc.scalar.activation(
                                    out=erev[par][:, hd, :], in_=pA[par][hd][:, 1, :], func=AF.Exp),
                                    reads=["pA%d_%d" % (par, hd)], writes=["erev%d_%d" % (par, hd)])
                            for hd in range(HB):
                                P.add("dve", lambda i=i, hd=hd, par=par: nc.vector.tensor_tensor(
                                    out=qv(par, hd), in0=qT[:, hd, i * 128:(i + 1) * 128].rearrange("p (a t) -> p a t", t=64),
                                    in1=ebT[par][:, hd, :].rearrange("p (a t) -> p a t", t=64), op=ALU.mult),
                                    reads=["qTh_%d_%d" % (hd, tb), "ebT%d_%d" % (par, hd), "qbz_init"],
                                    writes=["qbT%d_%d" % (par, hd)])
                                P.add("dve", lambda hd=hd, par=par: nc.vector.tensor_tensor(
                                    out=kbT[par][:, hd, :], in0=pA[par][hd][:, 2, :], in1=enbT[par][:, hd, :], op=ALU.mult),
                                    reads=["pA%d_%d" % (par, hd), "enbT%d_%d" % (par, hd)], writes=["kbT%d_%d" % (par, hd)])
                                for ch_ in range(2):
                                    P.add("dve", lambda i=i, hd=hd, par=par, ch_=ch_: nc.vector.scalar_tensor_tensor(
                                        out=kd[par][:, hd, ch_, :], in0=kk[:, i, hs(hd)], scalar=cf[:, 3, 2 + ch_:3 + ch_],
                                        in1=erev[par][:, hd, :], op0=ALU.mult, op1=ALU.mult),
                                        reads=["kk_%d" % i, "erev%d_%d" % (par, hd), "cf"],
                                        writes=["kd%d_%d_%d" % (par, hd, ch_)])
                            for hd in range(HB):
                                P.add("pe", lambda hd=hd, par=par: nc.tensor.matmul(
                                    pA[par][hd][:, 3, :], lhsT=kbT[par][:, hd, :], rhs=qv(par, hd), start=True, stop=True),
                                    reads=["kbT%d_%d" % (par, hd), "qbT%d_%d" % (par, hd)], writes=["pA%d_%d" % (par, hd)])
                            for hd in range(HB):
                                P.add("dve", lambda hd=hd, par=par: nc.vector.tensor_tensor(
                                    out=ATm[par][:, hd, :], in0=pA[par][hd][:, 3, :], in1=U2, op=ALU.mult),
                                    reads=["pA%d_%d" % (par, hd), "cf"], writes=["ATm%d_%d" % (par, hd)])
                            while pend_epi:
                                pend_epi.pop(0)()

                    def back(i, chs):
                            par = i % 2
                            tb = i // 4
                            hs = lambda hd: slice(hd * 128, (hd + 1) * 128)
                            for ch in chs:
                                cs = slice(ch * 64, (ch + 1) * 64)
                                for hd in range(HB):
                                    if ch == 0:
                                        P.add("pe", lambda i=i, hd=hd, par=par: nc.tensor.matmul(
                                            pOb[par][:, hd, :], lhsT=ATm[par][:, hd, :], rhs=vv[:, i, hs(hd)],
                                            start=(hd == 0), stop=False, skip_group_check=True),
                                            reads=["ATm%d_%d" % (par, hd), "vv_%d" % i], writes=["pO_%d" % par])
                                    P.add("pe", lambda hd=hd, par=par, cs=cs, ch=ch: nc.tensor.matmul(
                                        pOb[par][:, hd, :], lhsT=qbz[par][:, hd, ch, :], rhs=Sbf[:, hd, :],
                                        start=False, stop=(ch == 1), skip_group_check=True),
                                        reads=["qbT%d_%d" % (par, hd), "Sbf_%d" % hd], writes=["pO_%d" % par])
                                    P.add("pe", lambda i=i, hd=hd, par=par, ch=ch: nc.tensor.matmul(
                                        pSb[hd], lhsT=kd[par][:, hd, ch, :], rhs=vv[:, i, hs(hd)],
                                        start=True, stop=True),
                                        reads=["kd%d_%d_%d" % (par, hd, ch), "vv_%d" % i], writes=["pS_%d" % hd])
                                for hd in range(HB):
                                    col = ch * 64 + 63
                                    P.add("dve", lambda hd=hd, par=par, col=col: nc.vector.scalar_tensor_tensor(
                                        out=Sbf[:, hd, :], in0=S32[:, hd, :], scalar=ebT[par][:, hd, col:col + 1],
                                        in1=pSb[hd], op0=ALU.mult, op1=ALU.add),
                                        reads=["S32_%d" % hd, "ebT%d_%d" % (par, hd), "pS_%d" % hd], writes=["Sbf_%d" % hd])
                                for hd in range(HB):
                                    col = ch * 64 + 63
                                    P.add("dve", lambda hd=hd, par=par, col=col: nc.vector.scalar_tensor_tensor(
                                        out=S32[:, hd, :], in0=S32[:, hd, :], scalar=ebT[par][:, hd, col:col + 1],
                                        in1=pSb[hd], op0=ALU.mult, op1=ALU.add),
                                        reads=["S32_%d" % hd, "ebT%d_%d" % (par, hd), "pS_%d" % hd], writes=["S32_%d" % hd])
                            if 1 not in chs:
                                return
                            def epi(i=i, par=par, h0=h0):
                                for hd in range(HB):
                                    P.add("act", lambda hd=hd, par=par: nc.scalar.activation(
                                        out=junk[:], in_=pOb[par][:, hd, :], func=AF.Square, accum_out=ssq[par][:, hd:hd + 1]),
                                        reads=["pO_%d" % par], writes=["junkC", "ssqC%d_%d" % (par, hd)])
                                rsqrt(rsd[par][:], ssq[par][:], 1.0 / 128, "ssqC%d" % par, "rsdC%d" % par,
                                      rkeys=["ssqC%d_%d" % (par, hd) for hd in range(HB)])
                                for hd in range(HB):
                                    P.add("dve", lambda hd=hd, par=par, h0=h0: nc.vector.scalar_tensor_tensor(
                                        out=ost[par][:, hs(hd)], in0=pOb[par][:, hd, :], scalar=rsd[par][:, hd:hd + 1],
                                        in1=hgb[:, (h0 + hd) * 128:(h0 + hd + 1) * 128], op0=ALU.mult, op1=ALU.mult),
                                        reads=["pO_%d" % par, "rsdC%d" % par, "hgb"], writes=["ost%d" % par])
                                P.add("sp", lambda i=i, par=par, h0=h0: nc.sync.dma_start(
                                    out=o_scr[i * 128:(i + 1) * 128, 1024 + h0 * 128:1024 + h0 * 128 + W], in_=ost[par][:]),
                                    reads=["ost%d" % par], writes=["o_scr"], dma=True)
                            pend_epi.append(epi)

                    front(0)
                    for i in range(NT):
                        back(i, (0,))
                        if i + 1 < NT:
                            front(i + 1)
                        back(i, (1,))
                    while pend_epi:
                        pend_epi.pop(0)()
        if "C" in stages:
            phase_C()
            P.barrier()

        mixT = st.enter_context(nc.sbuf_tensor("mixT", [128, KC, S], BF16))

        def phase_D():
            with ExitStack() as sdd:
                sbd = lambda name, shape, dt: sdd.enter_context(nc.sbuf_tensor(name, list(shape), dt))
                wz = [wz0, sbd("wz1", [128, KC, 512], BF16)]
                ot = [sbd("ot%d" % i, [128, 512], F32) for i in range(2)]
                sz = [sbd("sz%d" % i, [128, 512], F32) for i in range(2)]
                mx = [sbd("mx%d" % i, [128, 512], BF16) for i in range(2)]
                pz = [sdd.enter_context(nc.psum_tensor("pz%d" % i, [128, 512], F32)) for i in range(2)]
                pt = [sdd.enter_context(nc.psum_tensor("ptD%d" % i, [128, 4, 128], BF16)) for i in range(2)]
                zcols = [OFF_ZA, OFF_ZA + 512, OFF_ZH, OFF_ZH + 512]
                n = 0
                pend_tr = []
                for zb in range(4):
                    wb = zb % 2
                    for k in range(KC):
                        if zb == 0:
                            break
                        P.add("pool", lambda k=k, wb=wb, zb=zb: nc.gpsimd.dma_start(
                            out=wz[wb][:, k, :], in_=w_in[k * 128:(k + 1) * 128, zcols[zb]:zcols[zb] + 512]),
                            writes=["wz%d_%d" % (wb, k)], dma=True)
                    for i in range(NT):
                        b = n % 2
                        n += 1
                        P.add("sp", lambda i=i, b=b, zb=zb: nc.sync.dma_start(
                            out=ot[b][:], in_=o_scr[i * 128:(i + 1) * 128, zb * 512:(zb + 1) * 512]),
                            reads=["o_scr"], writes=["ot%d" % b], dma=True)
                        for k in range(KC):
                            P.add("pe", lambda i=i, b=b, k=k, wb=wb: nc.tensor.matmul(
                                pz[b][:], lhsT=hT[:, k, i * 128:(i + 1) * 128], rhs=wz[wb][:, k, :],
                                start=(k == 0), stop=(k == KC - 1)),
                                reads=["hT_%d" % i, "wz%d_%d" % (wb, k)], writes=["pz%d" % b])
                        while pend_tr:
                            pend_tr.pop(0)()
                        P.add("act", lambda b=b: nc.scalar.activation(out=sz[b][:], in_=pz[b][:], func=AF.Silu),
                              reads=["pz%d" % b], writes=["sz%d" % b])
                        P.add("dve", lambda b=b: nc.vector.tensor_tensor(out=mx[b][:], in0=sz[b][:], in1=ot[b][:],
                                                                         op=ALU.mult),
                              reads=["sz%d" % b, "ot%d" % b], writes=["mx%d" % b])
                        def tr(b=b, i=i, zb=zb):
                            for kk in range(4):
                                P.add("pe", lambda kk=kk: nc.tensor.transpose(
                                    out=pt[b][:, kk, :], in_=mx[b][:, kk * 128:(kk + 1) * 128], identity=ident[:]),
                                    reads=["mx%d" % b, "ident"], writes=["ptD%d" % b])
                            P.add("dve", lambda: nc.vector.tensor_copy(
                                out=mixT[:, zb * 4:(zb + 1) * 4, i * 128:(i + 1) * 128], in_=pt[b][:]),
                                reads=["ptD%d" % b], writes=["mixT_%d" % i])
                        pend_tr.append(tr)
                while pend_tr:
                    pend_tr.pop(0)()
        if "D" in stages:
            phase_D()
            P.barrier()

        def phase_F():
            with ExitStack() as sf:
                sbf = lambda name, shape, dt: sf.enter_context(nc.sbuf_tensor(name, list(shape), dt))
                wo = hT
                fg = sbf("fg", [128, D], F32)
                xt = [sbf("xtF%d" % i, [128, D], F32) for i in range(2)]
                rt = [sbf("rtF%d" % i, [128, D], F32) for i in range(2)]
                yo = [sbf("yoF%d" % i, [128, D], F32) for i in range(2)]
                junk = sbf("junkF", [128, D], BF16)
                ssq = [sbf("ssqF%d" % i, [128, 1], F32) for i in range(2)]
                rstd = [sbf("rstdF%d" % i, [128, 1], F32) for i in range(2)]
                py = [sf.enter_context(nc.psum_tensor("py%d" % i, [128, 512], F32)) for i in range(8)]
                for k in range(KC):
                    P.add("pool", lambda k=k: nc.gpsimd.dma_start(out=wo[:, k, :], in_=w_out[k * 128:(k + 1) * 128, :]),
                          writes=["wo_%d" % k], dma=True)
                P.add("sp", lambda: nc.sync.dma_start(out=fg[:], in_=final_norm.partition_broadcast(128)),
                      writes=["fg"], dma=True)
                for i in range(NT):
                    b = i % 2
                    P.add("sp", lambda i=i, b=b: nc.sync.dma_start(out=xt[b][:], in_=x[i * 128:(i + 1) * 128, :]),
                          writes=["xtF%d" % b], dma=True)
                    for k in range(KC):
                        for nb in range(4):
                            pb = b * 4 + nb
                            P.add("pe", lambda i=i, k=k, nb=nb, pb=pb: nc.tensor.matmul(
                                py[pb][:], lhsT=mixT[:, k, i * 128:(i + 1) * 128], rhs=wo[:, k, nb * 512:(nb + 1) * 512],
                                start=(k == 0), stop=(k == KC - 1)),
                                reads=["mixT_%d" % i, "wo_%d" % k], writes=["py%d" % pb])
                    for nb in range(4):
                        pb = b * 4 + nb
                        P.add("dve", lambda b=b, nb=nb, pb=pb: nc.vector.tensor_tensor(
                            out=rt[b][:, nb * 512:(nb + 1) * 512], in0=py[pb][:], in1=xt[b][:, nb * 512:(nb + 1) * 512],
                            op=ALU.add),
                            reads=["py%d" % pb, "xtF%d" % b], writes=["rtF%d_%d" % (b, nb)])
                    rkeys = ["rtF%d_%d" % (b, nb) for nb in range(4)]
                    P.add("act", lambda b=b: nc.scalar.activation(out=junk[:], in_=rt[b][:], func=AF.Square,
                                                                  accum_out=ssq[b][:]),
                          reads=rkeys, writes=["junkF", "ssqF%d" % b])
                    rsqrt(rstd[b][:], ssq[b][:], 1.0 / D, "ssqF%d" % b, "rstdF%d" % b)
                    P.add("dve", lambda b=b: nc.vector.scalar_tensor_tensor(
                        out=yo[b][:], in0=rt[b][:], scalar=rstd[b][:, 0:1], in1=fg[:], op0=ALU.mult, op1=ALU.mult),
                        reads=rkeys + ["rstdF%d" % b, "fg"], writes=["yoF%d" % b])
                    P.add("sp", lambda i=i, b=b: nc.sync.dma_start(out=out[i * 128:(i + 1) * 128, :], in_=yo[b][:]),
                          reads=["yoF%d" % b], dma=True)
        if "F" in stages:
            phase_F()
        nops, nwaits = P.emit(st)
    return nc, (nops, nwaits)


def make_consts():
    c = {}
    c["c_ident"] = bf(np.eye(128))
    blk = np.arange(128) // 64
    same = blk[:, None] == blk[None, :]
    ii = np.arange(128)
    U2 = (same & (ii[:, None] <= ii[None, :])).astype(np.float32)
    L2 = (same & (ii[:, None] > ii[None, :])).astype(np.float32)
    cb = np.zeros((128, 128), np.float32)
    cb[64:, 0] = -80.0
    cb[:64, 1] = -80.0
    cb[:64, 2] = 1.0
    cb[64:, 3] = 1.0
    c["c_f32"] = np.ascontiguousarray(np.stack([U2, L2, np.eye(128, dtype=np.float32), cb], axis=1))

    pos = np.arange(S)
    E = np.zeros((128, S), np.float32)
    E[pos // 64, pos] = -NEG
    c["c_E"] = bf(E)
    kl = np.arange(128)[:, None]
    tl = np.arange(128)[None, :]
    diag = np.where(kl <= tl, 0.0, NEG)
    far = np.where(tl < kl, 0.0, NEG)
    dd = np.stack([diag, far], axis=0)[:, None, :, :].repeat(4, axis=1)
    c["c_diag"] = bf(np.ascontiguousarray(dd.transpose(2, 0, 1, 3)))
    n = np.arange(128)[:, None, None]
    cch = np.arange(NT)[None, :, None]
    tt = np.arange(128)[None, None, :]
    c["c_cmask"] = bf(np.where(16 * n + 31 <= 128 * cch + tt, 0.0, NEG))
    cs_ = 16 * np.arange(127)[:, None]
    ss_ = 64 * np.arange(32)[None, :]
    ov = np.clip(np.minimum(cs_ + 32, ss_ + 64) - np.maximum(cs_, ss_), 0, None) / 32.0
    mm = np.zeros((128, 32), np.float32)
    mm[:127] = ov
    c["c_mmap"] = bf(mm)
    t_abs = (128 * np.arange(NT)[None, :, None] + np.arange(128)[:, None, None])
    j = np.arange(32)[None, None, :]
    cur = t_abs // 64
    forced = ((j == 0) | (j == cur) | (j == cur - 1)).astype(np.float32)
    future = (j * 64 > t_abs).astype(np.float32)
    c["c_sel"] = np.ascontiguousarray(np.stack([1.0 - future, 1e4 * forced * (1.0 - future) - future], axis=2).astype(np.float32))

    def split_rows(p):
        a = (p // 128) * 128
        b_ = p % 128
        one = np.ones_like(p)
        return np.stack([a, a, b_, b_, one, one, one], axis=0).astype(np.float32)
    c["c_kaug"] = bf(split_rows(pos))
    c["c_kcaug"] = bf(split_rows(16 * np.arange(127) + 31))
    qa = np.zeros((4, 7, 4, S), np.float32)
    for g in range(4):
        for r in range(4):
            sl = np.float32(2.0 ** (-(4 * g + r + 1) / 2.0))
            s_hi = np.float32(bf(sl))
            s_lo = np.float32(bf(sl - s_hi))
            sp_ = np.float64(s_hi) + np.float64(s_lo)
            st = sp_ * pos.astype(np.float64)
            st1 = bf(st).astype(np.float64)
            st2 = bf(st - st1).astype(np.float64)
            st3 = bf(st - st1 - st2).astype(np.float64)
            qa[g, 0, r] = s_hi; qa[g, 1, r] = s_lo; qa[g, 2, r] = s_hi; qa[g, 3, r] = s_lo
            qa[g, 4, r] = -st1; qa[g, 5, r] = -st2; qa[g, 6, r] = -st3
    c["c_qaug"] = bf(np.ascontiguousarray(qa.reshape(4, 7, 4, NT, 128).transpose(0, 1, 3, 2, 4)))
    return c


_CACHE = {}


def kernel(**inputs):
    if "nc" not in _CACHE:
        _CACHE["nc"] = build_program()[0]
    nc = _CACHE["nc"]
    consts = make_consts()
    x = np.asarray(inputs["x"], dtype=np.float32)
    B = x.shape[0]
    shared = {
        "norm_in": np.ascontiguousarray(np.asarray(inputs["norm_in"], np.float32)[0]),
        "w_in": np.ascontiguousarray(np.asarray(inputs["w_in"], np.float32)[0]),
        "w_out": np.ascontiguousarray(np.asarray(inputs["w_out"], np.float32)[0]),
        "final_norm": np.ascontiguousarray(np.asarray(inputs["final_norm"], np.float32)),
        "nsa_out_norm": np.ascontiguousarray(np.asarray(inputs["nsa_out_norm"], np.float32)[0]),
        "hgrn_out_norm": np.ascontiguousarray(np.asarray(inputs["hgrn_out_norm"], np.float32)[0]),
        "lower_bounds": np.ascontiguousarray(np.asarray(inputs["lower_bounds"], np.float32)),
    }
    for nm in ("cmp_pe_k", "cmp_pe_v", "cmp_w1_k", "cmp_w1_v", "cmp_w2_k", "cmp_w2_v"):
        shared[nm] = np.ascontiguousarray(np.asarray(inputs[nm], np.float32)[0])
    shared.update(consts)
    in_maps = []
    for b in range(B):
        m = dict(shared)
        m["x"] = np.ascontiguousarray(x[b])
        in_maps.append(m)
    res = run_bass_kernel_spmd(nc, in_maps, core_ids=list(range(B)))
    return np.stack([np.asarray(r["out"], dtype=np.float32) for r in res.results], axis=0)
```

```python
import numpy as np
import ml_dtypes
from contextlib import ExitStack
import concourse.bass as bass
import concourse.mybir as mybir
from concourse.bass_utils import run_bass_kernel_spmd

F32 = mybir.dt.float32
BF16 = mybir.dt.bfloat16
AF = mybir.ActivationFunctionType
ALU = mybir.AluOpType
AX = mybir.AxisListType

S = 2048
D = 2048
DIN = 7728
NT = 16
KC = 16
EPS = 1e-6
OFF_QA, OFF_KC, OFF_VC, OFF_KS, OFF_VS, OFF_KW, OFF_VW = 0, 1024, 1280, 1536, 1792, 2048, 2304
OFF_GATE, OFF_ZA, OFF_QH, OFF_FH, OFF_IH, OFF_ZH = 2560, 2608, 3632, 4656, 5680, 6704
NEG = -30000.0


class Op:
    __slots__ = ("eng", "fn", "dma", "deps", "signal", "count", "slot", "target", "prev_slot_target")

    def __init__(self, eng, fn, dma):
        self.eng, self.fn, self.dma = eng, fn, dma
        self.deps = set()
        self.signal = False
        self.count = None
        self.slot = None
        self.target = None
        self.prev_slot_target = 0


class Prog:
    def __init__(self, nc, n_dma_slots=8):
        self.nc = nc
        self.ops = []
        self.last_writer = {}
        self.readers = {}
        self.n_dma_slots = n_dma_slots
        self.last_on_eng = {}
        self.dma_since_barrier = []
        self.barrier_set = None
        self.barrier_id = 0
        self.barrier_seen = {}

    def add(self, eng, fn, reads=(), writes=(), dma=False):
        i = len(self.ops)
        op = Op(eng, fn, dma)
        deps = {}
        for k in reads:
            j = self.last_writer.get(k)
            if j is not None:
                deps[j] = True
        for k in writes:
            j = self.last_writer.get(k)
            if j is not None:
                deps[j] = True
            for r in self.readers.get(k, ()):
                deps.setdefault(r, False)
        if self.barrier_set is not None and self.barrier_seen.get(eng, -1) < self.barrier_id:
            for j in self.barrier_set:
                deps.setdefault(j, True)
            self.barrier_seen[eng] = self.barrier_id
        for j, hard in deps.items():
            oj = self.ops[j]
            if not oj.dma and oj.eng == eng:
                if eng == "pe":
                    continue
            op.deps.add(j)
            if not oj.dma:
                oj.signal = True
        for k in reads:
            self.readers.setdefault(k, []).append(i)
        for k in writes:
            self.last_writer[k] = i
            self.readers[k] = []
        self.ops.append(op)
        self.last_on_eng[eng] = i
        if dma:
            self.dma_since_barrier.append(i)
        return i

    def barrier(self):
        b = set(self.last_on_eng.values()) | set(self.dma_since_barrier)
        if self.barrier_set is not None and any(
                self.barrier_seen.get(e, -1) < self.barrier_id for e in ("pe", "act", "dve", "pool", "sp")):
            b |= self.barrier_set
        self.barrier_set = b
        self.barrier_id += 1
        self.dma_since_barrier = []

    def emit(self, stack):
        nc = self.nc
        engs = {"pe": nc.tensor, "act": nc.scalar, "dve": nc.vector, "pool": nc.gpsimd, "sp": nc.sync}
        esem = {e: stack.enter_context(nc.semaphore("sem_" + e)) for e in engs}
        dsem = {}
        for q in ("sp", "pool", "act"):
            dsem[q] = [stack.enter_context(nc.semaphore("dsem_%s_%d" % (q, s))) for s in range(self.n_dma_slots)]
        cnt = {e: 0 for e in engs}
        dcount = {q: 0 for q in dsem}
        slot_total = {q: [0] * self.n_dma_slots for q in dsem}
        for op in self.ops:
            if op.dma:
                q = op.eng
                s = dcount[q] % self.n_dma_slots
                dcount[q] += 1
                op.slot = s
                op.prev_slot_target = slot_total[q][s]
                slot_total[q][s] += 16
                op.target = slot_total[q][s]
            elif op.signal:
                cnt[op.eng] += 1
                op.count = cnt[op.eng]
        seen = {e: {} for e in engs}
        nwaits = 0
        for op in self.ops:
            e = engs[op.eng]
            waits = []
            for j in op.deps:
                oj = self.ops[j]
                if oj.dma:
                    waits.append((("d", oj.eng, oj.slot), dsem[oj.eng][oj.slot], oj.target))
                else:
                    waits.append((("e", oj.eng), esem[oj.eng], oj.count))
            if op.dma and op.prev_slot_target > 0:
                waits.append((("d", op.eng, op.slot), dsem[op.eng][op.slot], op.prev_slot_target))
            best = {}
            for key, sem, val in waits:
                if val > seen[op.eng].get(key, 0) and val > best.get(key, (None, 0))[1]:
                    best[key] = (sem, val)
            for key, (sem, val) in best.items():
                e.wait_ge(sem, val)
                seen[op.eng][key] = val
                nwaits += 1
            inst = op.fn()
            if op.dma:
                inst.then_inc(dsem[op.eng][op.slot], 16)
            elif op.signal:
                inst.then_inc(esem[op.eng], 1)
        sp = nc.sync
        for q in dsem:
            for s in range(self.n_dma_slots):
                if slot_total[q][s] > seen["sp"].get(("d", q, s), 0):
                    sp.wait_ge(dsem[q][s], slot_total[q][s])
        return len(self.ops), nwaits


def bf(a):
    return np.asarray(a, dtype=np.float32).astype(ml_dtypes.bfloat16)


def build_program(stages=("A", "B", "C", "D", "F"), debug=()):
    nc = bass.Bass("TRN2", target_bir_lowering=False)
    P = Prog(nc)
    dt_in = lambda name, shape, dt=F32: nc.dram_tensor(name, list(shape), dt, kind="ExternalInput").ap()
    x = dt_in("x", [S, D])
    norm_in = dt_in("norm_in", [D])
    w_in = dt_in("w_in", [D, DIN])
    w_out = dt_in("w_out", [D, D])
    final_norm = dt_in("final_norm", [D])
    nsa_gain = dt_in("nsa_out_norm", [1024])
    hgrn_gain = dt_in("hgrn_out_norm", [1024])
    ident_d = dt_in("c_ident", [128, 128], BF16)
    lower_bounds = dt_in("lower_bounds", [2, 1024])
    cmp_pe_k = dt_in("cmp_pe_k", [32, 64]); cmp_pe_v = dt_in("cmp_pe_v", [32, 64])
    cmp_w1_k = dt_in("cmp_w1_k", [2048, 256]); cmp_w1_v = dt_in("cmp_w1_v", [2048, 256])
    cmp_w2_k = dt_in("cmp_w2_k", [256, 64]); cmp_w2_v = dt_in("cmp_w2_v", [256, 64])
    c_E = dt_in("c_E", [128, S], BF16)
    c_diag = dt_in("c_diag", [128, 2, 4, 128], BF16)
    c_cmask = dt_in("c_cmask", [128, NT, 128], BF16)
    c_mmap = dt_in("c_mmap", [128, 32], BF16)
    c_sel = dt_in("c_sel", [128, NT, 2, 32])
    c_kaug = dt_in("c_kaug", [7, S], BF16)
    c_kcaug = dt_in("c_kcaug", [7, 127], BF16)
    c_qaug = dt_in("c_qaug", [4, 7, NT, 4, 128], BF16)
    c_f32 = dt_in("c_f32", [128, 4, 128])
    out = nc.dram_tensor("out", [S, D], F32, kind="ExternalOutput").ap()
    o_scr = nc.dram_tensor("o_scr", [S, D], F32,
                           kind="ExternalOutput" if any(n == "o_scr" for n, _ in debug) else "Internal").ap()
    dbg = {}
    for name, shape in debug:
        if name == "o_scr":
            continue
        dbg[name] = nc.dram_tensor("dbg_" + name, list(shape), F32, kind="ExternalOutput").ap()

    with ExitStack() as st:
        sb = lambda name, shape, dt: st.enter_context(nc.sbuf_tensor(name, list(shape), dt))
        ps = lambda name, shape, dt: st.enter_context(nc.psum_tensor(name, list(shape), dt))

        epsb = sb("epsb", [128, 1], F32)
        P.add("dve", lambda: nc.vector.memset(epsb[:], EPS), writes=["epsb"])

        def rsqrt(o, i, scale, rkey, wkey, rkeys=None):
            np_ = o.shape[0]
            P.add("act", lambda: nc.scalar.activation(out=o, in_=i, func=AF.Ln, bias=epsb[0:np_, 0:1], scale=scale),
                  reads=(rkeys or [rkey]) + ["epsb"], writes=[wkey])
            P.add("act", lambda: nc.scalar.activation(out=o, in_=o, func=AF.Exp, scale=-0.5),
                  reads=[wkey], writes=[wkey])

        hT = sb("hT", [128, KC, S], BF16)
        wz0 = sb("wz0", [128, KC, 512], BF16)
        if "D" in stages:
            for k in range(KC):
                P.add("pool", lambda k=k: nc.gpsimd.dma_start(
                    out=wz0[:, k, :], in_=w_in[k * 128:(k + 1) * 128, OFF_ZA:OFF_ZA + 512]),
                    writes=["wz0_%d" % k], dma=True)
        ident = sb("ident", [128, 128], BF16)
        gin = sb("gin", [128, KC], F32)
        P.add("sp", lambda: nc.sync.dma_start(out=ident[:], in_=ident_d), writes=["ident"], dma=True)
        P.add("sp", lambda: nc.sync.dma_start(out=gin[:], in_=norm_in.rearrange("(k p) -> p k", p=128),
                                              allow_slow_non_contiguous=True),
              writes=["gin"], dma=True)

        def phase_A():
            with ExitStack() as sa:
                sba = lambda name, shape, dt: sa.enter_context(nc.sbuf_tensor(name, list(shape), dt))
                xt = [sba("xt%d" % i, [128, D], F32) for i in range(2)]
                xn = [sba("xn%d" % i, [128, D], BF16) for i in range(2)]
                junk = sba("junkA", [128, D], BF16)
                ssq = [sba("ssq%d" % i, [128, 1], F32) for i in range(2)]
                rstd = [sba("rstd%d" % i, [128, 1], F32) for i in range(2)]
                pt = [sa.enter_context(nc.psum_tensor("ptA%d" % i, [128, 8, 128], BF16)) for i in range(2)]
                for i in range(NT):
                    b = i % 2
                    P.add("sp", lambda i=i, b=b: nc.sync.dma_start(out=xt[b][:], in_=x[i * 128:(i + 1) * 128, :]),
                          writes=["xt%d" % b], dma=True)
                    P.add("act", lambda b=b: nc.scalar.activation(out=junk[:], in_=xt[b][:], func=AF.Square,
                                                                  accum_out=ssq[b][:]),
                          reads=["xt%d" % b], writes=["junkA", "ssq%d" % b])
                    rsqrt(rstd[b][:], ssq[b][:], 1.0 / D, "ssq%d" % b, "rstd%d" % b)
                    P.add("dve", lambda b=b: nc.vector.tensor_scalar(out=xn[b][:], in0=xt[b][:], scalar1=rstd[b][:, 0:1],
                                                                     scalar2=None, op0=ALU.mult),
                          reads=["xt%d" % b, "rstd%d" % b], writes=["xn%d" % b])
                    for half in range(2):
                        for kk in range(8):
                            k = half * 8 + kk
                            P.add("pe", lambda b=b, k=k, kk=kk, half=half: nc.tensor.transpose(
                                out=pt[half][:, kk, :], in_=xn[b][:, k * 128:(k + 1) * 128], identity=ident[:]),
                                reads=["xn%d" % b, "ident"], writes=["ptA%d" % half])
                        P.add("dve", lambda i=i, half=half: nc.vector.tensor_tensor(
                            out=hT[:, half * 8:(half + 1) * 8, i * 128:(i + 1) * 128], in0=pt[half][:],
                            in1=gin[:, half * 8:(half + 1) * 8].unsqueeze(2).to_broadcast([128, 8, 128]), op=ALU.mult),
                            reads=["ptA%d" % half, "gin"], writes=["hT_%d" % i])
        if "A" in stages:
            phase_A()
            P.barrier()
            if "hT" in dbg:
                with nc.sbuf_tensor("dbg_hT_sb", [128, KC, S], F32) as dsb:
                    P.add("dve", lambda: nc.vector.tensor_copy(out=dsb[:], in_=hT[:]),
                          reads=["hT_%d" % i for i in range(NT)], writes=["dsb"])
                    P.add("sp", lambda: nc.sync.dma_start(out=dbg["hT"].rearrange("(k p) t -> p k t", p=128), in_=dsb[:]),
                          reads=["dsb"], dma=True)
                    P.barrier()
        hT_keys = ["hT_%d" % i for i in range(NT)]

        def phase_O1():
            with nc.sbuf_tensor("ones_dbg", [128, D], F32) as ones_dbg:
                P.add("dve", lambda: nc.vector.memset(ones_dbg[:], 1.0), writes=["ones_dbg"])
                for i in range(NT):
                    P.add("sp", lambda i=i: nc.sync.dma_start(out=o_scr[i * 128:(i + 1) * 128, :], in_=ones_dbg[:]),
                          reads=["ones_dbg"], writes=["o_scr"], dma=True)
                P.barrier()
        if "O1" in stages:
            phase_O1()


        def phase_B():
            with ExitStack() as sB:
                sbb = lambda name, shape, dt: sB.enter_context(nc.sbuf_tensor(name, list(shape), dt))
                kT_aug = sbb("kT_aug", [128, 2, 4, S], BF16)
                v_aug = sbb("v_aug", [128, NT, 2, 4, 65], BF16)
                gate_sb = sbb("gate_sb", [128, NT, 48], F32)
                kcT_aug = sbb("kcT_aug", [128, 4, 127], BF16)
                vc_aug = sbb("vc_aug", [128, 4, 65], BF16)
                P.add("dve", lambda: nc.vector.memset(kT_aug[64:128, :, :, :], 0.0), writes=["kTaug_init"])
                P.add("dve", lambda: nc.vector.memset(kcT_aug[64:128, :, :], 0.0), writes=["kcTaug_init"])
                for xx in range(2):
                    for g in range(4):
                        P.add("sp", lambda xx=xx, g=g: nc.sync.dma_start(out=kT_aug[64:71, xx, g, :], in_=c_kaug),
                              reads=["kTaug_init"], writes=["kTaug_rows"], dma=True)
                        if xx == 0:
                            P.add("sp", lambda g=g: nc.sync.dma_start(out=kT_aug[96:128, 0, g, :], in_=c_E[0:32, :]),
                                  reads=["kTaug_init"], writes=["kTaug_rows"], dma=True)
                for g in range(4):
                    P.add("sp", lambda g=g: nc.sync.dma_start(out=kcT_aug[64:71, g, :], in_=c_kcaug),
                          reads=["kcTaug_init"], writes=["kcTaug_rows"], dma=True)
                P.add("dve", lambda: nc.vector.memset(v_aug[:], 1.0), writes=["v_aug_init"])
                P.add("dve", lambda: nc.vector.memset(vc_aug[:], 1.0), writes=["vc_aug_init"])

                with ExitStack() as s0:
                    sb0 = lambda name, shape, dt: s0.enter_context(nc.sbuf_tensor(name, list(shape), dt))
                    ps0 = lambda name, shape, dt: s0.enter_context(nc.psum_tensor(name, list(shape), dt))
                    slabB = sb0("slabB", [128, KC, 2, 2, 256], BF16)
                    wg = sb0("wg", [128, KC, 48], BF16)
                    kstg = [sb0("kstg%d" % i, [128, 512], BF16) for i in range(2)]
                    pp = [ps0("ppB%d" % i, [128, 512], F32) for i in range(2)]
                    pg = ps0("pgB", [128, 512], F32)
                    npj = 0
                    for k in range(KC):
                        P.add("pool", lambda k=k: nc.gpsimd.dma_start(
                            out=slabB[:, k, :, :, :].rearrange("p a b c -> p (a b c)"),
                            in_=w_in[k * 128:(k + 1) * 128, OFF_KS:OFF_KS + 1024]),
                            writes=["slabB_%d" % k], dma=True)
                        P.add("pool", lambda k=k: nc.gpsimd.dma_start(
                            out=wg[:, k, :], in_=w_in[k * 128:(k + 1) * 128, OFF_GATE:OFF_GATE + 48]),
                            writes=["wg_%d" % k], dma=True)
                    for xx in range(2):
                        for gp in range(2):
                            for tb in range(4):
                                b = npj % 2
                                npj += 1
                                for k in range(KC):
                                    P.add("pe", lambda k=k, xx=xx, gp=gp, tb=tb, b=b: nc.tensor.matmul(
                                        pp[b][:], lhsT=slabB[:, k, xx, 0, gp * 128:(gp + 1) * 128],
                                        rhs=hT[:, k, tb * 512:(tb + 1) * 512], start=(k == 0), stop=(k == KC - 1)),
                                        reads=["slabB_%d" % k] + hT_keys[tb * 4:tb * 4 + 4], writes=["ppB%d" % b])
                                P.add("act", lambda xx=xx, gp=gp, tb=tb, b=b: nc.scalar.copy(
                                    out=kT_aug[0:64, xx, 2 * gp, tb * 512:(tb + 1) * 512], in_=pp[b][0:64, :]),
                                    reads=["ppB%d" % b], writes=["kT_%d_%d" % (xx, 2 * gp)])
                                P.add("act", lambda b=b: nc.scalar.copy(out=kstg[b][64:128, :], in_=pp[b][64:128, :]),
                                      reads=["ppB%d" % b], writes=["kstg%d" % b])
                                P.add("sp", lambda xx=xx, gp=gp, tb=tb, b=b: nc.sync.dma_start(
                                    out=kT_aug[0:64, xx, 2 * gp + 1, tb * 512:(tb + 1) * 512], in_=kstg[b][64:128, :]),
                                    reads=["kstg%d" % b], writes=["kT_%d_%d" % (xx, 2 * gp + 1)], dma=True)
                    for i in range(NT):
                        b = npj % 2
                        npj += 1
                        for k in range(KC):
                            P.add("pe", lambda k=k, i=i, b=b: nc.tensor.matmul(
                                pp[b][:], lhsT=hT[:, k, i * 128:(i + 1) * 128], rhs=slabB[:, k, :, 1, :],
                                start=(k == 0), stop=(k == KC - 1)),
                                reads=["slabB_%d" % k, "hT_%d" % i], writes=["ppB%d" % b])
                        for k in range(KC):
                            P.add("pe", lambda k=k, i=i: nc.tensor.matmul(
                                pg[:, 0:48], lhsT=hT[:, k, i * 128:(i + 1) * 128], rhs=wg[:, k, :],
                                start=(k == 0), stop=(k == KC - 1)),
                                reads=["wg_%d" % k, "hT_%d" % i], writes=["pgB"])
                        for xx in range(2):
                            P.add("act", lambda xx=xx, i=i, b=b: nc.scalar.copy(
                                out=v_aug[:, i, xx, :, 0:64],
                                in_=pp[b][:, xx * 256:(xx + 1) * 256].rearrange("p (g d) -> p g d", d=64)),
                                reads=["ppB%d" % b, "v_aug_init"], writes=["v_aug_%d" % i])
                        P.add("act", lambda i=i: nc.scalar.activation(out=gate_sb[:, i, :], in_=pg[:, 0:48], func=AF.Exp,
                                                                      scale=-1.0),
                              reads=["pgB"], writes=["gate_%d" % i])
                    gkeys = ["gate_%d" % i for i in range(NT)]
                    P.add("dve", lambda: nc.vector.tensor_scalar(out=gate_sb[:], in0=gate_sb[:], scalar1=1.0, scalar2=None,
                                                                 op0=ALU.add), reads=gkeys, writes=gkeys)
                    P.add("dve", lambda: nc.vector.reciprocal(out=gate_sb[:], in_=gate_sb[:]), reads=gkeys, writes=gkeys)

                P.barrier()
                with ExitStack() as s0:
                    sb0 = lambda name, shape, dt: s0.enter_context(nc.sbuf_tensor(name, list(shape), dt))
                    ps0 = lambda name, shape, dt: s0.enter_context(nc.psum_tensor(name, list(shape), dt))
                    slabA = sb0("slabA", [128, KC, 512], BF16)
                    cmpT = sb0("cmpT", [128, 4, S], BF16)
                    w1 = [sb0("w1_%d" % kv, [128, 32, 256], BF16) for kv in range(2)]
                    w2 = [sb0("w2_%d" % kv, [128, 2, 64], BF16) for kv in range(2)]
                    peT = [sb0("peT_%d" % kv, [64, 32], BF16) for kv in range(2)]
                    bias_sb = sb0("bias_sb", [128, 4], F32)
                    hact = [sb0("hact%d" % i, [128, 2, 127], BF16) for i in range(2)]
                    gx = [sb0("gx%d" % i, [128, 127], F32) for i in range(2)]
                    gu = [sb0("gu%d" % i, [128, 127], F32) for i in range(2)]
                    ppc = [ps0("ppc%d" % i, [128, 512], F32) for i in range(2)]
                    ph = [ps0("phB%d" % i, [128, 512], F32) for i in range(2)]
                    pb_ = ps0("pbB", [128, 512], F32)
                    pc = ps0("pcB", [128, 512], F32)
                    npj = 0
                    for k in range(KC):
                        P.add("pool", lambda k=k: nc.gpsimd.dma_start(
                            out=slabA[:, k, :], in_=w_in[k * 128:(k + 1) * 128, OFF_KC:OFF_KC + 512]),
                            writes=["slabA_%d" % k], dma=True)
                    w1d = [cmp_w1_k, cmp_w1_v]
                    w2d = [cmp_w2_k, cmp_w2_v]
                    ped = [cmp_pe_k, cmp_pe_v]
                    for kv in range(2):
                        for half in range(2):
                            for lq in range(4):
                                P.add("pool", lambda kv=kv, half=half, lq=lq: nc.gpsimd.dma_start(
                                    out=w1[kv][half * 64:(half + 1) * 64, lq * 8:(lq + 1) * 8, :],
                                    in_=w1d[kv].rearrange("(l d) h -> d l h", d=64)[:, lq * 8:(lq + 1) * 8, :]),
                                    writes=["w1_%d" % kv], dma=True)
                        P.add("pool", lambda kv=kv: nc.gpsimd.dma_start(
                            out=w2[kv][:], in_=w2d[kv].rearrange("(c p) d -> p c d", p=128)),
                            writes=["w2_%d" % kv], dma=True)
                        P.add("pool", lambda kv=kv: nc.gpsimd.dma_start(
                            out=peT[kv][:], in_=ped[kv].rearrange("l d -> d l"), allow_slow_non_contiguous=True),
                            writes=["peT_%d" % kv], dma=True)
                    for cc in range(4):
                        for tb in range(4):
                            b = npj % 2
                            npj += 1
                            for k in range(KC):
                                P.add("pe", lambda k=k, cc=cc, tb=tb, b=b: nc.tensor.matmul(
                                    ppc[b][:], lhsT=slabA[:, k, cc * 128:(cc + 1) * 128], rhs=hT[:, k, tb * 512:(tb + 1) * 512],
                                    start=(k == 0), stop=(k == KC - 1)),
                                    reads=["slabA_%d" % k] + hT_keys[tb * 4:tb * 4 + 4], writes=["ppc%d" % b])
                            P.add("act", lambda cc=cc, tb=tb, b=b: nc.scalar.copy(
                                out=cmpT[:, cc, tb * 512:(tb + 1) * 512], in_=ppc[b][:]),
                                reads=["ppc%d" % b], writes=["cmpT_%d" % cc])
                    for kv in range(2):
                        for hc in range(2):
                            col = kv * 2 + hc
                            for l in range(32):
                                P.add("pe", lambda kv=kv, hc=hc, l=l, col=col: nc.tensor.matmul(
                                    pb_[:, col:col + 1], lhsT=w1[kv][0:64, l, hc * 128:(hc + 1) * 128], rhs=peT[kv][:, l:l + 1],
                                    start=(l == 0), stop=(l == 31), skip_group_check=True),
                                    reads=["w1_%d" % kv, "peT_%d" % kv], writes=["pbB"])
                    P.add("dve", lambda: nc.vector.tensor_copy(out=bias_sb[:], in_=pb_[:, 0:4]), reads=["pbB"], writes=["bias_sb"])
                    nh = 0
                    for kv in range(2):
                        for g in range(4):
                            base = 64 * (g % 2)
                            cc = kv * 2 + g // 2
                            hb = nh % 2
                            nh += 1
                            for hc in range(2):
                                for l in range(32):
                                    P.add("pe", lambda kv=kv, hc=hc, l=l, base=base, cc=cc, hb=hb: nc.tensor.matmul(
                                        ph[hb][:, hc * 128:hc * 128 + 127],
                                        lhsT=w1[kv][base:base + 64, l, hc * 128:(hc + 1) * 128],
                                        rhs=cmpT[base:base + 64, cc, l:l + 2017:16],
                                        start=(l == 0 and hc == 0), stop=(l == 31), skip_group_check=True),
                                        reads=["w1_%d" % kv, "cmpT_%d" % cc], writes=["phB%d" % hb])
                            for hc in range(2):
                                gb = hc
                                col = kv * 2 + hc
                                P.add("dve", lambda hb=hb, hc=hc, gb=gb, col=col: nc.vector.tensor_scalar(
                                    out=gx[gb][:], in0=ph[hb][:, hc * 128:hc * 128 + 127], scalar1=bias_sb[:, col:col + 1],
                                    scalar2=None, op0=ALU.add),
                                    reads=["phB%d" % hb, "bias_sb"], writes=["gx%d" % gb])
                                P.add("dve", lambda gb=gb: nc.vector.tensor_tensor(out=gu[gb][:], in0=gx[gb][:], in1=gx[gb][:],
                                                                                   op=ALU.mult),
                                      reads=["gx%d" % gb], writes=["gu%d" % gb])
                                P.add("dve", lambda gb=gb: nc.vector.tensor_scalar(out=gu[gb][:], in0=gu[gb][:], scalar1=0.044715,
                                                                                   scalar2=1.0, op0=ALU.mult, op1=ALU.add),
                                      reads=["gu%d" % gb], writes=["gu%d" % gb])
                                P.add("dve", lambda gb=gb: nc.vector.tensor_tensor(out=gu[gb][:], in0=gu[gb][:], in1=gx[gb][:],
                                                                                   op=ALU.mult),
                                      reads=["gu%d" % gb, "gx%d" % gb], writes=["gu%d" % gb])
                                P.add("act", lambda gb=gb: nc.scalar.activation(out=gu[gb][:], in_=gu[gb][:], func=AF.Exp,
                                                                                scale=-1.5957691216057308),
                                      reads=["gu%d" % gb], writes=["gu%d" % gb])
                                P.add("dve", lambda gb=gb: nc.vector.tensor_scalar(out=gu[gb][:], in0=gu[gb][:], scalar1=1.0,
                                                                                   scalar2=None, op0=ALU.add),
                                      reads=["gu%d" % gb], writes=["gu%d" % gb])
                                P.add("dve", lambda gb=gb: nc.vector.reciprocal(out=gu[gb][:], in_=gu[gb][:]),
                                      reads=["gu%d" % gb], writes=["gu%d" % gb])
                                P.add("dve", lambda gb=gb, hb=hb, hc=hc: nc.vector.tensor_tensor(
                                    out=hact[hb][:, hc, :], in0=gx[gb][:], in1=gu[gb][:], op=ALU.mult),
                                    reads=["gx%d" % gb, "gu%d" % gb], writes=["hact%d_%d" % (hb, hc)])
                            hkeys = ["hact%d_%d" % (hb, hc) for hc in range(2)]
                            if kv == 0:
                                for hc in range(2):
                                    P.add("pe", lambda hb=hb, hc=hc: nc.tensor.matmul(
                                        pc[0:64, 0:127], lhsT=w2[0][:, hc, :], rhs=hact[hb][:, hc, :],
                                        start=(hc == 0), stop=(hc == 1)),
                                        reads=hkeys + ["w2_0"], writes=["pcB"])
                                P.add("act", lambda g=g: nc.scalar.copy(out=kcT_aug[0:64, g, :], in_=pc[0:64, 0:127]),
                                      reads=["pcB"], writes=["kcT_%d" % g])
                            else:
                                for hc in range(2):
                                    P.add("pe", lambda hb=hb, hc=hc: nc.tensor.matmul(
                                        pc[0:127, 0:64], lhsT=hact[hb][:, hc, :], rhs=w2[1][:, hc, :],
                                        start=(hc == 0), stop=(hc == 1)),
                                        reads=hkeys + ["w2_1"], writes=["pcB"])
                                P.add("act", lambda g=g: nc.scalar.copy(out=vc_aug[0:127, g, 0:64], in_=pc[0:127, 0:64]),
                                      reads=["pcB", "vc_aug_init"], writes=["vc_%d" % g])
                P.barrier()
                if "kc" in dbg:
                    with nc.sbuf_tensor("dbg_kc_sb", [128, 2, 4, 127], F32) as dkc:
                        P.add("dve", lambda: nc.vector.memset(dkc[:], 0.0), writes=["dkc"])
                        P.add("dve", lambda: nc.vector.tensor_copy(out=dkc[0:71, 0, :, :], in_=kcT_aug[0:71, :, :]), writes=["dkc"])
                        P.add("dve", lambda: nc.vector.tensor_copy(out=dkc[:, 1, :, 0:65], in_=vc_aug[:]), writes=["dkc"])
                        P.add("sp", lambda: nc.sync.dma_start(out=dbg["kc"], in_=dkc[:]), reads=["dkc"], dma=True)
                        P.barrier()

                if "noB2" in stages:
                    return
                with ExitStack() as s2:
                    sb2 = lambda name, shape, dt: s2.enter_context(nc.sbuf_tensor(name, list(shape), dt))
                    ps2 = lambda name, shape, dt: s2.enter_context(nc.psum_tensor(name, list(shape), dt))
                    wqs = sb2("wqs", [128, KC, 256], BF16)
                    cdiag = sb2("cdiag", [128, 2, 4, 128], BF16)
                    ccm = sb2("ccm", [128, NT, 128], BF16)
                    cmm = sb2("cmm", [128, 32], BF16)
                    csel = sb2("csel", [128, NT, 2, 32], F32)
                    ngb = sb2("ngb", [128, 1024], F32)
                    tiny = sb2("tiny", [128, 1], F32)
                    P.add("sp", lambda: nc.sync.dma_start(out=cdiag[:], in_=c_diag), writes=["cdiag"], dma=True)
                    P.add("sp", lambda: nc.sync.dma_start(out=ccm[:], in_=c_cmask), writes=["ccm"], dma=True)
                    P.add("sp", lambda: nc.sync.dma_start(out=cmm[:], in_=c_mmap), writes=["cmm"], dma=True)
                    P.add("sp", lambda: nc.sync.dma_start(out=csel[:], in_=c_sel), writes=["csel"], dma=True)
                    P.add("sp", lambda: nc.sync.dma_start(out=ngb[:], in_=nsa_gain.partition_broadcast(128)),
                          writes=["ngb"], dma=True)
                    P.add("dve", lambda: nc.vector.memset(tiny[:], 1e-30), writes=["tiny"])
                    qTa = [sb2("qT_aug%d" % i, [128, NT, 4, 128], BF16) for i in range(2)]
                    PT = [sb2("PT%d" % i, [128, 512], BF16) for i in range(3)]
                    negselT = [sb2("negsel%d" % i, [128, 128], F32) for i in range(2)]
                    coef = sb2("coef", [128, 3, 4], F32)
                    coefC = sb2("coefC", [128, 4], F32)
                    pslc = sb2("pslc", [128, 32], F32)
                    score = sb2("score", [128, 32], F32)
                    top8 = sb2("top8", [128, 8], F32)
                    oacc = [sb2("oacc%d" % i, [128, 4, 64], F32) for i in range(3)]
                    rinvE = [sb2("rinvE%d" % i, [128, 3, 4], F32) for i in range(3)]
                    otmp = sb2("otmp", [128, 4, 64], F32)
                    ssq = [sb2("ssqB%d" % i, [128, 4], F32) for i in range(3)]
                    rsd = [sb2("rsdB%d" % i, [128, 4], F32) for i in range(3)]
                    ostage = [sb2("ostB%d" % i, [128, 256], F32) for i in range(3)]
                    idf = sb2("idfB", [128, 128], F32)
                    qstg = [sb2("qstg%d" % i, [128, 512], BF16) for i in range(2)]
                    pS = [ps2("pS%d" % i, [128, 512], F32) for i in range(3)]
                    pOc = ps2("pOc0", [128, 512], F32)
                    pOs2 = [ps2("pOs%d" % i, [128, 512], F32) for i in range(2)]
                    pOw2 = [ps2("pOw%d" % i, [128, 512], F32) for i in range(2)]
                    P.add("sp", lambda: nc.sync.dma_start(out=idf[:], in_=c_f32[:, 2, :]), writes=["idfB"], dma=True)
                    for i_ in range(2):
                        P.add("dve", lambda i_=i_: nc.vector.memset(negselT[i_][:], 0.0), writes=["negsel%d" % i_])
                    for qb in range(2):
                        P.add("dve", lambda qb=qb: nc.vector.memset(qTa[qb][64:128, :, :, :], 0.0), writes=["qaug_init%d" % qb])
                    O3 = lambda t_: t_[:, 0:260].rearrange("p (r e) -> p r e", e=65)
                    kOc = "pOc0"

                    def load_wq(g):
                        for k in range(KC):
                            P.add("pool", lambda k=k: nc.gpsimd.dma_start(
                                out=wqs[:, k, :], in_=w_in[k * 128:(k + 1) * 128, OFF_QA + g * 256:OFF_QA + (g + 1) * 256]),
                                writes=["wqs_%d" % k], dma=True)

                    def load_qaug(g):
                        qb = g % 2
                        P.add("sp", lambda: nc.sync.dma_start(out=qTa[qb][64:71, :, :, :], in_=c_qaug[g]),
                              reads=["qaug_init%d" % qb], writes=["qaug_rows%d" % qb], dma=True)

                    stream = []
                    for r in range(2):
                        for tb in range(4):
                            stream.append(("qp", 0, r, tb))
                    for g in range(4):
                        gj = []
                        gj.append(("cmp", g, 0, 0))
                        for c in range(NT):
                            gj.append(("selpe", g, c, 0))
                            if c + 1 < NT:
                                gj.append(("cmp", g, c + 1, 0))
                            gj += [("win", g, c, m) for m in range(max(0, c - 4), c + 1)]
                            gj += [("slc", g, c, m) for m in range(c + 1)]
                        if g + 1 < 4:
                            qps = [("qp", g + 1, r, tb) for r in range(2) for tb in range(4)]
                            out_ = []
                            for ji, jb in enumerate(gj):
                                out_.append(jb)
                                if ji >= 8 and (ji - 8) % 8 == 0 and qps:
                                    out_.append(qps.pop(0))
                            out_ += qps
                            gj = out_
                        stream += gj
                    n_st = len(stream)

                    def qkeys(g, c):
                        qb = g % 2
                        return ["qT%d_%d" % (qb, c // 4), "qaug_rows%d" % qb, "nsT%d_%d" % (qb, c)]

                    def emit_S(idx):
                        job = stream[idx]
                        sb_ = idx % 3
                        kind = job[0]
                        if kind == "selpe":
                            return
                        if kind == "qp":
                            _, g, r, tb = job
                            for k in range(KC):
                                P.add("pe", lambda k=k: nc.tensor.matmul(
                                    pS[sb_][:], lhsT=wqs[:, k, r * 128:(r + 1) * 128], rhs=hT[:, k, tb * 512:(tb + 1) * 512],
                                    start=(k == 0), stop=(k == KC - 1)),
                                    reads=["wqs_%d" % k] + hT_keys[tb * 4:tb * 4 + 4], writes=["pS%d" % sb_])
                            return
                        _, g, c, m = job
                        qT_aug = qTa[g % 2]
                        ms = slice(m * 128, (m + 1) * 128)
                        if kind == "cmp":
                            P.add("pe", lambda: nc.tensor.matmul(
                                pS[sb_][0:127, :], lhsT=kcT_aug[:, g, :], rhs=qT_aug[:, c, :, :], start=True, stop=False,
                                skip_group_check=True),
                                reads=["kcT_%d" % g, "kcTaug_rows"] + qkeys(g, c), writes=["pS%d" % sb_])
                            for r in range(4):
                                P.add("pe", lambda r=r: nc.tensor.matmul(
                                    pS[sb_][0:127, r * 128:(r + 1) * 128], lhsT=ident[0:127, 0:127], rhs=ccm[0:127, c, :],
                                    start=False, stop=(r == 3), skip_group_check=True),
                                    reads=["ident", "ccm"], writes=["pS%d" % sb_])
                        else:
                            xx = 0 if kind == "slc" else 1
                            extra = []
                            if m == c:
                                extra.append(0)
                            if kind == "win" and m == c - 4:
                                extra.append(1)
                            P.add("pe", lambda: nc.tensor.matmul(
                                pS[sb_][:], lhsT=kT_aug[:, xx, g, ms], rhs=qT_aug[:, c, :, :], start=True,
                                stop=(len(extra) == 0), skip_group_check=True),
                                reads=["kT_%d_%d" % (xx, g), "kTaug_rows"] + qkeys(g, c), writes=["pS%d" % sb_])
                            for ei, di in enumerate(extra):
                                last = ei == len(extra) - 1
                                P.add("pe", lambda last=last, di=di: nc.tensor.matmul(
                                    pS[sb_][:], lhsT=ident[:], rhs=cdiag[:, di, :, :], start=False, stop=last,
                                    skip_group_check=True),
                                    reads=["ident", "cdiag"], writes=["pS%d" % sb_])

                    def emit_act(idx):
                        job = stream[idx]
                        sb_ = idx % 3
                        if job[0] == "selpe":
                            return
                        if job[0] == "qp":
                            _, g, r, tb = job
                            qT_aug = qTa[g % 2]
                            sg = idx % 2
                            P.add("act", lambda: nc.scalar.activation(
                                out=qT_aug[0:64, tb * 4:(tb + 1) * 4, 2 * r, :],
                                in_=pS[sb_][0:64, :].rearrange("p (c t) -> p c t", t=128), func=AF.Copy, scale=0.125),
                                reads=["pS%d" % sb_], writes=["qT%d_%d" % (g % 2, tb)])
                            P.add("act", lambda: nc.scalar.activation(
                                out=qstg[sg][64:128, :], in_=pS[sb_][64:128, :], func=AF.Copy, scale=0.125),
                                reads=["pS%d" % sb_], writes=["qstg%d" % sg])
                            P.add("sp", lambda: nc.sync.dma_start(
                                out=qT_aug[0:64, tb * 4:(tb + 1) * 4, 2 * r + 1, :],
                                in_=qstg[sg][64:128, :].rearrange("p (c t) -> p c t", t=128)),
                                reads=["qstg%d" % sg], writes=["qT%d_%d" % (g % 2, tb)], dma=True)
                            return
                        np_ = 127 if job[0] == "cmp" else 128
                        P.add("act", lambda: nc.scalar.activation(out=PT[sb_][0:np_, :], in_=pS[sb_][0:np_, :], func=AF.Exp),
                              reads=["pS%d" % sb_], writes=["PT%d" % sb_])

                    def emit_PV(idx):
                        job = stream[idx]
                        pb_i = idx % 3
                        kind = job[0]
                        if kind in ("qp", "selpe"):
                            return
                        _, g, c, m = job
                        if kind == "cmp":
                            for r in range(4):
                                P.add("pe", lambda r=r: nc.tensor.matmul(
                                    O3(pOc)[:, r, :], lhsT=PT[pb_i][0:127, r * 128:(r + 1) * 128], rhs=vc_aug[0:127, g, :],
                                    start=(r == 0), stop=True, skip_group_check=True),
                                    reads=["PT%d" % pb_i, "vc_%d" % g, "vc_aug_init"], writes=[kOc])
                            for r in range(4):
                                P.add("pe", lambda r=r: nc.tensor.matmul(
                                    pOc[:, 260 + r * 32:260 + (r + 1) * 32], lhsT=PT[pb_i][0:127, r * 128:(r + 1) * 128],
                                    rhs=cmm[0:127, :], start=False, stop=True, skip_group_check=True),
                                    reads=["PT%d" % pb_i, "cmm"], writes=[kOc])
                        else:
                            xx = 0 if kind == "slc" else 1
                            pO = pOs2[c % 2] if kind == "slc" else pOw2[c % 2]
                            key = ("pOs%d" if kind == "slc" else "pOw%d") % (c % 2)
                            m0 = 0 if kind == "slc" else max(0, c - 4)
                            for r in range(4):
                                P.add("pe", lambda r=r: nc.tensor.matmul(
                                    O3(pO)[:, r, :], lhsT=PT[pb_i][:, r * 128:(r + 1) * 128], rhs=v_aug[:, m, xx, g, :],
                                    start=(m == m0 and r == 0), stop=(m == c), skip_group_check=True),
                                    reads=["PT%d" % pb_i, "v_aug_%d" % m], writes=[key])

                    def emit_select(g, c):
                        ob = (g * NT + c) % 3
                        rk = "rinvE%d_0" % ob
                        P.add("dve", lambda: nc.vector.tensor_scalar(
                            out=rinvE[ob][:, 0, :], in0=O3(pOc)[:, :, 64], scalar1=tiny[:, 0:1], scalar2=None, op0=ALU.add),
                            reads=[kOc, "tiny"], writes=[rk])
                        P.add("dve", lambda: nc.vector.reciprocal(out=rinvE[ob][:, 0, :], in_=rinvE[ob][:, 0, :]),
                              reads=[rk], writes=[rk])
                        for r in range(4):
                            if r == 0:
                                P.add("dve", lambda: nc.vector.tensor_scalar(
                                    out=pslc[:], in0=pOc[:, 260:292], scalar1=rinvE[ob][:, 0, 0:1], scalar2=None, op0=ALU.mult),
                                    reads=[kOc, rk], writes=["pslc"])
                            else:
                                P.add("dve", lambda r=r: nc.vector.scalar_tensor_tensor(
                                    out=pslc[:], in0=pOc[:, 260 + r * 32:292 + r * 32], scalar=rinvE[ob][:, 0, r:r + 1],
                                    in1=pslc[:], op0=ALU.mult, op1=ALU.add),
                                    reads=[kOc, rk, "pslc"], writes=["pslc"])
                        P.add("dve", lambda: nc.vector.tensor_tensor(out=score[:], in0=pslc[:], in1=csel[:, c, 0, :], op=ALU.mult),
                              reads=["pslc", "csel"], writes=["score"])
                        P.add("dve", lambda: nc.vector.tensor_tensor(out=score[:], in0=score[:], in1=csel[:, c, 1, :], op=ALU.add),
                              reads=["score", "csel"], writes=["score"])
                        P.add("dve", lambda: nc.vector.max(out=top8[:], in_=score[:]), reads=["score"], writes=["top8"])
                        P.add("dve", lambda: nc.vector.tensor_scalar(
                            out=negselT[c % 2][:, 96:128], in0=score[:], scalar1=top8[:, 7:8], scalar2=-1.0,
                            op0=ALU.is_ge, op1=ALU.add),
                            reads=["score", "top8"], writes=["negsel%d" % (c % 2)])
                        gsl0 = gate_sb[:, c, g * 12:(g + 1) * 12].rearrange("p (r x) -> p x r", x=3)[:, 0, :]
                        P.add("dve", lambda: nc.vector.tensor_tensor(out=coefC[:], in0=rinvE[ob][:, 0, :], in1=gsl0, op=ALU.mult),
                              reads=[rk, "gate_%d" % c], writes=["coefC"])
                        P.add("dve", lambda: nc.vector.tensor_tensor(
                            out=oacc[ob][:], in0=O3(pOc)[:, :, 0:64], in1=coefC[:].unsqueeze(2).to_broadcast([128, 4, 64]),
                            op=ALU.mult),
                            reads=[kOc, "coefC"], writes=["oacc%d" % ob])

                    def emit_selpe(g, c):
                        qT_aug = qTa[g % 2]
                        pOs = pOs2[c % 2]
                        kOs = "pOs%d" % (c % 2)
                        P.add("pe", lambda: nc.tensor.transpose(out=pOs[:, 260:388], in_=negselT[c % 2][:], identity=idf[:]),
                              reads=["negsel%d" % (c % 2), "idfB"], writes=[kOs])
                        P.add("dve", lambda: nc.vector.tensor_copy(
                            out=qT_aug[96:128, c, :, :], in_=pOs[96:128, 260:388].unsqueeze(1).to_broadcast([32, 4, 128])),
                            reads=[kOs, "qaug_init%d" % (g % 2)], writes=["nsT%d_%d" % (g % 2, c)])

                    def epi_part1(g, c):
                        ob = (g * NT + c) % 3
                        pOs, pOw = pOs2[c % 2], pOw2[c % 2]
                        kOs, kOw = "pOs%d" % (c % 2), "pOw%d" % (c % 2)
                        for bi, pO, key in ((1, pOs, kOs), (2, pOw, kOw)):
                            P.add("dve", lambda bi=bi, pO=pO: nc.vector.reciprocal(out=rinvE[ob][:, bi, :], in_=O3(pO)[:, :, 64]),
                                  reads=[key], writes=["rinvE%d_%d" % (ob, bi)])
                        gsl = gate_sb[:, c, g * 12:(g + 1) * 12].rearrange("p (r x) -> p x r", x=3)
                        P.add("dve", lambda: nc.vector.tensor_tensor(out=coef[:], in0=rinvE[ob][:], in1=gsl, op=ALU.mult),
                              reads=["rinvE%d_%d" % (ob, bi) for bi in range(3)] + ["gate_%d" % c], writes=["coef"])
                        for bi, pO, key in ((1, pOs, kOs), (2, pOw, kOw)):
                            P.add("dve", lambda bi=bi, pO=pO: nc.vector.tensor_tensor(
                                out=otmp[:], in0=O3(pO)[:, :, 0:64],
                                in1=coef[:, bi, :].unsqueeze(2).to_broadcast([128, 4, 64]), op=ALU.mult),
                                reads=[key, "coef"], writes=["otmp"])
                            P.add("dve", lambda: nc.vector.tensor_tensor(out=oacc[ob][:], in0=oacc[ob][:], in1=otmp[:], op=ALU.add),
                                  reads=["oacc%d" % ob, "otmp"], writes=["oacc%d" % ob])
                        P.add("dve", lambda: nc.vector.tensor_tensor(out=otmp[:], in0=oacc[ob][:], in1=oacc[ob][:], op=ALU.mult),
                              reads=["oacc%d" % ob], writes=["otmp"])
                        P.add("dve", lambda: nc.vector.tensor_reduce(out=ssq[ob][:], in_=otmp[:], axis=AX.X, op=ALU.add),
                              reads=["otmp"], writes=["ssqB%d" % ob])

                    def epi_tail(g, c):
                        ob = (g * NT + c) % 3
                        rsqrt(rsd[ob][:], ssq[ob][:], 1.0 / 64, "ssqB%d" % ob, "rsdB%d" % ob)
                        P.add("dve", lambda: nc.vector.tensor_tensor(
                            out=oacc[ob][:], in0=oacc[ob][:], in1=rsd[ob][:].unsqueeze(2).to_broadcast([128, 4, 64]), op=ALU.mult),
                            reads=["oacc%d" % ob, "rsdB%d" % ob], writes=["oacc%d" % ob])
                        P.add("dve", lambda: nc.vector.tensor_tensor(
                            out=ostage[ob][:], in0=oacc[ob][:].rearrange("p r d -> p (r d)"), in1=ngb[:, g * 256:(g + 1) * 256],
                            op=ALU.mult),
                            reads=["oacc%d" % ob, "ngb"], writes=["ostB%d" % ob])
                        P.add("sp", lambda: nc.sync.dma_start(
                            out=o_scr[c * 128:(c + 1) * 128, g * 256:(g + 1) * 256], in_=ostage[ob][:]),
                            reads=["ostB%d" % ob], writes=["o_scr"], dma=True)

                    load_wq(0)
                    load_qaug(0)
                    sel_done = set()
                    deferred = []
                    pend_p1, pend_tail = [], []
                    s_emitted = set()

                    def try_S(idx):
                        if idx >= n_st or idx in s_emitted:
                            return
                        job = stream[idx]
                        if job[0] == "slc" and (job[1], job[2]) not in sel_done:
                            deferred.append(idx)
                            return
                        s_emitted.add(idx)
                        emit_S(idx)

                    try_S(0)
                    try_S(1)
                    for idx, job in enumerate(stream):
                        try_S(idx + 2)
                        if idx not in s_emitted:
                            s_emitted.add(idx)
                            if idx in deferred:
                                deferred.remove(idx)
                            emit_S(idx)
                        emit_act(idx)
                        emit_PV(idx)
                        kind = job[0]
                        if kind == "qp":
                            _, g_, r_, tb_ = job
                            if r_ == 1 and tb_ == 3 and g_ + 1 < 4:
                                load_wq(g_ + 1)
                            continue
                        _, g, c, m = job
                        if kind == "selpe":
                            emit_selpe(g, c)
                            sel_done.add((g, c))
                            for d_ in list(deferred):
                                deferred.remove(d_)
                                try_S(d_)
                            continue
                        if kind == "cmp":
                            if c == 2 and g + 1 < 4:
                                load_qaug(g + 1)
                            seq = g * NT + c
                            while pend_tail and pend_tail[0][3] <= seq - 3:
                                gt, ct, _, _ = pend_tail.pop(0)
                                epi_tail(gt, ct)
                            emit_select(g, c)
                            while pend_p1:
                                epi_part1(*pend_p1.pop(0))
                        for pt_ in pend_tail:
                            pt_[2] -= 1
                        while pend_tail and pend_tail[0][2] <= 0 and (pend_tail[0][0], pend_tail[0][1]) not in pend_p1:
                            gt, ct, _, _ = pend_tail.pop(0)
                            epi_tail(gt, ct)
                        if kind == "slc" and m == c:
                            pend_p1.append((g, c))
                            pend_tail.append([g, c, 24, g * NT + c])
                    while pend_p1:
                        epi_part1(*pend_p1.pop(0))
                    while pend_tail:
                        gt, ct, _, _ = pend_tail.pop(0)
                        epi_tail(gt, ct)
        if "B" in stages:
            phase_B()
            P.barrier()

        HB = 2

        def phase_C():
            with ExitStack() as sc:
                sbc = lambda name, shape, dt: sc.enter_context(nc.sbuf_tensor(name, list(shape), dt))
                psc = lambda name, shape, dt: sc.enter_context(nc.psum_tensor(name, list(shape), dt))
                cf = sbc("cf", [128, 4, 128], F32)
                P.add("sp", lambda: nc.sync.dma_start(out=cf[:], in_=c_f32), writes=["cf"], dma=True)
                U2, L2, IDF = cf[:, 0, :], cf[:, 1, :], cf[:, 2, :]
                lbb = sbc("lbb", [128, 1024], F32)
                oml = sbc("oml", [128, 1024], F32)
                hgb = sbc("hgb", [128, 1024], F32)
                P.add("sp", lambda: nc.sync.dma_start(out=lbb[:], in_=lower_bounds[0].partition_broadcast(128)),
                      writes=["lbb"], dma=True)
                P.add("sp", lambda: nc.sync.dma_start(out=oml[:], in_=lower_bounds[1].partition_broadcast(128)),
                      writes=["oml"], dma=True)
                P.add("sp", lambda: nc.sync.dma_start(out=hgb[:], in_=hgrn_gain.partition_broadcast(128)),
                      writes=["hgb"], dma=True)
                P.add("dve", lambda: nc.vector.tensor_tensor(out=oml[:], in0=oml[:], in1=lbb[:], op=ALU.subtract),
                      reads=["lbb", "oml"], writes=["oml"])
                P.add("act", lambda: nc.scalar.activation(out=oml[:], in_=oml[:], func=AF.Exp), reads=["oml"], writes=["oml"])
                P.add("dve", lambda: nc.vector.tensor_scalar(out=oml[:], in0=oml[:], scalar1=1.0, scalar2=None, op0=ALU.add),
                      reads=["oml"], writes=["oml"])
                P.add("dve", lambda: nc.vector.reciprocal(out=lbb[:], in_=oml[:]), reads=["oml"], writes=["lbb"])
                P.add("dve", lambda: nc.vector.tensor_scalar(out=oml[:], in0=lbb[:], scalar1=-1.0, scalar2=1.0,
                                                             op0=ALU.mult, op1=ALU.add), reads=["lbb"], writes=["oml"])

                W = HB * 128
                wq = sbc("wq", [128, KC, W], BF16)
                wfi = sbc("wfi", [128, KC, 2, W], BF16)
                qT = sbc("qTh", [128, HB, S], BF16)
                logf = sbc("logf", [128, NT, W], F32)
                kk = sbc("kk", [128, NT, W], F32)
                vv = sbc("vv", [128, NT, W], BF16)
                S32 = sbc("S32", [128, HB, 128], F32)
                Sbf = sbc("Sbf", [128, HB, 128], BF16)
                tmpe = [sbc("tmpe%d" % i, [128, W], F32) for i in range(2)]
                tmpf = [sbc("tmpf%d" % i, [128, W], F32) for i in range(2)]
                ebT = [sbc("ebT%d" % i, [128, HB, 128], F32) for i in range(2)]
                enbT = [sbc("enbT%d" % i, [128, HB, 128], F32) for i in range(2)]
                erev = [sbc("erev%d" % i, [128, HB, 128], F32) for i in range(2)]
                qbz = [sbc("qbz%d" % i, [128, HB, 2, 128], BF16) for i in range(2)]
                for i_ in range(2):
                    P.add("dve", lambda i_=i_: nc.vector.memset(qbz[i_][:], 0.0), writes=["qbz_init"])
                qv = lambda par, hd: qbz[par][:, hd, :, :].rearrange("p a (b t) -> p (a b) t", t=64)[:, 0:4:3, :]
                kbT = [sbc("kbT%d" % i, [128, HB, 128], BF16) for i in range(2)]
                kd = [sbc("kd%d" % i, [128, HB, 2, 128], BF16) for i in range(2)]
                ATm = [sbc("ATm%d" % i, [128, HB, 128], BF16) for i in range(2)]
                ost = [sbc("ost%d" % i, [128, W], F32) for i in range(2)]
                junk = sbc("junkC", [128, 128], BF16)
                ssq = [sbc("ssqC%d" % i, [128, HB], F32) for i in range(2)]
                rsd = [sbc("rsdC%d" % i, [128, HB], F32) for i in range(2)]
                pA = [[psc("pA%d_%d" % (par, hd), [128, 4, 128], F32) for hd in range(HB)] for par in range(2)]
                pOb = [psc("pO_%d" % par, [128, 4, 128], F32) for par in range(2)]
                pproj = [pOb[i_][:, :, :].rearrange("p a b -> p (a b)") for i_ in range(2)]
                pSf = [psc("pS_%d" % hd, [128, 4, 128], F32) for hd in range(HB)]
                pSb = [t[:, 0, :] for t in pSf]

                npj = 0
                for h0 in range(0, 8, HB):
                    for k in range(KC):
                        P.add("pool", lambda k=k, h0=h0: nc.gpsimd.dma_start(
                            out=wq[:, k, :], in_=w_in[k * 128:(k + 1) * 128, OFF_QH + h0 * 128:OFF_QH + h0 * 128 + W]),
                            writes=["wq_%d" % k], dma=True)
                        P.add("pool", lambda k=k, h0=h0: nc.gpsimd.dma_start(
                            out=wfi[:, k, 0, :], in_=w_in[k * 128:(k + 1) * 128, OFF_FH + h0 * 128:OFF_FH + h0 * 128 + W]),
                            writes=["wf_%d" % k], dma=True)
                        P.add("pool", lambda k=k, h0=h0: nc.gpsimd.dma_start(
                            out=wfi[:, k, 1, :], in_=w_in[k * 128:(k + 1) * 128, OFF_IH + h0 * 128:OFF_IH + h0 * 128 + W]),
                            writes=["wi_%d" % k], dma=True)
                    for hd in range(HB):
                        for tb in range(4):
                            pp = npj % 2
                            npj += 1
                            for k in range(KC):
                                P.add("pe", lambda k=k, hd=hd, tb=tb, pp=pp: nc.tensor.matmul(
                                    pproj[pp], lhsT=wq[:, k, hd * 128:(hd + 1) * 128], rhs=hT[:, k, tb * 512:(tb + 1) * 512],
                                    start=(k == 0), stop=(k == KC - 1)),
                                    reads=["wq_%d" % k] + hT_keys[tb * 4:tb * 4 + 4], writes=["pO_%d" % pp])
                            P.add("act", lambda hd=hd, tb=tb, pp=pp: nc.scalar.copy(
                                out=qT[:, hd, tb * 512:(tb + 1) * 512], in_=pproj[pp]),
                                reads=["pO_%d" % pp], writes=["qTh_%d_%d" % (hd, tb)])
                    for i in range(NT):
                        pp = npj % 2
                        npj += 1
                        b = i % 2
                        for k in range(KC):
                            P.add("pe", lambda k=k, i=i, pp=pp: nc.tensor.matmul(
                                pproj[pp], lhsT=hT[:, k, i * 128:(i + 1) * 128], rhs=wfi[:, k, :, :],
                                start=(k == 0), stop=(k == KC - 1)),
                                reads=["wf_%d" % k, "wi_%d" % k, "hT_%d" % i], writes=["pO_%d" % pp])
                        P.add("act", lambda pp=pp, b=b: nc.scalar.activation(out=tmpe[b][:], in_=pproj[pp][:, 0:W],
                                                                             func=AF.Exp, scale=-1.0),
                              reads=["pO_%d" % pp], writes=["tmpe%d" % b])
                        P.add("act", lambda pp=pp, i=i: nc.scalar.copy(out=vv[:, i, :], in_=pproj[pp][:, W:2 * W]),
                              reads=["pO_%d" % pp], writes=["vv_%d" % i])
                        P.add("dve", lambda b=b: nc.vector.tensor_scalar(out=tmpe[b][:], in0=tmpe[b][:], scalar1=1.0,
                                                                         scalar2=None, op0=ALU.add),
                              reads=["tmpe%d" % b], writes=["tmpe%d" % b])
                        P.add("dve", lambda b=b: nc.vector.reciprocal(out=tmpe[b][:], in_=tmpe[b][:]),
                              reads=["tmpe%d" % b], writes=["tmpe%d" % b])
                        P.add("dve", lambda b=b, h0=h0: nc.vector.tensor_tensor(
                            out=tmpf[b][:], in0=tmpe[b][:], in1=oml[:, h0 * 128:h0 * 128 + W], op=ALU.mult),
                            reads=["tmpe%d" % b, "oml"], writes=["tmpf%d" % b])
                        P.add("dve", lambda b=b, h0=h0: nc.vector.tensor_tensor(
                            out=tmpf[b][:], in0=tmpf[b][:], in1=lbb[:, h0 * 128:h0 * 128 + W], op=ALU.add),
                            reads=["tmpf%d" % b, "lbb"], writes=["tmpf%d" % b])
                        P.add("act", lambda b=b, i=i: nc.scalar.activation(out=logf[:, i, :], in_=tmpf[b][:], func=AF.Ln),
                              reads=["tmpf%d" % b], writes=["logf_%d" % i])
                        P.add("dve", lambda b=b, i=i: nc.vector.tensor_scalar(
                            out=kk[:, i, :], in0=tmpf[b][:], scalar1=-1.0, scalar2=1.0, op0=ALU.mult, op1=ALU.add),
                            reads=["tmpf%d" % b], writes=["kk_%d" % i])
                    P.add("dve", lambda: nc.vector.memset(S32[:], 0.0), writes=["S32_%d" % hd for hd in range(HB)])
                    P.add("dve", lambda: nc.vector.memset(Sbf[:], 0.0), writes=["Sbf_%d" % hd for hd in range(HB)])
                    pend_epi = []
                    def front(i):
                            par = i % 2
                            tb = i // 4
                            hs = lambda hd: slice(hd * 128, (hd + 1) * 128)
                            for hd in range(HB):
                                P.add("pe", lambda i=i, hd=hd, par=par: nc.tensor.matmul(
                                    pA[par][hd][:, 0, :], lhsT=logf[:, i, hs(hd)], rhs=U2, start=True, stop=True),
                                    reads=["logf_%d" % i, "cf"], writes=["pA%d_%d" % (par, hd)])
                                P.add("pe", lambda i=i, hd=hd, par=par: nc.tensor.matmul(
                                    pA[par][hd][:, 1, :], lhsT=L2, rhs=logf[:, i, hs(hd)], start=True, stop=True),
                                    reads=["logf_%d" % i, "cf"], writes=["pA%d_%d" % (par, hd)])
                                P.add("pe", lambda i=i, hd=hd, par=par: nc.tensor.transpose(
                                    out=pA[par][hd][:, 2, :], in_=kk[:, i, hs(hd)], identity=IDF),
                                    reads=["kk_%d" % i, "cf"], writes=["pA%d_%d" % (par, hd)])
                            for hd in range(HB):
                                P.add("act", lambda hd=hd, par=par: nc.scalar.activation(
                                    out=ebT[par][:, hd, :], in_=pA[par][hd][:, 0, :], func=AF.Exp),
                                    reads=["pA%d_%d" % (par, hd)], writes=["ebT%d_%d" % (par, hd)])
                                P.add("act", lambda hd=hd, par=par: nc.scalar.activation(
                                    out=enbT[par][:, hd, :], in_=pA[par][hd][:, 0, :], func=AF.Exp, scale=-1.0),
                                    reads=["pA%d_%d" % (par, hd)], writes=["enbT%d_%d" % (par, hd)])
                                P.add("act", lambda hd=hd, par=par: nc.scalar.activation(
                                    out=erev[par][:, hd, :], in_=pA[par][hd][:, 1, :], func=AF.Exp),
                                    reads=["pA%d_%d" % (par, hd)], writes=["erev%d_%d" % (par, hd)])
                            for hd in range(HB):
                                P.add("dve", lambda i=i, hd=hd, par=par: nc.vector.tensor_tensor(
                                    out=qv(par, hd), in0=qT[:, hd, i * 128:(i + 1) * 128].rearrange("p (a t) -> p a t", t=64),
                                    in1=ebT[par][:, hd, :].rearrange("p (a t) -> p a t", t=64), op=ALU.mult),
                                    reads=["qTh_%d_%d" % (hd, tb), "ebT%d_%d" % (par, hd), "qbz_init"],
                                    writes=["qbT%d_%d" % (par, hd)])
                                P.add("dve", lambda hd=hd, par=par: nc.vector.tensor_tensor(
                                    out=kbT[par][:, hd, :], in0=pA[par][hd][:, 2, :], in1=enbT[par][:, hd, :], op=ALU.mult),
                                    reads=["pA%d_%d" % (par, hd), "enbT%d_%d" % (par, hd)], writes=["kbT%d_%d" % (par, hd)])
                                for ch_ in range(2):
                                    P.add("dve", lambda i=i, hd=hd, par=par, ch_=ch_: nc.vector.scalar_tensor_tensor(
                                        out=kd[par][:, hd, ch_, :], in0=kk[:, i, hs(hd)], scalar=cf[:, 3, 2 + ch_:3 + ch_],
                                        in1=erev[par][:, hd, :], op0=ALU.mult, op1=ALU.mult),
                                        reads=["kk_%d" % i, "erev%d_%d" % (par, hd), "cf"],
                                        writes=["kd%d_%d_%d" % (par, hd, ch_)])
                            for hd in range(HB):
                                P.add("pe", lambda hd=hd, par=par: nc.tensor.matmul(
                                    pA[par][hd][:, 3, :], lhsT=kbT[par][:, hd, :], rhs=qv(par, hd), start=True, stop=True),
                                    reads=["kbT%d_%d" % (par, hd), "qbT%d_%d" % (par, hd)], writes=["pA%d_%d" % (par, hd)])
                            for hd in range(HB):
                                P.add("dve", lambda hd=hd, par=par: nc.vector.tensor_tensor(
                                    out=ATm[par][:, hd, :], in0=pA[par][hd][:, 3, :], in1=U2, op=ALU.mult),
                                    reads=["pA%d_%d" % (par, hd), "cf"], writes=["ATm%d_%d" % (par, hd)])
                            while pend_epi:
                                pend_epi.pop(0)()

                    def back(i, chs):
                            par = i % 2
                            tb = i // 4
                            hs = lambda hd: slice(hd * 128, (hd + 1) * 128)
                            for ch in chs:
                                cs = slice(ch * 64, (ch + 1) * 64)
                                for hd in range(HB):
                                    if ch == 0:
                                        P.add("pe", lambda i=i, hd=hd, par=par: nc.tensor.matmul(
                                            pOb[par][:, hd, :], lhsT=ATm[par][:, hd, :], rhs=vv[:, i, hs(hd)],
                                            start=(hd == 0), stop=False, skip_group_check=True),
                                            reads=["ATm%d_%d" % (par, hd), "vv_%d" % i], writes=["pO_%d" % par])
                                    P.add("pe", lambda hd=hd, par=par, cs=cs, ch=ch: nc.tensor.matmul(
                                        pOb[par][:, hd, :], lhsT=qbz[par][:, hd, ch, :], rhs=Sbf[:, hd, :],
                                        start=False, stop=(ch == 1), skip_group_check=True),
                                        reads=["qbT%d_%d" % (par, hd), "Sbf_%d" % hd], writes=["pO_%d" % par])
                                    P.add("pe", lambda i=i, hd=hd, par=par, ch=ch: nc.tensor.matmul(
                                        pSb[hd], lhsT=kd[par][:, hd, ch, :], rhs=vv[:, i, hs(hd)],
                                        start=True, stop=True),
                                        reads=["kd%d_%d_%d" % (par, hd, ch), "vv_%d" % i], writes=["pS_%d" % hd])
                                for hd in range(HB):
                                    col = ch * 64 + 63
                                    P.add("dve", lambda hd=hd, par=par, col=col: nc.vector.scalar_tensor_tensor(
                                        out=Sbf[:, hd, :], in0=S32[:, hd, :], scalar=ebT[par][:, hd, col:col + 1],
                                        in1=pSb[hd], op0=ALU.mult, op1=ALU.add),
                                        reads=["S32_%d" % hd, "ebT%d_%d" % (par, hd), "pS_%d" % hd], writes=["Sbf_%d" % hd])
                                for hd in range(HB):
                                    col = ch * 64 + 63
                                    P.add("dve", lambda hd=hd, par=par, col=col: nc.vector.scalar_tensor_tensor(
                                        out=S32[:, hd, :], in0=S32[:, hd, :], scalar=ebT[par][:, hd, col:col + 1],
                                        in1=pSb[hd], op0=ALU.mult, op1=ALU.add),
                                        reads=["S32_%d" % hd, "ebT%d_%d" % (par, hd), "pS_%d" % hd], writes=["S32_%d" % hd])
                            if 1 not in chs:
                                return
                            def epi(i=i, par=par, h0=h0):
                                for hd in range(HB):
                                    P.add("act", lambda hd=hd, par=par: nc.scalar.activation(
                                        out=junk[:], in_=pOb[par][:, hd, :], func=AF.Square, accum_out=ssq[par][:, hd:hd + 1]),
                                        reads=["pO_%d" % par], writes=["junkC", "ssqC%d_%d" % (par, hd)])
                                rsqrt(rsd[par][:], ssq[par][:], 1.0 / 128, "ssqC%d" % par, "rsdC%d" % par,
                                      rkeys=["ssqC%d_%d" % (par, hd) for hd in range(HB)])
                                for hd in range(HB):
                                    P.add("dve", lambda hd=hd, par=par, h0=h0: nc.vector.scalar_tensor_tensor(
                                        out=ost[par][:, hs(hd)], in0=pOb[par][:, hd, :], scalar=rsd[par][:, hd:hd + 1],
                                        in1=hgb[:, (h0 + hd) * 128:(h0 + hd + 1) * 128], op0=ALU.mult, op1=ALU.mult),
                                        reads=["pO_%d" % par, "rsdC%d" % par, "hgb"], writes=["ost%d" % par])
                                P.add("sp", lambda i=i, par=par, h0=h0: nc.sync.dma_start(
                                    out=o_scr[i * 128:(i + 1) * 128, 1024 + h0 * 128:1024 + h0 * 128 + W], in_=ost[par][:]),
                                    reads=["ost%d" % par], writes=["o_scr"], dma=True)
                            pend_epi.append(epi)

                    front(0)
                    for i in range(NT):
                        back(i, (0,))
                        if i + 1 < NT:
                            front(i + 1)
                        back(i, (1,))
                    while pend_epi:
                        pend_epi.pop(0)()
        if "C" in stages:
            phase_C()
            P.barrier()

        mixT = st.enter_context(nc.sbuf_tensor("mixT", [128, KC, S], BF16))

        def phase_D():
            with ExitStack() as sdd:
                sbd = lambda name, shape, dt: sdd.enter_context(nc.sbuf_tensor(name, list(shape), dt))
                wz = [wz0, sbd("wz1", [128, KC, 512], BF16)]
                ot = [sbd("ot%d" % i, [128, 512], F32) for i in range(2)]
                sz = [sbd("sz%d" % i, [128, 512], F32) for i in range(2)]
                mx = [sbd("mx%d" % i, [128, 512], BF16) for i in range(2)]
                pz = [sdd.enter_context(nc.psum_tensor("pz%d" % i, [128, 512], F32)) for i in range(2)]
                pt = [sdd.enter_context(nc.psum_tensor("ptD%d" % i, [128, 4, 128], BF16)) for i in range(2)]
                zcols = [OFF_ZA, OFF_ZA + 512, OFF_ZH, OFF_ZH + 512]
                n = 0
                pend_tr = []
                for zb in range(4):
                    wb = zb % 2
                    for k in range(KC):
                        if zb == 0:
                            break
                        P.add("pool", lambda k=k, wb=wb, zb=zb: nc.gpsimd.dma_start(
                            out=wz[wb][:, k, :], in_=w_in[k * 128:(k + 1) * 128, zcols[zb]:zcols[zb] + 512]),
                            writes=["wz%d_%d" % (wb, k)], dma=True)
                    for i in range(NT):
                        b = n % 2
                        n += 1
                        P.add("sp", lambda i=i, b=b, zb=zb: nc.sync.dma_start(
                            out=ot[b][:], in_=o_scr[i * 128:(i + 1) * 128, zb * 512:(zb + 1) * 512]),
                            reads=["o_scr"], writes=["ot%d" % b], dma=True)
                        for k in range(KC):
                            P.add("pe", lambda i=i, b=b, k=k, wb=wb: nc.tensor.matmul(
                                pz[b][:], lhsT=hT[:, k, i * 128:(i + 1) * 128], rhs=wz[wb][:, k, :],
                                start=(k == 0), stop=(k == KC - 1)),
                                reads=["hT_%d" % i, "wz%d_%d" % (wb, k)], writes=["pz%d" % b])
                        while pend_tr:
                            pend_tr.pop(0)()
                        P.add("act", lambda b=b: nc.scalar.activation(out=sz[b][:], in_=pz[b][:], func=AF.Silu),
                              reads=["pz%d" % b], writes=["sz%d" % b])
                        P.add("dve", lambda b=b: nc.vector.tensor_tensor(out=mx[b][:], in0=sz[b][:], in1=ot[b][:],
                                                                         op=ALU.mult),
                              reads=["sz%d" % b, "ot%d" % b], writes=["mx%d" % b])
                        def tr(b=b, i=i, zb=zb):
                            for kk in range(4):
                                P.add("pe", lambda kk=kk: nc.tensor.transpose(
                                    out=pt[b][:, kk, :], in_=mx[b][:, kk * 128:(kk + 1) * 128], identity=ident[:]),
                                    reads=["mx%d" % b, "ident"], writes=["ptD%d" % b])
                            P.add("dve", lambda: nc.vector.tensor_copy(
                                out=mixT[:, zb * 4:(zb + 1) * 4, i * 128:(i + 1) * 128], in_=pt[b][:]),
                                reads=["ptD%d" % b], writes=["mixT_%d" % i])
                        pend_tr.append(tr)
                while pend_tr:
                    pend_tr.pop(0)()
        if "D" in stages:
            phase_D()
            P.barrier()

        def phase_F():
            with ExitStack() as sf:
                sbf = lambda name, shape, dt: sf.enter_context(nc.sbuf_tensor(name, list(shape), dt))
                wo = hT
                fg = sbf("fg", [128, D], F32)
                xt = [sbf("xtF%d" % i, [128, D], F32) for i in range(2)]
                rt = [sbf("rtF%d" % i, [128, D], F32) for i in range(2)]
                yo = [sbf("yoF%d" % i, [128, D], F32) for i in range(2)]
                junk = sbf("junkF", [128, D], BF16)
                ssq = [sbf("ssqF%d" % i, [128, 1], F32) for i in range(2)]
                rstd = [sbf("rstdF%d" % i, [128, 1], F32) for i in range(2)]
                py = [sf.enter_context(nc.psum_tensor("py%d" % i, [128, 512], F32)) for i in range(8)]
                for k in range(KC):
                    P.add("pool", lambda k=k: nc.gpsimd.dma_start(out=wo[:, k, :], in_=w_out[k * 128:(k + 1) * 128, :]),
                          writes=["wo_%d" % k], dma=True)
                P.add("sp", lambda: nc.sync.dma_start(out=fg[:], in_=final_norm.partition_broadcast(128)),
                      writes=["fg"], dma=True)
                for i in range(NT):
                    b = i % 2
                    P.add("sp", lambda i=i, b=b: nc.sync.dma_start(out=xt[b][:], in_=x[i * 128:(i + 1) * 128, :]),
                          writes=["xtF%d" % b], dma=True)
                    for k in range(KC):
                        for nb in range(4):
                            pb = b * 4 + nb
                            P.add("pe", lambda i=i, k=k, nb=nb, pb=pb: nc.tensor.matmul(
                                py[pb][:], lhsT=mixT[:, k, i * 128:(i + 1) * 128], rhs=wo[:, k, nb * 512:(nb + 1) * 512],
                                start=(k == 0), stop=(k == KC - 1)),
                                reads=["mixT_%d" % i, "wo_%d" % k], writes=["py%d" % pb])
                    for nb in range(4):
                        pb = b * 4 + nb
                        P.add("dve", lambda b=b, nb=nb, pb=pb: nc.vector.tensor_tensor(
                            out=rt[b][:, nb * 512:(nb + 1) * 512], in0=py[pb][:], in1=xt[b][:, nb * 512:(nb + 1) * 512],
                            op=ALU.add),
                            reads=["py%d" % pb, "xtF%d" % b], writes=["rtF%d_%d" % (b, nb)])
                    rkeys = ["rtF%d_%d" % (b, nb) for nb in range(4)]
                    P.add("act", lambda b=b: nc.scalar.activation(out=junk[:], in_=rt[b][:], func=AF.Square,
                                                                  accum_out=ssq[b][:]),
                          reads=rkeys, writes=["junkF", "ssqF%d" % b])
                    rsqrt(rstd[b][:], ssq[b][:], 1.0 / D, "ssqF%d" % b, "rstdF%d" % b)
                    P.add("dve", lambda b=b: nc.vector.scalar_tensor_tensor(
                        out=yo[b][:], in0=rt[b][:], scalar=rstd[b][:, 0:1], in1=fg[:], op0=ALU.mult, op1=ALU.mult),
                        reads=rkeys + ["rstdF%d" % b, "fg"], writes=["yoF%d" % b])
                    P.add("sp", lambda i=i, b=b: nc.sync.dma_start(out=out[i * 128:(i + 1) * 128, :], in_=yo[b][:]),
                          reads=["yoF%d" % b], dma=True)
        if "F" in stages:
            phase_F()
        nops, nwaits = P.emit(st)
    return nc, (nops, nwaits)


def make_consts():
    c = {}
    c["c_ident"] = bf(np.eye(128))
    blk = np.arange(128) // 64
    same = blk[:, None] == blk[None, :]
    ii = np.arange(128)
    U2 = (same & (ii[:, None] <= ii[None, :])).astype(np.float32)
    L2 = (same & (ii[:, None] > ii[None, :])).astype(np.float32)
    cb = np.zeros((128, 128), np.float32)
    cb[64:, 0] = -80.0
    cb[:64, 1] = -80.0
    cb[:64, 2] = 1.0
    cb[64:, 3] = 1.0
    c["c_f32"] = np.ascontiguousarray(np.stack([U2, L2, np.eye(128, dtype=np.float32), cb], axis=1))

    pos = np.arange(S)
    E = np.zeros((128, S), np.float32)
    E[pos // 64, pos] = -NEG
    c["c_E"] = bf(E)
    kl = np.arange(128)[:, None]
    tl = np.arange(128)[None, :]
    diag = np.where(kl <= tl, 0.0, NEG)
    far = np.where(tl < kl, 0.0, NEG)
    dd = np.stack([diag, far], axis=0)[:, None, :, :].repeat(4, axis=1)
    c["c_diag"] = bf(np.ascontiguousarray(dd.transpose(2, 0, 1, 3)))
    n = np.arange(128)[:, None, None]
    cch = np.arange(NT)[None, :, None]
    tt = np.arange(128)[None, None, :]
    c["c_cmask"] = bf(np.where(16 * n + 31 <= 128 * cch + tt, 0.0, NEG))
    cs_ = 16 * np.arange(127)[:, None]
    ss_ = 64 * np.arange(32)[None, :]
    ov = np.clip(np.minimum(cs_ + 32, ss_ + 64) - np.maximum(cs_, ss_), 0, None) / 32.0
    mm = np.zeros((128, 32), np.float32)
    mm[:127] = ov
    c["c_mmap"] = bf(mm)
    t_abs = (128 * np.arange(NT)[None, :, None] + np.arange(128)[:, None, None])
    j = np.arange(32)[None, None, :]
    cur = t_abs // 64
    forced = ((j == 0) | (j == cur) | (j == cur - 1)).astype(np.float32)
    future = (j * 64 > t_abs).astype(np.float32)
    c["c_sel"] = np.ascontiguousarray(np.stack([1.0 - future, 1e4 * forced * (1.0 - future) - future], axis=2).astype(np.float32))

    def split_rows(p):
        a = (p // 128) * 128
        b_ = p % 128
        one = np.ones_like(p)
        return np.stack([a, a, b_, b_, one, one, one], axis=0).astype(np.float32)
    c["c_kaug"] = bf(split_rows(pos))
    c["c_kcaug"] = bf(split_rows(16 * np.arange(127) + 31))
    qa = np.zeros((4, 7, 4, S), np.float32)
    for g in range(4):
        for r in range(4):
            sl = np.float32(2.0 ** (-(4 * g + r + 1) / 2.0))
            s_hi = np.float32(bf(sl))
            s_lo = np.float32(bf(sl - s_hi))
            sp_ = np.float64(s_hi) + np.float64(s_lo)
            st = sp_ * pos.astype(np.float64)
            st1 = bf(st).astype(np.float64)
            st2 = bf(st - st1).astype(np.float64)
            st3 = bf(st - st1 - st2).astype(np.float64)
            qa[g, 0, r] = s_hi; qa[g, 1, r] = s_lo; qa[g, 2, r] = s_hi; qa[g, 3, r] = s_lo
            qa[g, 4, r] = -st1; qa[g, 5, r] = -st2; qa[g, 6, r] = -st3
    c["c_qaug"] = bf(np.ascontiguousarray(qa.reshape(4, 7, 4, NT, 128).transpose(0, 1, 3, 2, 4)))
    return c


_CACHE = {}


def kernel(**inputs):
    if "nc" not in _CACHE:
        _CACHE["nc"] = build_program()[0]
    nc = _CACHE["nc"]
    consts = make_consts()
    x = np.asarray(inputs["x"], dtype=np.float32)
    B = x.shape[0]
    shared = {
        "norm_in": np.ascontiguousarray(np.asarray(inputs["norm_in"], np.float32)[0]),
        "w_in": np.ascontiguousarray(np.asarray(inputs["w_in"], np.float32)[0]),
        "w_out": np.ascontiguousarray(np.asarray(inputs["w_out"], np.float32)[0]),
        "final_norm": np.ascontiguousarray(np.asarray(inputs["final_norm"], np.float32)),
        "nsa_out_norm": np.ascontiguousarray(np.asarray(inputs["nsa_out_norm"], np.float32)[0]),
        "hgrn_out_norm": np.ascontiguousarray(np.asarray(inputs["hgrn_out_norm"], np.float32)[0]),
        "lower_bounds": np.ascontiguousarray(np.asarray(inputs["lower_bounds"], np.float32)),
    }
    for nm in ("cmp_pe_k", "cmp_pe_v", "cmp_w1_k", "cmp_w1_v", "cmp_w2_k", "cmp_w2_v"):
        shared[nm] = np.ascontiguousarray(np.asarray(inputs[nm], np.float32)[0])
    shared.update(consts)
    in_maps = []
    for b in range(B):
        m = dict(shared)
        m["x"] = np.ascontiguousarray(x[b])
        in_maps.append(m)
    res = run_bass_kernel_spmd(nc, in_maps, core_ids=list(range(B)))
    return np.stack([np.asarray(r["out"], dtype=np.float32) for r in res.results], axis=0)
```
